# Optimizing a Trainium2 kernel written in Bass

```python
import jax, jax.numpy as jnp
from jax import lax
import numpy as np

D_MODEL = 1024
BATCH = 8
SEQ = 2048
DEPTH = 1

RW_HEAD = 64
RW_HEADS = 8
RW_WIDTH = RW_HEADS * RW_HEAD
DECAY_LORA = 64
AAA_LORA = 64
GATE_LORA = 128
RW_GN_EPS = 64e-5
RW_COLS = 3 * RW_WIDTH + DECAY_LORA + AAA_LORA + GATE_LORA
RW_SPLITS = [RW_WIDTH, 2 * RW_WIDTH, 3 * RW_WIDTH, 3 * RW_WIDTH + DECAY_LORA, 3 * RW_WIDTH + DECAY_LORA + AAA_LORA]

GLA_HEADS = 4
GLA_DK = 64
GLA_DV = 128
GLA_KW = GLA_HEADS * GLA_DK
GLA_VW = GLA_HEADS * GLA_DV
GLA_GATE_LORA = 16
GLA_TAU = 16.0
GLA_CHUNK = 64
GLA_NORM_EPS = 1e-5
GLA_COLS = 2 * GLA_KW + 2 * GLA_VW + GLA_GATE_LORA
GLA_SPLITS = [GLA_KW, 2 * GLA_KW, 2 * GLA_KW + GLA_VW, 2 * GLA_KW + 2 * GLA_VW]

N_IN = RW_COLS + GLA_COLS + 2 * D_MODEL

D_FF = ((-(-8 * D_MODEL // 3)) + 255) // 256 * 256

ALPHA = (2.0 * DEPTH) ** 0.25
BETA = (8.0 * DEPTH) ** -0.25
LN_EPS = 1e-5

kernel_name = "hybrid_rwkv7_gla_deepnorm_adaln_block"


def _layer_norm(x, eps):
    x32 = x.astype(jnp.float32)
    mu = jnp.mean(x32, -1, keepdims=True)
    var = jnp.mean(jnp.square(x32 - mu), -1, keepdims=True)
    return (x32 - mu) * lax.rsqrt(var + eps)


def _token_shift(p, mu):
    p_prev = jnp.pad(p, ((0, 0), (1, 0), (0, 0)))[:, :-1, :]
    return p + mu * (p_prev - p)


def _rwkv7_branch(p, mu, w0, w2, a0, a2, g2, k_k, k_a, r_k, gn_g, gn_b):
    B, T, _ = p.shape
    H, N = RW_HEADS, RW_HEAD
    f32 = jnp.float32
    p = _token_shift(p, mu)
    r, k, v, wd, ad, gd = jnp.split(p, RW_SPLITS, axis=-1)
    w = -jax.nn.softplus(-(w0 + jnp.tanh(wd) @ w2).astype(f32)) - 0.5
    decay = jnp.exp(-jnp.exp(w))
    a = jax.nn.sigmoid((a0 + ad @ a2).astype(f32))
    g = jax.nn.sigmoid(gd) @ g2
    kk = (k * k_k).astype(f32).reshape(B, T, H, N)
    kk = kk / jnp.maximum(jnp.sqrt(jnp.sum(kk * kk, -1, keepdims=True)), 1e-12)
    k = k.astype(f32) * (1.0 + (a - 1.0) * k_a.astype(f32))
    heads = lambda t: t.astype(f32).reshape(B, T, H, N)
    rh, kh, vh, wh, ah = heads(r), heads(k), heads(v), heads(decay), heads(a)
    bh = kk * ah

    def step(S, inp):
        r_t, w_t, k_t, v_t, kk_t, b_t = inp
        sa = jnp.einsum('bhvk,bhk->bhv', S, -kk_t)
        S = S * w_t[:, :, None, :] + sa[..., None] * b_t[:, :, None, :] + v_t[..., None] * k_t[:, :, None, :]
        return S, jnp.einsum('bhvk,bhk->bhv', S, r_t)

    xs = tuple(jnp.swapaxes(t, 0, 1) for t in (rh, wh, kh, vh, kk, bh))
    _, y = lax.scan(step, jnp.zeros((B, H, N, N), f32), xs)
    y = jnp.swapaxes(y, 0, 1)
    y = _layer_norm(y, RW_GN_EPS).reshape(B, T, RW_WIDTH) * gn_g.astype(f32) + gn_b.astype(f32)
    bonus = jnp.sum(rh * kh * r_k.astype(f32), -1, keepdims=True) * vh
    out = (y + bonus.reshape(B, T, RW_WIDTH)) * g.astype(f32)
    return out.astype(p.dtype)


def _gla_branch(p, a2, a_b, norm_g):
    B, T, _ = p.shape
    H, DK, DV, C = GLA_HEADS, GLA_DK, GLA_DV, GLA_CHUNK
    NC = T // C
    f32 = jnp.float32
    q, k, v, gg, ad = jnp.split(p, GLA_SPLITS, axis=-1)
    log_a = jax.nn.log_sigmoid((ad @ a2 + a_b).astype(f32)) / GLA_TAU

    def chunks(t, d):
        return t.astype(f32).reshape(B, NC, C, H, d).transpose(0, 3, 1, 2, 4)

    qc = chunks(q, DK) * (DK ** -0.5)
    kc, vc, lc = chunks(k, DK), chunks(v, DV), chunks(log_a, DK)
    b = jnp.cumsum(lc, axis=3)
    q_s = qc * jnp.exp(b)
    k_s = kc * jnp.exp(-b)
    causal = jnp.tril(jnp.ones((C, C), dtype=bool))
    att = jnp.where(causal, jnp.einsum('bhncd,bhnsd->bhncs', q_s, k_s), 0.0)
    o_intra = jnp.einsum('bhncs,bhnsv->bhncv', att, vc)
    b_last = b[:, :, :, -1:, :]
    chunk_kv = jnp.einsum('bhncd,bhncv->bhndv', kc * jnp.exp(b_last - b), vc)
    chunk_decay = jnp.exp(b_last[:, :, :, 0, :])

    def step(S, inp):
        dec, kv = inp
        return dec[..., None] * S + kv, S

    _, S_prev = lax.scan(step, jnp.zeros((B, H, DK, DV), f32),
                         (jnp.moveaxis(chunk_decay, 2, 0), jnp.moveaxis(chunk_kv, 2, 0)))
    S_prev = jnp.moveaxis(S_prev, 0, 2)
    o = o_intra + jnp.einsum('bhncd,bhndv->bhncv', q_s, S_prev)
    o = o.transpose(0, 2, 3, 1, 4).reshape(B, T, H, DV)
    o = o * lax.rsqrt(jnp.mean(o * o, -1, keepdims=True) + GLA_NORM_EPS) * norm_g.astype(f32)
    o = o.reshape(B, T, GLA_VW) * jax.nn.silu(gg.astype(f32))
    return o.astype(p.dtype)


def setup_inputs(seed: int = 0) -> dict:
    key = jax.random.key(seed)
    ks = jax.random.split(key, 32)
    L, D = DEPTH, D_MODEL
    nrm = lambda k, shape, s: jax.random.normal(k, shape, jnp.float32) * s
    rw_w0 = jnp.broadcast_to(jnp.linspace(-6.0, -1.0, RW_WIDTH, dtype=jnp.float32), (L, RW_WIDTH)) + nrm(ks[6], (L, RW_WIDTH), 0.1)
    return {
        "x": nrm(ks[0], (BATCH, SEQ, D), 1.0),
        "c": nrm(ks[1], (BATCH, D), 1.0),
        "w_ada": nrm(ks[2], (L, D, 6 * D), 0.5 * D ** -0.5),
        "b_ada": nrm(ks[3], (L, 6 * D), 0.02),
        "w_in": nrm(ks[4], (L, D, N_IN), D ** -0.5),
        "mu_rw": jax.random.uniform(ks[5], (L, RW_COLS), jnp.float32),
        "rw_w0": rw_w0,
        "rw_w2": nrm(ks[7], (L, DECAY_LORA, RW_WIDTH), 0.1 * DECAY_LORA ** -0.5),
        "rw_a0": nrm(ks[8], (L, RW_WIDTH), 0.1),
        "rw_a2": nrm(ks[9], (L, AAA_LORA, RW_WIDTH), 0.5 * AAA_LORA ** -0.5),
        "rw_g2": nrm(ks[10], (L, GATE_LORA, RW_WIDTH), GATE_LORA ** -0.5),
        "rw_k_k": 0.85 + nrm(ks[11], (L, RW_WIDTH), 0.02),
        "rw_k_a": 1.0 + nrm(ks[12], (L, RW_WIDTH), 0.02),
        "rw_r_k": nrm(ks[13], (L, RW_HEADS, RW_HEAD), 0.1),
        "rw_gn_g": 1.0 + nrm(ks[14], (L, RW_WIDTH), 0.05),
        "rw_gn_b": nrm(ks[15], (L, RW_WIDTH), 0.02),
        "gla_a2": nrm(ks[16], (L, GLA_GATE_LORA, GLA_KW), GLA_GATE_LORA ** -0.5),
        "gla_a_b": nrm(ks[17], (L, GLA_KW), 0.1),
        "gla_norm_g": 1.0 + nrm(ks[18], (L, GLA_DV), 0.05),
        "w_rw_branch": nrm(ks[19], (L, RW_WIDTH, D), BETA * RW_WIDTH ** -0.5),
        "w_gla_branch": nrm(ks[20], (L, GLA_VW, D), BETA * GLA_VW ** -0.5),
        "w_mix_out": nrm(ks[21], (L, D, D), BETA * D ** -0.5),
        "ln1_g": 1.0 + nrm(ks[22], (L, D), 0.05),
        "ln1_b": nrm(ks[23], (L, D), 0.02),
        "w_ffn_in": nrm(ks[24], (L, D, 2 * D_FF), D ** -0.5),
        "w_ffn_out": nrm(ks[25], (L, D_FF, D), BETA * D_FF ** -0.5),
        "ln2_g": 1.0 + nrm(ks[26], (L, D), 0.05),
        "ln2_b": nrm(ks[27], (L, D), 0.02),
    }


def reference(x, c, w_ada, b_ada, w_in, mu_rw, rw_w0, rw_w2, rw_a0, rw_a2, rw_g2, rw_k_k, rw_k_a,
              rw_r_k, rw_gn_g, rw_gn_b, gla_a2, gla_a_b, gla_norm_g, w_rw_branch, w_gla_branch,
              w_mix_out, ln1_g, ln1_b, w_ffn_in, w_ffn_out, ln2_g, ln2_b):
    for l in range(DEPTH):
        mod = (jax.nn.silu(c) @ w_ada[l] + b_ada[l])[:, None, :]
        shift1, scale1, gate1, shift2, scale2, gate2 = jnp.split(mod, 6, axis=-1)

        u = x * (1.0 + scale1) + shift1
        proj = u @ w_in[l]
        p_rw, p_gla, p_gate = jnp.split(proj, [RW_COLS, RW_COLS + GLA_COLS], axis=-1)
        gate_rw, gate_gla = jnp.split(p_gate, 2, axis=-1)
        o_rw = _rwkv7_branch(p_rw, mu_rw[l], rw_w0[l], rw_w2[l], rw_a0[l], rw_a2[l], rw_g2[l],
                             rw_k_k[l], rw_k_a[l], rw_r_k[l], rw_gn_g[l], rw_gn_b[l])
        o_gla = _gla_branch(p_gla, gla_a2[l], gla_a_b[l], gla_norm_g[l])
        merged = (jax.nn.sigmoid(gate_rw) * (o_rw @ w_rw_branch[l])
                  + jax.nn.sigmoid(gate_gla) * (o_gla @ w_gla_branch[l]))
        mix = merged @ w_mix_out[l]
        x = (_layer_norm(ALPHA * x + gate1 * mix, LN_EPS) * ln1_g[l] + ln1_b[l]).astype(x.dtype)

        u2 = x * (1.0 + scale2) + shift2
        h_gate, h_up = jnp.split(u2 @ w_ffn_in[l], 2, axis=-1)
        ffn = (jax.nn.silu(h_gate) * h_up) @ w_ffn_out[l]
        x = (_layer_norm(ALPHA * x + gate2 * ffn, LN_EPS) * ln2_g[l] + ln2_b[l]).astype(x.dtype)
    return x
```

```python
import math
from contextlib import ExitStack

import numpy as np
import concourse.bass as bass
import concourse.mybir as mybir
from concourse.bass_utils import run_bass_kernel_spmd

F32 = mybir.dt.float32
BF16 = mybir.dt.bfloat16
AF = mybir.ActivationFunctionType
ALU = mybir.AluOpType
AX = mybir.AxisListType

D = 1024
T = 2048
NCH = T // 128
RW = 1792
GL = 1552
NIN = 5392
DFF = 2816
ALPHA = 2.0 ** 0.25
EM05 = math.exp(-0.5)


class Buf:
    __slots__ = ("name", "ap", "writer", "readers")

    def __init__(self, name, ap):
        self.name = name
        self.ap = ap
        self.writer = None
        self.readers = {}

    def __getitem__(self, key):
        return self.ap[key]


class KB:
    def __init__(self, nc, stack, n_dma_sems=8):
        self.nc = nc
        self.engs = {"pe": nc.tensor, "act": nc.scalar, "dve": nc.vector, "pool": nc.gpsimd, "sp": nc.sync}
        self.sem, self.cnt, self.waited = {}, {}, {}
        for e in self.engs:
            self.sem[e] = stack.enter_context(nc.semaphore("s_" + e))
            self.cnt[e] = 0
            self.waited[e] = {}
        self.dma_sems, self.dma_val, self.dma_rr = {}, {}, {}
        for q in ("sp", "act", "pool"):
            self.dma_sems[q] = [stack.enter_context(nc.semaphore(f"d_{q}{i}")) for i in range(n_dma_sems)]
            self.dma_val[q] = [0] * n_dma_sems
            self.dma_rr[q] = 0
        self.out_events = []
        self.pending = {}

    def _wait(self, eng, ev):
        sem, val, _ = ev
        if self.waited[eng].get(sem.name, 0) >= val:
            return
        self.engs[eng].wait_ge(sem, val)
        self.waited[eng][sem.name] = val

    def _collect(self, eng, reads, writes):
        evs = {}

        def add(ev, kind):
            if ev is None:
                return
            sem, val, src = ev
            if src == eng and (eng == "pe" or kind == "war"):
                return
            if sem.name not in evs or evs[sem.name][1] < val:
                evs[sem.name] = ev
        for b in reads:
            add(b.writer, "raw")
        for b in writes:
            add(b.writer, "waw")
            for ev in b.readers.values():
                add(ev, "war")
        return evs

    def _record(self, ev, reads, writes):
        for b in reads:
            b.readers[ev[0].name] = ev
        for b in writes:
            b.writer = ev
            b.readers = {}

    def op(self, eng, fn, reads=(), writes=(), inc=True):
        for ev in self._collect(eng, reads, writes).values():
            self._wait(eng, ev)
        ins = fn(self.engs[eng])
        pend = self.pending.setdefault(eng, [])
        if not inc:
            pend.append((tuple(reads), tuple(writes)))
            return
        self.cnt[eng] += 1
        ins.then_inc(self.sem[eng], 1)
        ev = (self.sem[eng], self.cnt[eng], eng)
        for r_, w_ in pend:
            self._record(ev, r_, w_)
        pend.clear()
        self._record(ev, reads, writes)

    def dma(self, q, out, in_, reads=(), writes=(), is_output=False):
        for ev in self._collect(q, reads, writes).values():
            self._wait(q, ev)
        i = self.dma_rr[q]
        self.dma_rr[q] = (i + 1) % len(self.dma_sems[q])
        sem = self.dma_sems[q][i]
        prev = self.dma_val[q][i]
        if prev > 0:
            self._wait(q, (sem, prev, "dma"))
        ins = self.engs[q].dma_start(out=out, in_=in_)
        ins.then_inc(sem, 16)
        self.dma_val[q][i] = prev + 16
        ev = (sem, prev + 16, "dma")
        self._record(ev, reads, writes)
        if is_output:
            self.out_events.append(ev)

    def barrier(self):
        assert not any(self.pending.values()), "pending non-incrementing ops at barrier"
        evs = [(self.sem[e], self.cnt[e], e) for e in self.engs if self.cnt[e] > 0]
        for q in self.dma_sems:
            for s, v in zip(self.dma_sems[q], self.dma_val[q]):
                if v > 0:
                    evs.append((s, v, "dma"))
        for e in self.engs:
            for ev in evs:
                if ev[2] != e or e != "pe":
                    self._wait(e, ev)

    def finish(self):
        for ev in self.out_events:
            self._wait("sp", ev)
        self.final_counts = dict(self.cnt)
        KB.last = self


def _bc_rows(ap, nparts, n):
    return bass.AP(ap.tensor, ap.offset, [[0, nparts], [1, n]])


def build_nc(stop_after=99, taps=()):
    nc = bass.Bass("TRN2", target_bir_lowering=False)
    din = {}

    def inp(name, shape):
        din[name] = nc.dram_tensor(name, list(shape), F32, kind="ExternalInput").ap()
        return din[name]

    xT_d = inp("xT", [D, T])
    x_d = inp("x", [T, D])
    cpp_d = inp("cpp", [128, 8])
    wada_d = inp("w_ada", [D, 6 * D])
    bpp_d = inp("b_pp", [128, 32])
    brow_d = inp("b_row", [1, 6 * D])
    win_d = inp("w_in", [D, NIN])
    mu_d = inp("mu", [1, RW])
    w0_d = inp("rw_w0", [1, 512])
    a0_d = inp("rw_a0", [1, 512])
    w2_d = inp("rw_w2", [64, 512])
    a2_d = inp("rw_a2", [64, 512])
    g2_d = inp("rw_g2", [128, 512])
    kk_d = inp("rw_k_k", [1, 512])
    ka_d = inp("rw_k_a", [1, 512])
    rk_d = inp("rw_r_k", [1, 512])
    gng_d = inp("rw_gn_g", [1, 512])
    gnb_d = inp("rw_gn_b", [1, 512])
    ga2_d = inp("gla_a2", [16, 256])
    gab_d = inp("gla_a_b", [1, 256])
    gng2_d = inp("gla_norm_g", [1, 128])
    wbr_d = inp("w_rw_branch", [512, D])
    wbg_d = inp("w_gla_branch", [512, D])
    wmix_d = inp("w_mix_out", [D, D])
    ln1g_d = inp("ln1_g", [1, D])
    ln1b_d = inp("ln1_b", [1, D])
    wfi_d = inp("w_ffn_in", [D, 2 * DFF])
    wfo_d = inp("w_ffn_out", [DFF, D])
    ln2g_d = inp("ln2_g", [1, D])
    ln2b_d = inp("ln2_b", [1, D])
    cident_d = inp("c_ident", [128, 128])
    ctri_d = inp("c_tri", [128, 6, 128])
    y_d = nc.dram_tensor("y", [T, D], F32, kind="ExternalOutput").ap()
    tap_d = {}

    with ExitStack() as st0:
        k = KB(nc, st0)

        def sb(stack, name, shape, dt):
            t = stack.enter_context(nc.sbuf_tensor("sb_" + name, list(shape), dt))
            return Buf(name, t[:])

        def MM(out, lhsT, rhs, st, sp, R, W, last=None):
            k.op("pe", lambda e: e.matmul(out, lhsT, rhs, start=st, stop=sp), R, W, inc=(sp if last is None else last))

        def TR(out, in_, idn, R, W, last=True):
            k.op("pe", lambda e: e.transpose(out, in_, idn), R, W, inc=last)

        def ACT(out, in_, fn, R, W, bias=None, scale=None):
            kw = {}
            if bias is not None:
                kw["bias"] = bias
            if scale is not None:
                kw["scale"] = scale
            k.op("act", lambda e: e.activation(out, in_, fn, **kw), R, W)

        def TT(eng, out, a, b, op, R, W):
            if "nopool" in taps and eng == "pool":
                eng = "dve"
            k.op(eng, lambda e: e.tensor_tensor(out, a, b, op), R, W)

        def STT(eng, out, a, s, b, op0, op1, R, W):
            k.op(eng, lambda e: e.scalar_tensor_tensor(out, a, s, b, op0, op1), R, W)

        def TS(eng, out, a, s1, s2, op0, op1, R, W):
            if op1 is None:
                k.op(eng, lambda e: e.tensor_scalar(out, a, s1, None, op0), R, W)
            else:
                k.op(eng, lambda e: e.tensor_scalar(out, a, s1, s2, op0, op1), R, W)

        def CP(eng, out, in_, R, W):
            if "nopool" in taps and eng == "pool":
                eng = "dve"
            if eng == "act":
                ACT(out, in_, AF.Copy, R, W)
            else:
                k.op(eng, lambda e: e.tensor_copy(out, in_), R, W)

        def tap(name, ap, shape, reads):
            if name not in taps:
                return
            tap_d[name] = nc.dram_tensor("tap_" + name, list(shape), F32, kind="ExternalOutput").ap()
            k.dma("pool", tap_d[name], ap, reads=reads, is_output=True)

        banks = []
        for i in range(8):
            t = st0.enter_context(nc.psum_tensor(f"pb{i}", [128, 512], F32))
            banks.append(Buf(f"pb{i}", t[:]))
        bank_rr = [0]

        pinned = set()

        def bank(pin=False):
            for _ in range(8):
                b = banks[bank_rr[0]]
                bank_rr[0] = (bank_rr[0] + 1) % 8
                if b.name not in pinned:
                    if pin:
                        pinned.add(b.name)
                    return b
            raise RuntimeError("all PSUM banks pinned")

        def unpin(*bs):
            for b in bs:
                pinned.discard(b.name)

        big = sb(st0, "big", [128, 16400], F32)
        bigb = big.ap.bitcast(BF16)
        uT_ap = bigb[:, 0:16416].rearrange("p (c t) -> p c t", c=8)
        orwT_ap = bigb[:, 16416:24608].rearrange("p (c t) -> p c t", c=4)
        oglaT_ap = bigb[:, 24608:32800].rearrange("p (c t) -> p c t", c=4)
        x1_ap = big.ap[:, 0:16384].rearrange("p (b d) -> p b d", b=16)
        uTb = [Buf(f"uT{c}", None) for c in range(8)]
        orwTb = Buf("orwT", None)
        oglaTb = Buf("oglaT", None)
        x1b = [Buf(f"x1_{b}", None) for b in range(16)]

        identf = sb(st0, "identf", [128, 128], F32)
        identb = sb(st0, "identb", [128, 128], BF16)
        modp = sb(st0, "modp", [128, 32], F32)
        ones1 = sb(st0, "ones1", [1, 128], F32)
        scb = sb(st0, "scb", [128, 8], BF16)
        k.dma("sp", identf[:], cident_d, writes=[identf])
        k.dma("pool", identb[:], cident_d, writes=[identb])
        k.op("dve", lambda e: e.memset(ones1[:], 1.0), writes=[ones1])

        win_v = win_d.rearrange("(c p) n -> p c n", p=128)

        with ExitStack() as st:
            Nb = [[sb(st, f"N{h}{i}", [128, 4, 128], BF16) for i in range(2)] for h in range(2)]
            Lb = [[sb(st, f"L{h}{i}", [128, 4, 128], BF16) for i in range(2)] for h in range(2)]
            Sm = [sb(st, f"Sm{h}", [128, 4, 128], BF16) for h in range(2)]
            W1 = [sb(st, f"W1_{c}", [128, RW], BF16) for c in range(8)]
            W2 = [sb(st, f"W2_{c}", [128, RW], BF16) for c in range(8)]
            with ExitStack() as stp:
                cpp = sb(stp, "cpp", [128, 8], F32)
                bpp = sb(stp, "bpp", [128, 32], F32)
                wa = [sb(stp, f"wa{i}", [128, 8, 1024], BF16) for i in range(2)]
                mur = sb(stp, "mur", [128, RW], F32)
                omr = sb(stp, "omr", [128, RW], F32)
                stg = [sb(stp, f"stg{i}", [128, T], F32) for i in range(2)]
                k.dma("sp", cpp[:], cpp_d, writes=[cpp])
                k.dma("sp", bpp[:], bpp_d, writes=[bpp])
                ACT(scb[:], cpp[:], AF.Silu, [cpp], [scb])
                wada_v = wada_d.rearrange("(c p) n -> p c n", p=128)
                parts = (0, 1, 3, 4)
                for pi in range(2):
                    k.dma("pool", wa[pi][:], wada_v[:, :, parts[pi] * 1024:(parts[pi] + 1) * 1024], writes=[wa[pi]])
                k.dma("sp", mur[:], _bc_rows(mu_d, 128, RW), writes=[mur])
                TS("dve", omr[:], mur[:], -1.0, 1.0, ALU.mult, ALU.add, [mur], [omr])
                pm = bank(True)

                def p0_mm(pi):
                    w = wa[pi % 2]
                    for m in range(8):
                        col = pi * 8 + m
                        for c in range(8):
                            MM(pm[:, col:col + 1], w[:, c, m * 128:(m + 1) * 128], scb[:, c:c + 1], c == 0, c == 7, [w, scb], [pm], last=(m == 7 and c == 7))

                def prep(c):
                    w_ = stg[c % 2]
                    k.dma("sp", w_[:, 0:RW], win_v[:, c, 0:RW], writes=[w_])
                    TT("dve", W1[c][:], w_[:, 0:RW], omr[:], ALU.mult, [w_, omr], [W1[c]])
                    TT("pool", W2[c][:], w_[:, 0:RW], mur[:], ALU.mult, [w_, mur], [W2[c]])
                for c in range(4):
                    prep(c)
                p0_mm(0)
                p0_mm(1)
                for pi in range(2, 4):
                    k.dma("pool", wa[pi % 2][:], wada_v[:, :, parts[pi] * 1024:(parts[pi] + 1) * 1024], writes=[wa[pi % 2]])
                for c in range(4, 8):
                    prep(c)
                p0_mm(2)
                p0_mm(3)
                TT("dve", modp[:], pm[:, 0:32], bpp[:], ALU.add, [pm, bpp], [modp])
                unpin(pm)
                TS("dve", modp[:, 8:16], modp[:, 8:16], 1.0, None, ALU.add, None, [modp], [modp])
                TS("dve", modp[:, 24:32], modp[:, 24:32], 1.0, None, ALU.add, None, [modp], [modp])
                tap("modp", modp[:], [128, 32], [modp])
                for c in range(8):
                    s_ = stg[c % 2]
                    k.dma("sp", s_[:], xT_d[c * 128:(c + 1) * 128, :], writes=[s_])
                    k.op("dve", lambda e, c=c: e.memset(uT_ap[:, c, 0:1], 0.0), writes=[uTb[c]])
                    ACT(uT_ap[:, c, 1:T + 1], s_[:], AF.Identity, [s_, modp], [uTb[c]],
                        bias=modp[:, c:c + 1], scale=modp[:, 8 + c:9 + c])
                tap("uT", uT_ap[:, :, 1:T + 1], [128, 8, T], uTb)
                k.barrier()

            rows = sb(st, "rwrows", [128, 5, 512], F32)
            w0r = sb(st, "w0r", [1, 512], F32)
            a0r = sb(st, "a0r", [1, 512], F32)
            w2b = sb(st, "w2b", [128, 512], BF16)
            a2b = sb(st, "a2b", [128, 512], BF16)
            g2b = sb(st, "g2b", [128, 512], BF16)
            Mtri = sb(st, "Mtri", [128, 3, 128], F32)
            negcol = sb(st, "negcol", [128, 1], F32)
            mSI = sb(st, "mSI", [128, 2, 2, 128], F32)
            mND = sb(st, "mND", [128, 2, 128], F32)
            mSL4 = sb(st, "mSL4", [128, 2, 4, 128], F32)
            idb4 = sb(st, "idb4", [128, 4, 128], BF16)
            id8f = sb(st, "id8f", [64, 8, 64], F32)
            Hb = [sb(st, f"Hb{i}", [128, 8, 64], BF16) for i in range(2)]
            for i, d_ in enumerate((kk_d, ka_d, rk_d, gng_d, gnb_d)):
                k.dma("sp", rows[:, i, :], _bc_rows(d_, 128, 512), writes=[rows])
            k.dma("sp", w0r[:], w0_d, writes=[w0r])
            k.dma("sp", a0r[:], a0_d, writes=[a0r])
            k.op("dve", lambda e: e.memset(w2b[:], 0.0), writes=[w2b])
            k.op("dve", lambda e: e.memset(a2b[:], 0.0), writes=[a2b])
            k.dma("pool", w2b[0:64, :], w2_d, writes=[w2b])
            k.dma("pool", a2b[64:128, :], a2_d, writes=[a2b])
            k.dma("pool", g2b[:], g2_d, writes=[g2b])
            k.dma("sp", Mtri[:], ctri_d[:, 0:3, :], writes=[Mtri])
            TS("dve", Mtri[:], Mtri[:], -EM05, None, ALU.mult, None, [Mtri], [Mtri])
            k.op("dve", lambda e: e.memset(negcol[:], -EM05), writes=[negcol])
            for h2 in range(2):
                k.dma("sp", mSI[:, h2, 0, :], ctri_d[:, 0, :], writes=[mSI])
                k.dma("sp", mND[:, h2, :], ctri_d[:, 3, :], writes=[mND])
                k.dma("sp", mSI[:, h2, 1, :], ctri_d[:, 1, :], writes=[mSI])
            for h4 in range(4):
                k.dma("sp", mSL4[:, 0, h4, :], ctri_d[:, 4, :], writes=[mSL4])
                k.dma("sp", mSL4[:, 1, h4, :], ctri_d[:, 5, :], writes=[mSL4])
                k.dma("pool", idb4[:, h4, :], cident_d, writes=[idb4])
            for h in range(8):
                k.dma("sp", id8f[:, h, :], cident_d[0:64, 0:64], writes=[id8f])
            k.op("dve", lambda e: e.memset(Hb[0][:], 0.0), writes=[Hb[0]])
            k.op("dve", lambda e: e.memset(Hb[1][:], 0.0), writes=[Hb[1]])

            def f32t(name):
                return sb(st, name, [128, 512], F32)
            sgm, a_t, g_t, r_t, k_t, v_t = [f32t(n) for n in ("sgm", "a_t", "g_t", "r_t", "k_t", "v_t")]
            kkn, kmod, bvec = f32t("kkn"), f32t("kmod"), f32t("bvec")
            S0 = f32t("S0")
            ogl_f = big.ap[:, 12304:16400]
            EinT = Buf("EinT", ogl_f[:, 0:512].rearrange("p (c t) -> p c t", c=4))
            EninT = Buf("EninT", ogl_f[:, 512:1024].rearrange("p (c t) -> p c t", c=4))
            EexT = Buf("EexT", ogl_f[:, 1024:1536].rearrange("p (c t) -> p c t", c=4))
            bon = Buf("bon", ogl_f[:, 1536:2048])
            S1 = Buf("S1", ogl_f[:, 2048:2560])
            S2 = Buf("S2", ogl_f[:, 2560:3072])
            Eex = Buf("Eex", ogl_f[:, 3072:3584])
            Erev = Buf("Erev", ogl_f[:, 3584:4096])
            v_bf = sb(st, "v_bf", [128, 512], BF16)
            twad = sb(st, "twad", [128, 128], BF16)
            sgT = sb(st, "sgT", [128, 128], BF16)
            small = sb(st, "small", [128, 6, 8], F32)
            X = sb(st, "X", [128, 8, 2, 64], BF16)
            Bh = sb(st, "Bh", [128, 512], BF16)
            Kh = sb(st, "Kh", [128, 512], BF16)
            AR = sb(st, "AR", [128, 4, 2, 128], BF16)
            BTz = sb(st, "BTz", [128, 4, 2, 128], BF16)
            KTz = sb(st, "KTz", [128, 4, 2, 128], BF16)
            k.op("dve", lambda e: e.memset(BTz[:], 0.0), writes=[BTz])
            k.op("dve", lambda e: e.memset(KTz[:], 0.0), writes=[KTz])
            gC = sb(st, "gC", [64, 8], F32)
            ArbT = [sb(st, f"ArbT{h}", [128, 4, 128], BF16) for h in range(2)]
            MakT = [sb(st, f"MakT{h}", [128, 4, 128], BF16) for h in range(2)]
            ArkT = [sb(st, f"ArkT{h}", [128, 4, 128], BF16) for h in range(2)]
            WU = sb(st, "WU", [128, 8, 2, 64], BF16)
            Dg = sb(st, "Dg", [64, 8, 64], F32)
            PTb = sb(st, "PTb", [128, 8, 64], BF16)
            QeT = sb(st, "QeT", [128, 8, 128], BF16)
            k.op("dve", lambda e: e.memset(PTb[:], 0.0), writes=[PTb])
            k.op("dve", lambda e: e.memset(QeT[:], 0.0), writes=[QeT])
            o_bf = sb(st, "o_bf", [128, 512], BF16)

            def v3(ap, a):
                return ap.rearrange("p (a b) -> p a b", a=a)

            def bfv(b_, half):
                return b_.ap.bitcast(BF16)[:, half * 512:(half + 1) * 512].rearrange("p (c t) -> p c t", c=4)
            alias3 = [(bfv(sgm, hf_), bfv(a_t, hf_), bfv(k_t, hf_)) for hf_ in range(2)]
            aliasb = [Buf(f"alias{hf_}", None) for hf_ in range(2)]

            def hv(b_, kind):
                out = []
                for hf_ in range(2):
                    if kind == "tok":
                        ap_ = b_.ap[:, hf_ * 256:(hf_ + 1) * 256]
                    elif kind == "ch":
                        ap_ = b_.ap[:, 2 * hf_:2 * hf_ + 2]
                    elif kind == "hd":
                        ap_ = b_.ap[:, 4 * hf_:4 * hf_ + 4]
                    else:
                        ap_ = b_.ap[:, :, 4 * hf_:4 * hf_ + 4]
                    out.append(Buf(f"{b_.name}_{hf_}", ap_))
                return out
            sgmH, a_tH, g_tH, r_tH, k_tH, v_tH = [hv(b_, "tok") for b_ in (sgm, a_t, g_t, r_t, k_t, v_t)]
            kknH, kmodH, bvecH, S0H, S1H, S2H = [hv(b_, "tok") for b_ in (kkn, kmod, bvec, S0, S1, S2)]
            EexH, ErevH, bonH, v_bfH, BhH, KhH, o_bfH = [hv(b_, "tok") for b_ in (Eex, Erev, bon, v_bf, Bh, Kh, o_bf)]
            EinTH, EninTH, EexTH, ARH, BTzH, KTzH = [hv(b_, "ch") for b_ in (EinT, EninT, EexT, AR, BTz, KTz)]
            XH, WUH, DgH, PTbH, QeTH, gCH = [hv(b_, "hd") for b_ in (X, WU, Dg, PTb, QeT, gC)]
            HbH = [hv(b_, "hd") for b_ in Hb]
            smallH = hv(small, "sm")
            orwTH = [Buf(f"orwT_{hf_}", None) for hf_ in range(2)]

            nch_run = NCH if "rw_short" not in taps else 2
            if "rw_cut0" in taps:
                nch_run = 0
            def PROJ(n):
                t0 = n * 128
                ucur = [uT_ap[:, c, t0 + 1:t0 + 129] for c in range(8)]
                uprv = [uT_ap[:, c, t0:t0 + 128] for c in range(8)]

                def proj_tok(pb_, c0, c1):
                    for c in range(8):
                        MM(pb_[:, 0:c1 - c0], ucur[c], W1[c][:, c0:c1], c == 0, False, [uTb[c], W1[c]], [pb_])
                        MM(pb_[:, 0:c1 - c0], uprv[c], W2[c][:, c0:c1], False, c == 7, [uTb[c], W2[c]], [pb_])

                def proj_ch(out_ap, pb_, c0, c1):
                    for c in range(8):
                        MM(out_ap, W1[c][:, c0:c1], ucur[c], c == 0, False, [uTb[c], W1[c]], [pb_])
                        MM(out_ap, W2[c][:, c0:c1], uprv[c], False, c == 7, [uTb[c], W2[c]], [pb_])

                pL = bank()
                pLv = v3(pL[:], 4)
                proj_ch(pLv[:, 0, :], pL, 1536, 1664)
                proj_ch(pLv[:, 2, :], pL, 1664, 1792)
                ACT(twad[0:64, :], pLv[0:64, 0, :], AF.Tanh, [pL], [twad])
                ACT(twad[64:128, :], pLv[64:128, 0, :], AF.Copy, [pL], [twad])
                ACT(sgT[:], pLv[:, 2, :], AF.Sigmoid, [pL], [sgT])
                pR, pK, pV = bank(True), bank(True), bank(True)
                proj_tok(pR, 0, 512)
                proj_tok(pK, 512, 1024)
                proj_tok(pV, 1024, 1536)
                pW, pA, pG = bank(True), bank(True), bank(True)
                MM(pW[:], twad[:], w2b[:], True, False, [twad, w2b], [pW])
                MM(pW[:], ones1[:], w0r[:], False, True, [ones1, w0r], [pW])
                MM(pA[:], twad[:], a2b[:], True, False, [twad, a2b], [pA])
                MM(pA[:], ones1[:], a0r[:], False, True, [ones1, a0r], [pA])
                MM(pG[:], sgT[:], g2b[:], True, True, [sgT, g2b], [pG])
                return pR, pK, pV, pW, pA, pG

            nxt = PROJ(0) if nch_run > 0 else None
            e1_done = False
            for n in range(nch_run):
                t0 = n * 128
                if not (n > 0 and e1_done):
                    pR, pK, pV, pW, pA, pG = nxt
                Hc, Hn = HbH[n % 2], HbH[(n + 1) % 2]
                pg = bank(True)
                pCs, pTs, pYs = [None, None], [None, None], [None, None]
                v4 = lambda ap: ap.rearrange("p (a b) -> p a b", a=4)
                cs_ = lambda hf: slice(hf * 256, hf * 256 + 256)

                def E1(hf):
                    if hf == 1:
                        return
                    ACT(sgm[:], pW[:], AF.Sigmoid, [pW], sgmH)
                    CP("dve", k_t[:], pK[:], [pK], k_tH)
                    ACT(a_t[:], pA[:], AF.Sigmoid, [pA], a_tH)
                    ACT(r_t[:], pR[:], AF.Copy, [pR], r_tH)
                    ACT(v_t[:], pV[:], AF.Copy, [pV], v_tH)
                    CP("pool", v_bf[:], v_t[:], v_tH, v_bfH)
                    unpin(pR, pK, pV, pW, pA)

                def Eg():
                    ACT(g_t[:], pG[:], AF.Copy, [pG], g_tH)
                    unpin(pG)

                def C1(hf):
                    s_ = sgmH[hf]
                    pC, pT = bank(True), bank(True)
                    MM(pC[:, 0:256], Mtri[:, 0, :], s_[:], True, True, [Mtri, s_], [pC], last=False)
                    MM(pC[:, 256:512], Mtri[:, 2, :], s_[:], True, True, [Mtri, s_], [pC])
                    for i in range(2):
                        MM(v3(pT[:], 4)[:, i, :], s_[:, i * 128:(i + 1) * 128], Mtri[:, 1, :], True, True, [Mtri, s_], [pT], last=False)
                        MM(v3(pT[:], 4)[:, 2 + i, :], s_[:, i * 128:(i + 1) * 128], Mtri[:, 0, :], True, True, [Mtri, s_], [pT], last=(i == 1))
                    for j in range(4):
                        MM(pg[0:64, 4 * hf + j:4 * hf + j + 1], s_[:, j * 64:(j + 1) * 64], negcol[:], True, True, [s_, negcol], [pg], last=(j == 3))
                    pCs[hf], pTs[hf] = pC, pT

                def X1(hf):
                    pC, pT = pCs[hf], pTs[hf]
                    ACT(EexH[hf][:], pC[:, 0:256], AF.Exp, [pC], [EexH[hf]])
                    ACT(ErevH[hf][:], pC[:, 256:512], AF.Exp, [pC], [ErevH[hf]])
                    ACT(EinTH[hf][:], v3(pT[:], 4)[:, 0:2, :], AF.Exp, [pT], [EinTH[hf]])
                    ACT(EninTH[hf][:], v3(pT[:], 4)[:, 0:2, :], AF.Exp, [pT], [EninTH[hf]], scale=-1.0)
                    ACT(EexTH[hf][:], v3(pT[:], 4)[:, 2:4, :], AF.Exp, [pT], [EexTH[hf]])
                    ACT(gCH[hf][:], pg[0:64, 4 * hf:4 * hf + 4], AF.Exp, [pg], [gCH[hf]])
                    unpin(pC, pT)
                    if hf == 1:
                        unpin(pg)

                def K1(hf):
                    cs, sm = cs_(hf), smallH[hf]
                    TT("pool", S0H[hf][:], k_tH[hf][:], rows[:, 0, cs], ALU.mult, [k_tH[hf], rows], [S0H[hf]])
                    TT("pool", S1H[hf][:], S0H[hf][:], S0H[hf][:], ALU.mult, [S0H[hf]], [S1H[hf]])
                    k.op("dve", lambda e: e.tensor_reduce(sm[:, 0, :], v4(S1H[hf][:]), AX.X, ALU.add), [S1H[hf]], [sm])
                    TS("dve", sm[:, 1, :], sm[:, 0, :], 1e-24, None, ALU.add, None, [sm], [sm])
                    k.op("dve", lambda e: e.reciprocal(sm[:, 1, :], sm[:, 1, :]), [sm], [sm])
                    ACT(sm[:, 1, :], sm[:, 1, :], AF.Sqrt, [sm], [sm])
                    TT("dve", v4(kknH[hf][:]), v4(S0H[hf][:]), sm[:, 1, :].unsqueeze(2).to_broadcast([128, 4, 64]), ALU.mult, [S0H[hf], sm], [kknH[hf]])

                def A1(hf):
                    cs = cs_(hf)
                    STT("dve", S2H[hf][:], a_tH[hf][:], -1.0, rows[:, 1, cs], ALU.add, ALU.mult, [a_tH[hf], rows], [S2H[hf]])
                    STT("dve", kmodH[hf][:], S2H[hf][:], 1.0, k_tH[hf][:], ALU.add, ALU.mult, [S2H[hf], k_tH[hf]], [kmodH[hf]])
                    TT("pool", bvecH[hf][:], kknH[hf][:], a_tH[hf][:], ALU.mult, [kknH[hf], a_tH[hf]], [bvecH[hf]])

                def O1(hf):
                    STT("dve", XH[hf][:, :, 0, :], v4(kknH[hf][:]), -1.0, v4(EexH[hf][:]), ALU.mult, ALU.mult, [kknH[hf], EexH[hf]], [XH[hf]])
                    TT("dve", BhH[hf][:], bvecH[hf][:], ErevH[hf][:], ALU.mult, [bvecH[hf], ErevH[hf]], [BhH[hf]])
                    TT("pool", KhH[hf][:], kmodH[hf][:], ErevH[hf][:], ALU.mult, [kmodH[hf], ErevH[hf]], [KhH[hf]])

                def B1(hf):
                    cs, sm = cs_(hf), smallH[hf]
                    TT("pool", S1H[hf][:], r_tH[hf][:], kmodH[hf][:], ALU.mult, [r_tH[hf], kmodH[hf]], [S1H[hf]])
                    TT("pool", S1H[hf][:], S1H[hf][:], rows[:, 2, cs], ALU.mult, [S1H[hf], rows], [S1H[hf]])
                    k.op("dve", lambda e: e.tensor_reduce(sm[:, 2, :], v4(S1H[hf][:]), AX.X, ALU.add), [S1H[hf]], [sm])
                    TT("dve", v4(bonH[hf][:]), v4(v_tH[hf][:]), sm[:, 2, :].unsqueeze(2).to_broadcast([128, 4, 64]), ALU.mult, [v_tH[hf], sm], [bonH[hf]])

                def T1(hf):
                    pA_, pB_ = bank(), bank()
                    for i in range(2):
                        sl = slice(i * 128, (i + 1) * 128)
                        TR(v3(pA_[:], 4)[:, i, :], kknH[hf][:, sl], identf[:], [kknH[hf], identf], [pA_], last=False)
                        TR(v3(pA_[:], 4)[:, 2 + i, :], r_tH[hf][:, sl], identf[:], [r_tH[hf], identf], [pA_], last=(i == 1))
                        TR(v3(pB_[:], 4)[:, i, :], bvecH[hf][:, sl], identf[:], [bvecH[hf], identf], [pB_], last=False)
                        TR(v3(pB_[:], 4)[:, 2 + i, :], kmodH[hf][:, sl], identf[:], [kmodH[hf], identf], [pB_], last=(i == 1))
                    ARh = ARH[hf]
                    STT("dve", ARh[:, :, 0, :], v3(pA_[:], 4)[:, 0:2, :], -1.0, EexTH[hf][:], ALU.mult, ALU.mult, [pA_, EexTH[hf]], [ARh])
                    TT("dve", ARh[:, :, 1, :], v3(pA_[:], 4)[:, 2:4, :], EinTH[hf][:], ALU.mult, [pA_, EinTH[hf]], [ARh])
                    for hh in range(2):
                        ps_ = slice(hh * 64, hh * 64 + 64)
                        TT("dve", BTzH[hf][ps_, :, hh, :], v3(pB_[:], 4)[ps_, 0:2, :], EninTH[hf][ps_], ALU.mult, [pB_, EninTH[hf]], [BTzH[hf]])
                        TT("dve", KTzH[hf][ps_, :, hh, :], v3(pB_[:], 4)[ps_, 2:4, :], EninTH[hf][ps_], ALU.mult, [pB_, EninTH[hf]], [KTzH[hf]])

                def I1(hf):
                    pLA = [bank(), bank()]
                    pMA = [bank(), bank()]
                    pLL = bank()
                    for j in range(4):
                        c4l, hh = j // 2, j % 2
                        ar_rhs = ARH[hf][:, c4l, :, :].rearrange("p a t -> p (a t)")
                        o1 = pLA[j // 2][:, (j % 2) * 256:(j % 2) * 256 + 256]
                        o2 = pMA[j // 2][:, (j % 2) * 256:(j % 2) * 256 + 256]
                        MM(o1, BTzH[hf][:, c4l, hh, :], ar_rhs, True, True, [BTzH[hf], ARH[hf]], [pLA[j // 2]], last=(j == 3))
                        MM(o2, KTzH[hf][:, c4l, hh, :], ar_rhs, True, True, [KTzH[hf], ARH[hf]], [pMA[j // 2]], last=(j == 3))
                        MM(pLL[:, j * 128:(j + 1) * 128], ARH[hf][:, c4l, 0, :], BTzH[hf][:, c4l, hh, :], True, True, [ARH[hf], BTzH[hf]], [pLL], last=(j == 3))
                    N0, L0 = Nb[hf][0], Lb[hf][0]
                    for q in range(2):
                        src = pLA[q][:].rearrange("p (h a t) -> p h a t", h=2, a=2)
                        TT("dve", N0[:, 2 * q:2 * q + 2, :], src[:, :, 0, :], mND[:], ALU.mult, [pLA[q], mND], [N0])
                        TT("dve", ArbT[hf][:, 2 * q:2 * q + 2, :], src[:, :, 1, :], mSI[:, :, 1, :], ALU.mult, [pLA[q], mSI], [ArbT[hf]])
                        src2 = pMA[q][:].rearrange("p (h a t) -> p h a t", h=2, a=2)
                        TT("dve", MakT[hf][:, 2 * q:2 * q + 2, :], src2[:, :, 0, :], mSI[:, :, 0, :], ALU.mult, [pMA[q], mSI], [MakT[hf]])
                        TT("dve", ArkT[hf][:, 2 * q:2 * q + 2, :], src2[:, :, 1, :], mSI[:, :, 1, :], ALU.mult, [pMA[q], mSI], [ArkT[hf]])
                    TT("dve", L0[:], v3(pLL[:], 4), mSL4[:, 0], ALU.mult, [pLL, mSL4], [L0])
                    TT("dve", bfv(sgm, hf), v3(pLL[:], 4), mSL4[:, 1], ALU.mult, [pLL, mSL4], [sgmH[hf]])
                    TT("pool", Sm[hf][:], N0[:], idb4[:], ALU.add, [N0, idb4], [Sm[hf]])

                def NLa(hf, lev):
                    cur = lev % 2
                    Nc, Lc = Nb[hf][cur], Lb[hf][cur]
                    Nn, Ln = Nb[hf][1 - cur], Lb[hf][1 - cur]
                    pL2 = bank()
                    for j in range(4):
                        MM(v3(pL2[:], 4)[:, j, :], Nc[:, j, :], Lc[:, j, :], True, True, [Nc, Lc], [pL2], last=(j == 3))
                    if lev < 4:
                        pN2 = bank()
                        for j in range(4):
                            MM(v3(pN2[:], 4)[:, j, :], Lc[:, j, :], Nc[:, j, :], True, True, [Nc, Lc], [pN2], last=(j == 3))
                    ACT(Ln[:], v3(pL2[:], 4), AF.Copy, [pL2], [Ln])
                    if lev < 4:
                        CP("dve", Nn[:], v3(pN2[:], 4), [pN2], [Nn])

                def NLb(hf, lev):
                    Ln = Lb[hf][1 - (lev % 2)]
                    pS = bank()
                    for j in range(4):
                        MM(v3(pS[:], 4)[:, j, :], Ln[:, j, :], Sm[hf][:, j, :], True, False, [Ln, Sm[hf]], [pS])
                        MM(v3(pS[:], 4)[:, j, :], identb[:], Sm[hf][:, j, :], False, True, [identb, Sm[hf]], [pS], last=(j == 3))
                    CP("act" if hf == 0 else "dve", Sm[hf][:], v3(pS[:], 4), [pS], [Sm[hf]])

                def MGa(hf):
                    Lo_ap, Tm_ap, Zb_ap = bfv(sgm, hf), bfv(a_t, hf), bfv(k_t, hf)
                    pTt = bank()
                    pTtb = pTt[:].bitcast(BF16)[:, 0:512].rearrange("p (c t) -> p c t", c=4)
                    for j in range(4):
                        TR(pTtb[:, j, :], Sm[hf][:, j, :], identb[:], [Sm[hf], identb], [pTt], last=(j == 3))
                    ACT(Tm_ap, pTtb, AF.Copy, [pTt], [a_tH[hf]])
                    pZ_ = bank()
                    for j in range(4):
                        MM(v3(pZ_[:], 4)[:, j, :], Lo_ap[:, j, :], Sm[hf][:, j, :], True, True, [sgmH[hf], Sm[hf]], [pZ_], last=(j == 3))
                    CP("dve", Zb_ap, v3(pZ_[:], 4), [pZ_], [k_tH[hf]])

                def MGb(hf):
                    Tm_ap, Zb_ap = bfv(a_t, hf), bfv(k_t, hf)
                    pS = bank()
                    for j in range(4):
                        MM(v3(pS[:], 4)[:, j, :], Tm_ap[:, j, :], Zb_ap[:, j, :], True, False, [a_tH[hf], k_tH[hf]], [pS])
                        MM(v3(pS[:], 4)[:, j, :], identb[:], Sm[hf][:, j, :], False, True, [identb, Sm[hf]], [pS], last=(j == 3))
                    ACT(Sm[hf][:], v3(pS[:], 4), AF.Copy, [pS], [Sm[hf]])

                def P1a(hf):
                    pMV = bank()
                    for j in range(4):
                        MM(pMV[:, j * 64:(j + 1) * 64], MakT[hf][:, j, :], v_bfH[hf][:, j * 64:(j + 1) * 64], True, True, [MakT[hf], v_bfH[hf]], [pMV], last=(j == 3))
                    ACT(XH[hf][:, :, 1, :], v4(pMV[:, 0:256]), AF.Copy, [pMV], [XH[hf]])

                def P1b(hf):
                    pWU = bank()
                    for j in range(4):
                        MM(pWU[:, j * 128:(j + 1) * 128], Sm[hf][:, j, :], XH[hf][:, j, :, :].rearrange("p a b -> p (a b)"), True, True, [Sm[hf], XH[hf]], [pWU], last=(j == 3))
                    ACT(WUH[hf][:].rearrange("p h a b -> p (h a b)"), pWU[:], AF.Copy, [pWU], [WUH[hf]])

                def P1c(hf):
                    pP = bank()
                    for j in range(4):
                        MM(pP[0:64, j * 64:(j + 1) * 64], WUH[hf][:, j, 0, :], BhH[hf][:, j * 64:(j + 1) * 64], True, True, [WUH[hf], BhH[hf]], [pP], last=(j == 3))
                    TT("pool", DgH[hf][:], id8f[:, 0:4, :], gCH[hf][:].unsqueeze(2).to_broadcast([64, 4, 64]), ALU.mult, [id8f, gCH[hf]], [DgH[hf]])
                    TT("dve", PTbH[hf][0:64], v4(pP[0:64, 0:256]), DgH[hf][:], ALU.add, [pP, DgH[hf]], [PTbH[hf]])
                    pQ = bank()
                    for j in range(4):
                        p0 = (j % 2) * 64
                        oq = pQ[0:64, j * 128:(j + 1) * 128]
                        MM(oq, WUH[hf][:, j, 0, :], ArbT[hf][:, j, :], True, False, [WUH[hf], ArbT[hf]], [pQ])
                        MM(oq, identb[:, p0:p0 + 64], ARH[hf][:, j // 2, 1, :], False, True, [identb, ARH[hf]], [pQ], last=(j == 3))
                    ACT(QeTH[hf][0:64], v3(pQ[0:64, :], 4), AF.Copy, [pQ], [QeTH[hf]])

                def P1d(hf):
                    pY = bank(True)
                    for j in range(4):
                        oy = pY[:, j * 64:(j + 1) * 64]
                        vj = v_bfH[hf][:, j * 64:(j + 1) * 64]
                        MM(oy, QeTH[hf][:, j, :], Hc[hf][:, j, :], True, False, [QeTH[hf], Hc[hf]], [pY])
                        MM(oy, ArbT[hf][:, j, :], WUH[hf][:, j, 1, :], False, False, [ArbT[hf], WUH[hf]], [pY])
                        MM(oy, ArkT[hf][:, j, :], vj, False, True, [ArkT[hf], v_bfH[hf]], [pY], last=(j == 3))
                    pH = bank()
                    for j in range(4):
                        oh = pH[0:64, j * 64:(j + 1) * 64]
                        vj = v_bfH[hf][:, j * 64:(j + 1) * 64]
                        MM(oh, PTbH[hf][:, j, :], Hc[hf][:, j, :], True, False, [PTbH[hf], Hc[hf]], [pH])
                        MM(oh, BhH[hf][:, j * 64:(j + 1) * 64], WUH[hf][:, j, 1, :], False, False, [BhH[hf], WUH[hf]], [pH])
                        MM(oh, KhH[hf][:, j * 64:(j + 1) * 64], vj, False, True, [KhH[hf], v_bfH[hf]], [pH], last=(j == 3))
                    ACT(Hn[hf][0:64], v4(pH[0:64, 0:256]), AF.Copy, [pH], [Hn[hf]])
                    pYs[hf] = pY

                def F1(hf):
                    cs, sm, pY = cs_(hf), smallH[hf], pYs[hf]
                    y_, q_ = S0H[hf], S2H[hf]
                    bc = lambda r_: sm[:, r_, :].unsqueeze(2).to_broadcast([128, 4, 64])
                    ACT(y_[:], pY[:, 0:256], AF.Copy, [pY], [y_])
                    unpin(pY)
                    k.op("dve", lambda e: e.tensor_reduce(sm[:, 3, :], v4(y_[:]), AX.X, ALU.add), [y_], [sm])
                    TT("pool", q_[:], y_[:], y_[:], ALU.mult, [y_], [q_])
                    k.op("dve", lambda e: e.tensor_reduce(sm[:, 4, :], v4(q_[:]), AX.X, ALU.add), [q_], [sm])
                    TS("dve", sm[:, 3, :], sm[:, 3, :], 1.0 / 64, None, ALU.mult, None, [sm], [sm])
                    TT("dve", sm[:, 5, :], sm[:, 3, :], sm[:, 3, :], ALU.mult, [sm], [sm])
                    STT("dve", sm[:, 4, :], sm[:, 4, :], 1.0 / 64, sm[:, 5, :], ALU.mult, ALU.subtract, [sm], [sm])
                    TS("dve", sm[:, 4, :], sm[:, 4, :], 64e-5, None, ALU.add, None, [sm], [sm])
                    k.op("dve", lambda e: e.reciprocal(sm[:, 4, :], sm[:, 4, :]), [sm], [sm])
                    ACT(sm[:, 4, :], sm[:, 4, :], AF.Sqrt, [sm], [sm])

                def F1b(hf):
                    cs, sm = cs_(hf), smallH[hf]
                    y_ = S0H[hf]
                    bc = lambda r_: sm[:, r_, :].unsqueeze(2).to_broadcast([128, 4, 64])
                    TT("dve", v4(y_[:]), v4(y_[:]), bc(3), ALU.subtract, [y_, sm], [y_])
                    TT("dve", v4(y_[:]), v4(y_[:]), bc(4), ALU.mult, [y_, sm], [y_])
                    TT("pool", y_[:], y_[:], rows[:, 3, cs], ALU.mult, [y_, rows], [y_])
                    TT("pool", y_[:], y_[:], rows[:, 4, cs], ALU.add, [y_, rows], [y_])
                    TT("pool", y_[:], y_[:], bonH[hf][:], ALU.add, [y_, bonH[hf]], [y_])
                    TT("pool", o_bfH[hf][:], y_[:], g_tH[hf][:], ALU.mult, [y_, g_tH[hf]], [o_bfH[hf]])
                    pO = bank()
                    pOb = pO[:].bitcast(BF16)[:, 0:256].rearrange("p (c t) -> p c t", c=2)
                    for i in range(2):
                        TR(pOb[:, i, :], o_bfH[hf][:, i * 128:(i + 1) * 128], identb[:], [o_bfH[hf], identb], [pO], last=(i == 1))
                    ACT(orwT_ap[:, 2 * hf:2 * hf + 2, t0:t0 + 128], pOb, AF.Copy, [pO], [orwTH[hf]])

                both = lambda fn, *a: [fn(hf_, *a) for hf_ in range(2)]
                if not (n > 0 and e1_done):
                    both(E1)
                Eg()
                for fn in (C1, K1, X1, A1, O1, B1, T1, I1):
                    both(fn)
                for lev in range(5):
                    both(NLa, lev)
                    both(NLb, lev)
                for fn in (MGa, MGb, P1a, P1b, P1c, P1d):
                    both(fn)
                if n + 1 < nch_run:
                    nxt = PROJ(n + 1)
                both(F1)
                if n + 1 < nch_run:
                    pR, pK, pV, pW, pA, pG = nxt
                    both(E1)
                    e1_done = True
                else:
                    e1_done = False
                both(F1b)
            if "rw_short" in taps:
                tap("orwT", orwT_ap[:, :, 0:256], [128, 4, 256], orwTH)
            else:
                tap("orwT", orwT_ap, [128, 4, T], orwTH)
            k.barrier()
        if stop_after <= 2:
            k.finish()
            return nc, tap_d

        with ExitStack() as st:
            WG = [sb(st, f"WG{c}", [128, GL], BF16) for c in range(8)]
            for c in range(8):
                k.dma("pool", WG[c][:], win_v[:, c, RW:RW + GL], writes=[WG[c]])
            ga2 = sb(st, "ga2", [128, 256], F32)
            ngr = sb(st, "ngr", [128, 4, 128], F32)
            Gtri = sb(st, "Gtri", [128, 3, 128], F32)
            c16 = sb(st, "c16", [128, 1], F32)
            mI4 = sb(st, "mI4", [128, 4, 128], F32)
            k.op("dve", lambda e: e.memset(ga2[:], 0.0), writes=[ga2])
            k.dma("sp", ga2[0:16, :], ga2_d, writes=[ga2])
            gabr = sb(st, "gabr", [128, 256], F32)
            k.dma("sp", gabr[:], _bc_rows(gab_d, 128, 256), writes=[gabr])
            k.dma("sp", ngr[:], bass.AP(gng2_d.tensor, gng2_d.offset, [[0, 128], [0, 4], [1, 128]]), writes=[ngr])
            k.dma("sp", Gtri[:], ctri_d[:, 0:3, :], writes=[Gtri])
            TS("dve", Gtri[:], Gtri[:], -1.0 / 16, None, ALU.mult, None, [Gtri], [Gtri])
            k.op("dve", lambda e: e.memset(c16[:], -1.0 / 16), writes=[c16])
            for h4 in range(4):
                k.dma("sp", mI4[:, h4, :], ctri_d[:, 1, :], writes=[mI4])
            Sst = sb(st, "Sst", [128, 4, 128], F32)
            Sbf = sb(st, "Sbf", [128, 4, 128], BF16)
            k.op("dve", lambda e: e.memset(Sst[:], 0.0), writes=[Sst])
            k.op("dve", lambda e: e.memset(Sbf[:], 0.0), writes=[Sbf])

            def v3(ap, a):
                return ap.rearrange("p (a b) -> p a b", a=a)

            def gset(i):
                B = {}
                B["adT"] = sb(st, f"g_adT{i}", [128, 128], F32)
                k.op("dve", lambda e: e.memset(B["adT"][:], 0.0), writes=[B["adT"]])

                B["qkT"] = sb(st, f"g_qkT{i}", [128, 4, 128], F32)
                B["gk"] = sb(st, f"g_gk{i}", [128, 256], F32)
                B["ez"] = sb(st, f"g_ez{i}", [128, 256], F32)
                B["lz"] = sb(st, f"g_lz{i}", [128, 256], F32)
                B["Erev"] = sb(st, f"g_Erev{i}", [128, 256], F32)
                B["Ein"] = sb(st, f"g_Ein{i}", [128, 2, 128], F32)
                B["Enin"] = sb(st, f"g_Enin{i}", [128, 2, 128], F32)
                B["decs"] = sb(st, f"g_decs{i}", [128, 2], F32)
                B["kdec"] = sb(st, f"g_kdec{i}", [128, 256], BF16)
                B["gv"] = sb(st, f"g_v{i}", [128, 512], BF16)
                B["sgg"] = sb(st, f"g_sgg{i}", [128, 512], F32)
                B["QsT"] = sb(st, f"g_QsT{i}", [128, 2, 128], BF16)
                B["KsTz"] = sb(st, f"g_KsTz{i}", [128, 2, 2, 128], BF16)
                k.op("dve", lambda e: e.memset(B["KsTz"][:], 0.0), writes=[B["KsTz"]])
                B["attT"] = sb(st, f"g_attT{i}", [128, 4, 128], BF16)
                B["osb"] = sb(st, f"g_osb{i}", [128, 512], F32)
                B["osq"] = sb(st, f"g_osq{i}", [128, 512], F32)
                B["gsm"] = sb(st, f"g_sm{i}", [128, 2, 4], F32)
                B["of"] = sb(st, f"g_of{i}", [128, 512], BF16)
                return B
            GS = [gset(0), gset(1)]

            def G1(n, B):
                ucur = [uT_ap[:, c, n * 128 + 1:n * 128 + 129] for c in range(8)]
                pC, pD = bank(), bank()
                pCv = v3(pC[:], 4)
                for i, c0 in enumerate((0, 128, 256, 384)):
                    for c in range(8):
                        MM(pCv[:, i, :], WG[c][:, c0:c0 + 128], ucur[c], c == 0, c == 7, [WG[c], uTb[c]], [pC])
                for c in range(8):
                    MM(pD[0:16, 0:128], WG[c][:, 1536:1552], ucur[c], c == 0, c == 7, [WG[c], uTb[c]], [pD])
                CP("dve", B["adT"][0:16, :], pD[0:16, 0:128], [pD], [B["adT"]])
                ACT(B["qkT"][:], pCv, AF.Copy, [pC], [B["qkT"]])

            def G2(n, B):
                ucur = [uT_ap[:, c, n * 128 + 1:n * 128 + 129] for c in range(8)]
                pK, pV, pGg = bank(), bank(), bank()
                for c in range(8):
                    MM(pK[:, 0:256], ucur[c], WG[c][:, 256:512], c == 0, c == 7, [WG[c], uTb[c]], [pK])
                for c in range(8):
                    MM(pV[:], ucur[c], WG[c][:, 512:1024], c == 0, c == 7, [WG[c], uTb[c]], [pV])
                for c in range(8):
                    MM(pGg[:], ucur[c], WG[c][:, 1024:1536], c == 0, c == 7, [WG[c], uTb[c]], [pGg])
                CP("dve", B["gk"][:], pK[:, 0:256], [pK], [B["gk"]])
                ACT(B["gv"][:], pV[:], AF.Copy, [pV], [B["gv"]])
                ACT(B["sgg"][:], pGg[:], AF.Silu, [pGg], [B["sgg"]])

            def G3(n, B):
                pZ = bank()
                MM(pZ[:, 0:256], B["adT"][:], ga2[:], True, True, [B["adT"], ga2], [pZ])
                TT("dve", B["ez"][:], pZ[:, 0:256], gabr[:], ALU.add, [pZ, gabr], [B["ez"]])
                ACT(B["ez"][:], B["ez"][:], AF.Exp, [B["ez"]], [B["ez"]], scale=-1.0)
                ACT(B["lz"][:], B["ez"][:], AF.Ln, [B["ez"]], [B["lz"]], bias=1.0)

            def G4(n, B):
                lz = B["lz"]
                pB, pDc = bank(), bank()
                MM(pB[:, 0:256], Gtri[:, 2, :], lz[:], True, True, [Gtri, lz], [pB], last=False)
                for c2 in range(2):
                    MM(pB[:, 256 + c2 * 128:384 + c2 * 128], lz[:, c2 * 128:(c2 + 1) * 128], Gtri[:, 1, :], True, True, [Gtri, lz], [pB], last=(c2 == 1))
                for c2 in range(2):
                    MM(pDc[:, c2:c2 + 1], lz[:, c2 * 128:(c2 + 1) * 128], c16[:], True, True, [lz, c16], [pDc], last=(c2 == 1))
                ACT(B["Erev"][:], pB[:, 0:256], AF.Exp, [pB], [B["Erev"]])
                ACT(B["Ein"][:], v3(pB[:, 256:512], 2), AF.Exp, [pB], [B["Ein"]])
                ACT(B["Enin"][:], v3(pB[:, 256:512], 2), AF.Exp, [pB], [B["Enin"]], scale=-1.0)
                ACT(B["decs"][:], pDc[:, 0:2], AF.Exp, [pDc], [B["decs"]])
                TT("dve", B["kdec"][:], B["gk"][:], B["Erev"][:], ALU.mult, [B["gk"], B["Erev"]], [B["kdec"]])
                STT("dve", B["QsT"][:], B["qkT"][:, 0:2, :], 0.125, B["Ein"][:], ALU.mult, ALU.mult, [B["qkT"], B["Ein"]], [B["QsT"]])
                for hh in range(2):
                    ps_ = slice(hh * 64, hh * 64 + 64)
                    TT("pool", B["KsTz"][ps_, :, hh, :], B["qkT"][ps_, 2:4, :], B["Enin"][ps_], ALU.mult, [B["qkT"], B["Enin"]], [B["KsTz"]])

            def G5(n, B):
                pA = bank()
                for h in range(4):
                    MM(v3(pA[:], 4)[:, h, :], B["KsTz"][:, h // 2, h % 2, :], B["QsT"][:, h // 2, :], True, True, [B["KsTz"], B["QsT"]], [pA], last=(h == 3))
                TT("dve", B["attT"][:], v3(pA[:], 4), mI4[:], ALU.mult, [pA, mI4], [B["attT"]])

            def G6(n, B):
                pOo = bank(True)
                for h in range(4):
                    oo = v3(pOo[:], 4)[:, h, :]
                    MM(oo, B["attT"][:, h, :], B["gv"][:, h * 128:(h + 1) * 128], True, False, [B["attT"], B["gv"]], [pOo])
                    MM(oo, B["QsT"][:, h // 2, :], Sbf[:, h, :], False, True, [B["QsT"], Sbf], [pOo], last=(h == 3))
                pKV = bank()
                for h in range(4):
                    c2 = h // 2
                    MM(v3(pKV[:], 4)[:, h, :], B["kdec"][:, c2 * 128:(c2 + 1) * 128], B["gv"][:, h * 128:(h + 1) * 128], True, True, [B["kdec"], B["gv"]], [pKV], last=(h == 3))
                for h in range(4):
                    c2, p0 = h // 2, (h % 2) * 64
                    STT("dve", Sst[p0:p0 + 64, h, :], Sst[p0:p0 + 64, h, :], B["decs"][p0:p0 + 64, c2:c2 + 1],
                        v3(pKV[:], 4)[p0:p0 + 64, h, :], ALU.mult, ALU.add, [Sst, B["decs"], pKV], [Sst])
                CP("pool", Sbf[:], Sst[:], [Sst], [Sbf])
                B["pOo"] = pOo

            def G7(n, B):
                t0 = n * 128
                pOo, osb, osq, gsm = B["pOo"], B["osb"], B["osq"], B["gsm"]
                ACT(osb[:], pOo[:], AF.Copy, [pOo], [osb])
                unpin(pOo)
                TT("pool", osq[:], osb[:], osb[:], ALU.mult, [osb], [osq])
                k.op("dve", lambda e: e.tensor_reduce(gsm[:, 0, :], v3(osq[:], 4), AX.X, ALU.add), [osq], [gsm])
                TS("dve", gsm[:, 1, :], gsm[:, 0, :], 1.0 / 128, 1e-5, ALU.mult, ALU.add, [gsm], [gsm])
                k.op("dve", lambda e: e.reciprocal(gsm[:, 1, :], gsm[:, 1, :]), [gsm], [gsm])
                ACT(gsm[:, 1, :], gsm[:, 1, :], AF.Sqrt, [gsm], [gsm])
                TT("dve", v3(osb[:], 4), v3(osb[:], 4), gsm[:, 1, :].unsqueeze(2).to_broadcast([128, 4, 128]), ALU.mult, [osb, gsm], [osb])
                TT("pool", v3(osb[:], 4), v3(osb[:], 4), ngr[:], ALU.mult, [osb, ngr], [osb])
                TT("pool", B["of"][:], osb[:], B["sgg"][:], ALU.mult, [osb, B["sgg"]], [B["of"]])
                pO = bank()
                pOb = pO[:].bitcast(BF16)[:, 0:512].rearrange("p (c t) -> p c t", c=4)
                for c4 in range(4):
                    TR(pOb[:, c4, :], B["of"][:, c4 * 128:(c4 + 1) * 128], identb[:], [B["of"], identb], [pO], last=(c4 == 3))
                ACT(oglaT_ap[:, :, t0:t0 + 128], pOb, AF.Copy, [pO], [oglaTb])

            nch_run = NCH if "gla_short" not in taps else 2
            for n in range(0, nch_run, 2):
                for fn in (G1, G2, G3, G4, G5, G6, G7):
                    fn(n, GS[0])
                    fn(n + 1, GS[1])
            if "gla_short" in taps:
                tap("oglaT", oglaT_ap[:, :, 0:256], [128, 4, 256], [oglaTb])
            else:
                tap("oglaT", oglaT_ap, [128, 4, T], [oglaTb])
            k.barrier()
        if stop_after <= 3:
            k.finish()
            return nc, tap_d

        with ExitStack() as st3:
            mT = sb(st3, "mT", [128, 8, T], BF16)
            mTb = [Buf(f"mT{d}", None) for d in range(8)]
            with ExitStack() as st:
                Wgt = [sb(st, f"Wgt{c}", [128, 2048], BF16) for c in range(8)]
                Wbr = sb(st, "Wbr", [128, 4, D], BF16)
                Wbg = sb(st, "Wbg", [128, 4, D], BF16)
                for c in range(8):
                    k.dma("pool", Wgt[c][:], win_v[:, c, RW + GL:NIN], writes=[Wgt[c]])
                k.dma("pool", Wbr[:], wbr_d.rearrange("(c p) n -> p c n", p=128), writes=[Wbr])
                k.dma("pool", Wbg[:], wbg_d.rearrange("(c p) n -> p c n", p=128), writes=[Wbg])
                sr = [sb(st, f"m_sr{i}", [128, 512], F32) for i in range(2)]
                sg_ = [sb(st, f"m_sg{i}", [128, 512], F32) for i in range(2)]
                it = 0
                for sbk in range(4):
                    tok = slice(sbk * 512, (sbk + 1) * 512)
                    for dt in range(8):
                        dsl = slice(dt * 128, (dt + 1) * 128)
                        p1, p2, p3, p4 = bank(), bank(), bank(), bank()
                        for c in range(8):
                            MM(p1[:], Wgt[c][:, dsl], uT_ap[:, c, sbk * 512 + 1:sbk * 512 + 513], c == 0, c == 7, [Wgt[c], uTb[c]], [p1])
                        for c in range(8):
                            MM(p2[:], Wgt[c][:, 1024 + dt * 128:1024 + (dt + 1) * 128], uT_ap[:, c, sbk * 512 + 1:sbk * 512 + 513],
                               c == 0, c == 7, [Wgt[c], uTb[c]], [p2])
                        for c in range(4):
                            MM(p3[:], Wbr[:, c, dsl], orwT_ap[:, c, tok], c == 0, c == 3, [Wbr] + orwTH, [p3])
                        for c in range(4):
                            MM(p4[:], Wbg[:, c, dsl], oglaT_ap[:, c, tok], c == 0, c == 3, [Wbg, oglaTb], [p4])
                        a_, b_ = sr[it % 2], sg_[it % 2]
                        it += 1
                        ACT(a_[:], p1[:], AF.Sigmoid, [p1], [a_])
                        ACT(b_[:], p2[:], AF.Sigmoid, [p2], [b_])
                        TT("dve", a_[:], a_[:], p3[:], ALU.mult, [a_, p3], [a_])
                        TT("dve", b_[:], b_[:], p4[:], ALU.mult, [b_, p4], [b_])
                        TT("pool", mT[:, dt, tok], a_[:], b_[:], ALU.add, [a_, b_], [mTb[dt]])
                tap("mT", mT[:], [128, 8, T], mTb)
                k.barrier()
            if stop_after <= 4:
                k.finish()
                return nc, tap_d

            with ExitStack() as st:
                Wmx = sb(st, "Wmx", [128, 8, D], BF16)
                k.dma("pool", Wmx[:], wmix_d.rearrange("(c p) n -> p c n", p=128), writes=[Wmx])
                g1row = sb(st, "g1row", [128, D], F32)
                lnr = sb(st, "ln1r", [128, 2, D], F32)
                k.dma("sp", lnr[:, 0, :], _bc_rows(ln1g_d, 128, D), writes=[lnr])
                k.dma("sp", lnr[:, 1, :], _bc_rows(ln1b_d, 128, D), writes=[lnr])
                screp = sb(st, "screp", [128, 8, 128], BF16)
                CP("dve", screp[:], scb[:].unsqueeze(2).to_broadcast([128, 8, 128]), [scb], [screp])
                brow1 = sb(st, "brow1", [1, D], F32)
                k.dma("sp", brow1[:], brow_d[:, 2 * D:3 * D], writes=[brow1])
                with ExitStack() as stw:
                    wg1 = sb(stw, "wg1", [128, 8, D], BF16)
                    k.dma("pool", wg1[:], wada_d.rearrange("(c p) n -> p c n", p=128)[:, :, 2 * D:3 * D], writes=[wg1])
                    for nh in range(2):
                        pg_ = bank()
                        for c in range(8):
                            MM(pg_[:], screp[:, c, :], wg1[:, c, nh * 512:(nh + 1) * 512], c == 0, False, [screp, wg1], [pg_])
                        MM(pg_[:], ones1[:], brow1[:, nh * 512:(nh + 1) * 512], False, True, [ones1, brow1], [pg_])
                        ACT(g1row[:, nh * 512:(nh + 1) * 512], pg_[:], AF.Copy, [pg_], [g1row])
                    k.barrier()
                xin = [sb(st, f"xin{i}", [128, D], F32) for i in range(2)]
                ybuf = [sb(st, f"ybuf{i}", [128, D], F32) for i in range(2)]
                stat = [sb(st, f"l1stat{i}", [128, 2, 6], F32) for i in range(2)]
                mv = [sb(st, f"l1mv{i}", [128, 4], F32) for i in range(2)]
                pms = {}

                def M0(tb):
                    xi = xin[tb % 2]
                    k.dma("sp", xi[:], x_d[tb * 128:(tb + 1) * 128, :], writes=[xi])
                    pm_ = [bank(True), bank(True)]
                    for nh in range(2):
                        for dt in range(8):
                            MM(pm_[nh][:], mT[:, dt, tb * 128:(tb + 1) * 128], Wmx[:, dt, nh * 512:(nh + 1) * 512], dt == 0, dt == 7, [mTb[dt], Wmx], [pm_[nh]])
                    pms[tb] = pm_

                def M1(tb):
                    yb, pm_ = ybuf[tb % 2], pms[tb]
                    for nh in range(2):
                        hs = slice(nh * 512, (nh + 1) * 512)
                        TT("dve", yb[:, hs], pm_[nh][:], g1row[:, hs], ALU.mult, [pm_[nh], g1row], [yb])
                    unpin(*pm_)
                    STT("dve", yb[:], xin[tb % 2][:], ALPHA, yb[:], ALU.mult, ALU.add, [xin[tb % 2], yb], [yb])

                def M2(tb):
                    yb, st_, mv_ = ybuf[tb % 2], stat[tb % 2], mv[tb % 2]
                    for nh in range(2):
                        k.op("dve", lambda e, nh=nh: e.bn_stats(st_[:, nh, :], yb[:, nh * 512:(nh + 1) * 512]), [yb], [st_])
                    k.op("dve", lambda e: e.bn_aggr(mv_[:, 0:2], st_[:]), [st_], [mv_])
                    TS("dve", mv_[:, 2:3], mv_[:, 1:2], 1e-5, None, ALU.add, None, [mv_], [mv_])
                    k.op("dve", lambda e: e.reciprocal(mv_[:, 2:3], mv_[:, 2:3]), [mv_], [mv_])
                    ACT(mv_[:, 2:3], mv_[:, 2:3], AF.Sqrt, [mv_], [mv_])

                def M3(tb):
                    yb, mv_ = ybuf[tb % 2], mv[tb % 2]
                    TS("dve", yb[:], yb[:], mv_[:, 0:1], mv_[:, 2:3], ALU.subtract, ALU.mult, [yb, mv_], [yb])
                    TT("pool", yb[:], yb[:], lnr[:, 0, :], ALU.mult, [yb, lnr], [yb])
                    TT("pool", x1_ap[:, tb, :], yb[:], lnr[:, 1, :], ALU.add, [yb, lnr], [x1b[tb]])

                for tb0 in range(0, 16, 2):
                    for fn in (M0, M1, M2, M3):
                        fn(tb0)
                        fn(tb0 + 1)
                tap("x1", x1_ap, [128, 16, D], x1b)
                k.barrier()
        if stop_after <= 5:
            k.finish()
            return nc, tap_d

        with ExitStack() as st:
            Wo = sb(st, "Wo", [128, 22, D], BF16)
            Wob = [Buf(f"Wo{f}", None) for f in range(22)]
            wfo_v = wfo_d.rearrange("(f p) n -> p f n", p=128)
            lnr = sb(st, "ln2r", [128, 2, D], F32)
            k.dma("sp", lnr[:, 0, :], _bc_rows(ln2g_d, 128, D), writes=[lnr])
            k.dma("sp", lnr[:, 1, :], _bc_rows(ln2b_d, 128, D), writes=[lnr])
            g2row = sb(st, "g2row", [128, D], BF16)
            screp = sb(st, "screp2", [128, 8, 128], BF16)
            CP("dve", screp[:], scb[:].unsqueeze(2).to_broadcast([128, 8, 128]), [scb], [screp])
            u2T = sb(st, "u2T", [128, 8, 1024], BF16)
            actT = sb(st, "actT", [128, 22, 1024], BF16)
            actb = [Buf(f"act{f}", None) for f in range(22)]
            Wgu = [sb(st, f"Wgu{i}", [128, 8, 2, 128], BF16) for i in range(3)]
            sgf = [sb(st, f"f_sg{i}", [128, 512], F32) for i in range(2)]
            ybuf = [sb(st, f"f_y{i}", [128, D], F32) for i in range(2)]
            stat = sb(st, "l2stat", [128, 2, 6], F32)
            mv = sb(st, "l2mv", [128, 4], F32)
            brow2 = ybuf[0]
            k.dma("sp", brow2[0:1, :], brow_d[:, 5 * D:6 * D], writes=[brow2])
            wg2 = actT[:, 14:22, :]
            wg2b = actb[14:22]

            def side_work(f):
                if f == 0:
                    k.dma("pool", wg2, wada_d.rearrange("(c p) n -> p c n", p=128)[:, :, 5 * D:6 * D], writes=wg2b)
                if f == 3:
                    for nh in range(2):
                        pg_ = bank()
                        for c in range(8):
                            MM(pg_[:], screp[:, c, :], wg2[:, c, nh * 512:(nh + 1) * 512], c == 0, False, [screp] + wg2b, [pg_])
                        MM(pg_[:], ones1[:], brow2[0:1, nh * 512:(nh + 1) * 512], False, True, [ones1, brow2], [pg_])
                        ACT(g2row[:, nh * 512:(nh + 1) * 512], pg_[:], AF.Copy, [pg_], [g2row])
                if 2 <= f < 13:
                    f0 = 2 * (f - 2)
                    k.dma("pool", Wo[:, f0:f0 + 2, :], wfo_v[:, f0:f0 + 2, :], writes=Wob[f0:f0 + 2])
                if 6 <= f < 17:
                    f0 = 2 * (f - 6)
                    TT("dve", Wo[:, f0, :], Wo[:, f0, :], g2row[:], ALU.mult, [Wob[f0], g2row], [Wob[f0]])
                    TT("pool", Wo[:, f0 + 1, :], Wo[:, f0 + 1, :], g2row[:], ALU.mult, [Wob[f0 + 1], g2row], [Wob[f0 + 1]])
            wfi_v = wfi_d.rearrange("(c p) (g n) -> p c g n", p=128, g=2)
            wi = 0
            it = 0
            for sbk in range(2):
                for c in range(8):
                    for jb in range(2):
                        pt_ = bank()
                        for j in range(4):
                            tb = sbk * 8 + jb * 4 + j
                            TR(pt_[:, j * 128:(j + 1) * 128], x1_ap[:, tb, c * 128:(c + 1) * 128], identf[:], [x1b[tb], identf], [pt_], last=(j == 3))
                        ACT(u2T[:, c, jb * 512:(jb + 1) * 512], pt_[:], AF.Identity, [pt_, modp], [u2T],
                            bias=modp[:, 16 + c:17 + c], scale=modp[:, 24 + c:25 + c])
                for f in range(22):
                    w_ = Wgu[wi % 3]
                    wi += 1
                    k.dma("pool", w_[:, :, 0, :], wfi_v[:, :, 0, f * 128:(f + 1) * 128], writes=[w_])
                    k.dma("pool", w_[:, :, 1, :], wfi_v[:, :, 1, f * 128:(f + 1) * 128], writes=[w_])
                    if sbk == 0:
                        side_work(f)
                    for hh in range(2):
                        ph, pu = bank(), bank()
                        for c in range(8):
                            MM(ph[:], w_[:, c, 0, :], u2T[:, c, hh * 512:(hh + 1) * 512], c == 0, c == 7, [w_, u2T], [ph])
                        for c in range(8):
                            MM(pu[:], w_[:, c, 1, :], u2T[:, c, hh * 512:(hh + 1) * 512], c == 0, c == 7, [w_, u2T], [pu])
                        s_ = sgf[it % 2]
                        it += 1
                        ACT(s_[:], ph[:], AF.Silu, [ph], [s_])
                        TT("dve", actT[:, f, hh * 512:(hh + 1) * 512], s_[:], pu[:], ALU.mult, [s_, pu], [actb[f]])
                for j in range(8):
                    tb = sbk * 8 + j
                    yb = ybuf[tb % 2]
                    po = [bank(), bank()]
                    for nh in range(2):
                        for f in range(22):
                            MM(po[nh][:], actT[:, f, j * 128:(j + 1) * 128], Wo[:, f, nh * 512:(nh + 1) * 512], f == 0, f == 21, [actb[f], Wob[f]], [po[nh]])
                    for nh in range(2):
                        hs = slice(nh * 512, (nh + 1) * 512)
                        STT("dve", yb[:, hs], x1_ap[:, tb, hs], ALPHA, po[nh][:], ALU.mult, ALU.add, [x1b[tb], po[nh]], [yb])
                    for nh in range(2):
                        k.op("dve", lambda e, nh=nh, yb=yb: e.bn_stats(stat[:, nh, :], yb[:, nh * 512:(nh + 1) * 512]), [yb], [stat])
                    k.op("dve", lambda e: e.bn_aggr(mv[:, 0:2], stat[:]), [stat], [mv])
                    TS("dve", mv[:, 2:3], mv[:, 1:2], 1e-5, None, ALU.add, None, [mv], [mv])
                    k.op("dve", lambda e: e.reciprocal(mv[:, 2:3], mv[:, 2:3]), [mv], [mv])
                    ACT(mv[:, 2:3], mv[:, 2:3], AF.Sqrt, [mv], [mv])
                    TS("dve", yb[:], yb[:], mv[:, 0:1], mv[:, 2:3], ALU.subtract, ALU.mult, [yb, mv], [yb])
                    TT("pool", yb[:], yb[:], lnr[:, 0, :], ALU.mult, [yb, lnr], [yb])
                    TT("pool", yb[:], yb[:], lnr[:, 1, :], ALU.add, [yb, lnr], [yb])
                    k.dma("sp", y_d[tb * 128:(tb + 1) * 128, :], yb[:], reads=[yb], is_output=True)
        k.finish()
    return nc, tap_d


def _host_inputs(inputs):
    f = lambda a: np.ascontiguousarray(a, dtype=np.float32)
    sh = {}
    b = inputs["b_ada"][0]
    sh["w_ada"] = f(inputs["w_ada"][0])
    sh["b_pp"] = f(b.reshape(6, 8, 128)[[0, 1, 3, 4]].transpose(2, 0, 1).reshape(128, 32))
    sh["b_row"] = f(b.reshape(1, -1))
    sh["w_in"] = f(inputs["w_in"][0])
    sh["mu"] = f(inputs["mu_rw"][0].reshape(1, -1))
    for nm in ("rw_w0", "rw_a0", "rw_k_k", "rw_k_a", "rw_r_k", "rw_gn_g", "rw_gn_b", "gla_a_b", "gla_norm_g",
               "ln1_g", "ln1_b", "ln2_g", "ln2_b"):
        sh[nm] = f(inputs[nm][0].reshape(1, -1))
    for nm in ("rw_w2", "rw_a2", "rw_g2", "gla_a2", "w_rw_branch", "w_gla_branch", "w_mix_out", "w_ffn_in", "w_ffn_out"):
        sh[nm] = f(inputs[nm][0])
    sh["c_ident"] = np.eye(128, dtype=np.float32)
    s = np.arange(128)[:, None]
    t = np.arange(128)[None, :]
    bd = (s // 64) == (t // 64)
    sh["c_tri"] = f(np.stack([(s < t), (s <= t), (s > t), (s < t) & bd, (s > t) & bd, (s >= 64) & (t < 64)], axis=1).astype(np.float32))
    maps = []
    x = inputs["x"]
    c = inputs["c"]
    for bi in range(x.shape[0]):
        m = dict(sh)
        m["xT"] = f(x[bi].T)
        m["x"] = f(x[bi])
        m["cpp"] = f(c[bi].reshape(8, 128).T)
        maps.append(m)
    return maps


def kernel(**inputs):
    maps = _host_inputs(inputs)
    nc, _ = build_nc()
    res = run_bass_kernel_spmd(nc, maps, core_ids=list(range(len(maps))))
    return np.stack([np.asarray(r["y"], dtype=np.float32) for r in res.results], axis=0)
```

```python
import math
from contextlib import ExitStack

import numpy as np
import concourse.bass as bass
import concourse.mybir as mybir
from concourse.bass_utils import run_bass_kernel_spmd

F32 = mybir.dt.float32
BF16 = mybir.dt.bfloat16
AF = mybir.ActivationFunctionType
ALU = mybir.AluOpType
AX = mybir.AxisListType

D = 1024
T = 2048
NCH = T // 128
RW = 1792
GL = 1552
NIN = 5392
DFF = 2816
ALPHA = 2.0 ** 0.25
EM05 = math.exp(-0.5)


class Buf:
    __slots__ = ("name", "ap", "writer", "readers")

    def __init__(self, name, ap):
        self.name = name
        self.ap = ap
        self.writer = None
        self.readers = {}

    def __getitem__(self, key):
        return self.ap[key]


class KB:
    def __init__(self, nc, stack, n_dma_sems=8):
        self.nc = nc
        self.engs = {"pe": nc.tensor, "act": nc.scalar, "dve": nc.vector, "pool": nc.gpsimd, "sp": nc.sync}
        self.sem, self.cnt, self.waited = {}, {}, {}
        for e in self.engs:
            self.sem[e] = stack.enter_context(nc.semaphore("s_" + e))
            self.cnt[e] = 0
            self.waited[e] = {}
        self.dma_sems, self.dma_val, self.dma_rr = {}, {}, {}
        for q in ("sp", "act", "pool"):
            self.dma_sems[q] = [stack.enter_context(nc.semaphore(f"d_{q}{i}")) for i in range(n_dma_sems)]
            self.dma_val[q] = [0] * n_dma_sems
            self.dma_rr[q] = 0
        self.out_events = []
        self.pending = {}

    def _wait(self, eng, ev):
        sem, val, _ = ev
        if self.waited[eng].get(sem.name, 0) >= val:
            return
        self.engs[eng].wait_ge(sem, val)
        self.waited[eng][sem.name] = val

    def _collect(self, eng, reads, writes):
        evs = {}

        def add(ev, kind):
            if ev is None:
                return
            sem, val, src = ev
            if src == eng and (eng == "pe" or (kind == "war" and eng != "pool")):
                return
            if sem.name not in evs or evs[sem.name][1] < val:
                evs[sem.name] = ev
        for b in reads:
            add(b.writer, "raw")
        for b in writes:
            add(b.writer, "waw")
            for ev in b.readers.values():
                add(ev, "war")
        return evs

    def _record(self, ev, reads, writes):
        for b in reads:
            b.readers[ev[0].name] = ev
        for b in writes:
            b.writer = ev
            b.readers = {}

    def op(self, eng, fn, reads=(), writes=(), inc=True):
        for ev in self._collect(eng, reads, writes).values():
            self._wait(eng, ev)
        ins = fn(self.engs[eng])
        pend = self.pending.setdefault(eng, [])
        if not inc:
            pend.append((tuple(reads), tuple(writes)))
            return
        self.cnt[eng] += 1
        ins.then_inc(self.sem[eng], 1)
        ev = (self.sem[eng], self.cnt[eng], eng)
        for r_, w_ in pend:
            self._record(ev, r_, w_)
        pend.clear()
        self._record(ev, reads, writes)

    def dma(self, q, out, in_, reads=(), writes=(), is_output=False):
        for ev in self._collect(q, reads, writes).values():
            self._wait(q, ev)
        i = self.dma_rr[q]
        self.dma_rr[q] = (i + 1) % len(self.dma_sems[q])
        sem = self.dma_sems[q][i]
        prev = self.dma_val[q][i]
        if prev > 0:
            self._wait(q, (sem, prev, "dma"))
        ins = self.engs[q].dma_start(out=out, in_=in_)
        ins.then_inc(sem, 16)
        self.dma_val[q][i] = prev + 16
        ev = (sem, prev + 16, "dma")
        self._record(ev, reads, writes)
        if is_output:
            self.out_events.append(ev)

    def barrier(self):
        assert not any(self.pending.values()), "pending non-incrementing ops at barrier"
        evs = [(self.sem[e], self.cnt[e], e) for e in self.engs if self.cnt[e] > 0]
        for q in self.dma_sems:
            for s, v in zip(self.dma_sems[q], self.dma_val[q]):
                if v > 0:
                    evs.append((s, v, "dma"))
        for e in self.engs:
            for ev in evs:
                if ev[2] != e or e != "pe":
                    self._wait(e, ev)

    def finish(self):
        for ev in self.out_events:
            self._wait("sp", ev)
        self.final_counts = dict(self.cnt)
        KB.last = self


def _bc_rows(ap, nparts, n):
    return bass.AP(ap.tensor, ap.offset, [[0, nparts], [1, n]])


def build_nc(stop_after=99, taps=()):
    nc = bass.Bass("TRN2", target_bir_lowering=False)
    din = {}

    def inp(name, shape):
        din[name] = nc.dram_tensor(name, list(shape), F32, kind="ExternalInput").ap()
        return din[name]

    xT_d = inp("xT", [D, T])
    x_d = inp("x", [T, D])
    cpp_d = inp("cpp", [128, 8])
    wada_d = inp("w_ada", [D, 6 * D])
    bpp_d = inp("b_pp", [128, 32])
    brow_d = inp("b_row", [1, 6 * D])
    win_d = inp("w_in", [D, NIN])
    mu_d = inp("mu", [1, RW])
    w0_d = inp("rw_w0", [1, 512])
    a0_d = inp("rw_a0", [1, 512])
    w2_d = inp("rw_w2", [64, 512])
    a2_d = inp("rw_a2", [64, 512])
    g2_d = inp("rw_g2", [128, 512])
    kk_d = inp("rw_k_k", [1, 512])
    ka_d = inp("rw_k_a", [1, 512])
    rk_d = inp("rw_r_k", [1, 512])
    gng_d = inp("rw_gn_g", [1, 512])
    gnb_d = inp("rw_gn_b", [1, 512])
    ga2_d = inp("gla_a2", [16, 256])
    gab_d = inp("gla_a_b", [1, 256])
    gng2_d = inp("gla_norm_g", [1, 128])
    wbr_d = inp("w_rw_branch", [512, D])
    wbg_d = inp("w_gla_branch", [512, D])
    wmix_d = inp("w_mix_out", [D, D])
    ln1g_d = inp("ln1_g", [1, D])
    ln1b_d = inp("ln1_b", [1, D])
    wfi_d = inp("w_ffn_in", [D, 2 * DFF])
    wfo_d = inp("w_ffn_out", [DFF, D])
    ln2g_d = inp("ln2_g", [1, D])
    ln2b_d = inp("ln2_b", [1, D])
    cident_d = inp("c_ident", [128, 128])
    ctri_d = inp("c_tri", [128, 6, 128])
    y_d = nc.dram_tensor("y", [T, D], F32, kind="ExternalOutput").ap()
    tap_d = {}

    with ExitStack() as st0:
        k = KB(nc, st0)

        def sb(stack, name, shape, dt):
            t = stack.enter_context(nc.sbuf_tensor("sb_" + name, list(shape), dt))
            return Buf(name, t[:])

        def MM(out, lhsT, rhs, st, sp, R, W, last=None):
            k.op("pe", lambda e: e.matmul(out, lhsT, rhs, start=st, stop=sp), R, W, inc=(sp if last is None else last))

        def TR(out, in_, idn, R, W, last=True):
            k.op("pe", lambda e: e.transpose(out, in_, idn), R, W, inc=last)

        def ACT(out, in_, fn, R, W, bias=None, scale=None):
            kw = {}
            if bias is not None:
                kw["bias"] = bias
            if scale is not None:
                kw["scale"] = scale
            k.op("act", lambda e: e.activation(out, in_, fn, **kw), R, W)

        def TT(eng, out, a, b, op, R, W):
            if "nopool" in taps and eng == "pool":
                eng = "dve"
            k.op(eng, lambda e: e.tensor_tensor(out, a, b, op), R, W)

        def STT(eng, out, a, s, b, op0, op1, R, W):
            k.op(eng, lambda e: e.scalar_tensor_tensor(out, a, s, b, op0, op1), R, W)

        def TS(eng, out, a, s1, s2, op0, op1, R, W):
            if op1 is None:
                k.op(eng, lambda e: e.tensor_scalar(out, a, s1, None, op0), R, W)
            else:
                k.op(eng, lambda e: e.tensor_scalar(out, a, s1, s2, op0, op1), R, W)

        def CP(eng, out, in_, R, W):
            if "nopool" in taps and eng == "pool":
                eng = "dve"
            if eng == "act":
                ACT(out, in_, AF.Copy, R, W)
            else:
                k.op(eng, lambda e: e.tensor_copy(out, in_), R, W)

        def tap(name, ap, shape, reads):
            if name not in taps:
                return
            tap_d[name] = nc.dram_tensor("tap_" + name, list(shape), F32, kind="ExternalOutput").ap()
            k.dma("pool", tap_d[name], ap, reads=reads, is_output=True)

        banks = []
        for i in range(8):
            t = st0.enter_context(nc.psum_tensor(f"pb{i}", [128, 512], F32))
            banks.append(Buf(f"pb{i}", t[:]))
        bank_rr = [0]

        pinned = set()

        def bank(pin=False):
            for _ in range(8):
                b = banks[bank_rr[0]]
                bank_rr[0] = (bank_rr[0] + 1) % 8
                if b.name not in pinned:
                    if pin:
                        pinned.add(b.name)
                    return b
            raise RuntimeError("all PSUM banks pinned")

        def unpin(*bs):
            for b in bs:
                pinned.discard(b.name)

        big = sb(st0, "big", [128, 16400], F32)
        bigb = big.ap.bitcast(BF16)
        uT_ap = bigb[:, 0:16416].rearrange("p (c t) -> p c t", c=8)
        orwT_ap = bigb[:, 16416:24608].rearrange("p (c t) -> p c t", c=4)
        oglaT_ap = bigb[:, 24608:32800].rearrange("p (c t) -> p c t", c=4)
        x1_ap = big.ap[:, 0:16384].rearrange("p (b d) -> p b d", b=16)
        uTb = [Buf(f"uT{c}", None) for c in range(8)]
        orwTb = Buf("orwT", None)
        oglaTb = Buf("oglaT", None)
        x1b = [Buf(f"x1_{b}", None) for b in range(16)]

        identf = sb(st0, "identf", [128, 128], F32)
        identb = sb(st0, "identb", [128, 128], BF16)
        modp = sb(st0, "modp", [128, 32], F32)
        ones1 = sb(st0, "ones1", [1, 128], F32)
        scb = sb(st0, "scb", [128, 8], BF16)
        k.dma("sp", identf[:], cident_d, writes=[identf])
        k.dma("pool", identb[:], cident_d, writes=[identb])
        k.op("dve", lambda e: e.memset(ones1[:], 1.0), writes=[ones1])

        win_v = win_d.rearrange("(c p) n -> p c n", p=128)

        with ExitStack() as st:
            Nb = [[sb(st, f"N{h}{i}", [128, 4, 128], BF16) for i in range(2)] for h in range(2)]
            Lb = [[sb(st, f"L{h}{i}", [128, 4, 128], BF16) for i in range(2)] for h in range(2)]
            Sm = [sb(st, f"Sm{h}", [128, 4, 128], BF16) for h in range(2)]
            W1 = [sb(st, f"W1_{c}", [128, RW], BF16) for c in range(8)]
            W2 = [sb(st, f"W2_{c}", [128, RW], BF16) for c in range(8)]
            with ExitStack() as stp:
                cpp = sb(stp, "cpp", [128, 8], F32)
                bpp = sb(stp, "bpp", [128, 32], F32)
                wa = [sb(stp, f"wa{i}", [128, 8, 1024], BF16) for i in range(2)]
                mur = sb(stp, "mur", [128, RW], F32)
                omr = sb(stp, "omr", [128, RW], F32)
                stg = [sb(stp, f"stg{i}", [128, T], F32) for i in range(2)]
                k.dma("sp", cpp[:], cpp_d, writes=[cpp])
                k.dma("sp", bpp[:], bpp_d, writes=[bpp])
                ACT(scb[:], cpp[:], AF.Silu, [cpp], [scb])
                wada_v = wada_d.rearrange("(c p) n -> p c n", p=128)
                parts = (0, 1, 3, 4)
                for pi in range(2):
                    k.dma("pool", wa[pi][:], wada_v[:, :, parts[pi] * 1024:(parts[pi] + 1) * 1024], writes=[wa[pi]])
                k.dma("sp", mur[:], _bc_rows(mu_d, 128, RW), writes=[mur])
                TS("dve", omr[:], mur[:], -1.0, 1.0, ALU.mult, ALU.add, [mur], [omr])
                pm = bank(True)

                def p0_mm(pi):
                    w = wa[pi % 2]
                    for m in range(8):
                        col = pi * 8 + m
                        for c in range(8):
                            MM(pm[:, col:col + 1], w[:, c, m * 128:(m + 1) * 128], scb[:, c:c + 1], c == 0, c == 7, [w, scb], [pm], last=(m == 7 and c == 7))

                def prep(c):
                    w_ = stg[c % 2]
                    k.dma("sp", w_[:, 0:RW], win_v[:, c, 0:RW], writes=[w_])
                    TT("dve", W1[c][:], w_[:, 0:RW], omr[:], ALU.mult, [w_, omr], [W1[c]])
                    TT("pool", W2[c][:], w_[:, 0:RW], mur[:], ALU.mult, [w_, mur], [W2[c]])
                for c in range(4):
                    prep(c)
                p0_mm(0)
                p0_mm(1)
                for pi in range(2, 4):
                    k.dma("pool", wa[pi % 2][:], wada_v[:, :, parts[pi] * 1024:(parts[pi] + 1) * 1024], writes=[wa[pi % 2]])
                for c in range(4, 8):
                    prep(c)
                p0_mm(2)
                p0_mm(3)
                TT("dve", modp[:], pm[:, 0:32], bpp[:], ALU.add, [pm, bpp], [modp])
                unpin(pm)
                TS("dve", modp[:, 8:16], modp[:, 8:16], 1.0, None, ALU.add, None, [modp], [modp])
                TS("dve", modp[:, 24:32], modp[:, 24:32], 1.0, None, ALU.add, None, [modp], [modp])
                tap("modp", modp[:], [128, 32], [modp])
                for c in range(8):
                    s_ = stg[c % 2]
                    k.dma("sp", s_[:], xT_d[c * 128:(c + 1) * 128, :], writes=[s_])
                    k.op("dve", lambda e, c=c: e.memset(uT_ap[:, c, 0:1], 0.0), writes=[uTb[c]])
                    ACT(uT_ap[:, c, 1:T + 1], s_[:], AF.Identity, [s_, modp], [uTb[c]],
                        bias=modp[:, c:c + 1], scale=modp[:, 8 + c:9 + c])
                tap("uT", uT_ap[:, :, 1:T + 1], [128, 8, T], uTb)
                k.barrier()

            rows = sb(st, "rwrows", [128, 5, 512], F32)
            w0r = sb(st, "w0r", [1, 512], F32)
            a0r = sb(st, "a0r", [1, 512], F32)
            w2b = sb(st, "w2b", [128, 512], BF16)
            a2b = sb(st, "a2b", [128, 512], BF16)
            g2b = sb(st, "g2b", [128, 512], BF16)
            Mtri = sb(st, "Mtri", [128, 3, 128], F32)
            negcol = sb(st, "negcol", [128, 1], F32)
            mSI = sb(st, "mSI", [128, 2, 2, 128], F32)
            mND = sb(st, "mND", [128, 2, 128], F32)
            mSL4 = sb(st, "mSL4", [128, 2, 4, 128], F32)
            idb4 = sb(st, "idb4", [128, 4, 128], BF16)
            id8f = sb(st, "id8f", [64, 8, 64], F32)
            Hb = [sb(st, f"Hb{i}", [128, 8, 64], BF16) for i in range(2)]
            for i, d_ in enumerate((kk_d, ka_d, rk_d, gng_d, gnb_d)):
                k.dma("sp", rows[:, i, :], _bc_rows(d_, 128, 512), writes=[rows])
            k.dma("sp", w0r[:], w0_d, writes=[w0r])
            k.dma("sp", a0r[:], a0_d, writes=[a0r])
            k.op("dve", lambda e: e.memset(w2b[:], 0.0), writes=[w2b])
            k.op("dve", lambda e: e.memset(a2b[:], 0.0), writes=[a2b])
            k.dma("pool", w2b[0:64, :], w2_d, writes=[w2b])
            k.dma("pool", a2b[64:128, :], a2_d, writes=[a2b])
            k.dma("pool", g2b[:], g2_d, writes=[g2b])
            k.dma("sp", Mtri[:], ctri_d[:, 0:3, :], writes=[Mtri])
            TS("dve", Mtri[:], Mtri[:], -EM05, None, ALU.mult, None, [Mtri], [Mtri])
            k.op("dve", lambda e: e.memset(negcol[:], -EM05), writes=[negcol])
            for h2 in range(2):
                k.dma("sp", mSI[:, h2, 0, :], ctri_d[:, 0, :], writes=[mSI])
                k.dma("sp", mND[:, h2, :], ctri_d[:, 3, :], writes=[mND])
                k.dma("sp", mSI[:, h2, 1, :], ctri_d[:, 1, :], writes=[mSI])
            for h4 in range(4):
                k.dma("sp", mSL4[:, 0, h4, :], ctri_d[:, 4, :], writes=[mSL4])
                k.dma("sp", mSL4[:, 1, h4, :], ctri_d[:, 5, :], writes=[mSL4])
                k.dma("pool", idb4[:, h4, :], cident_d, writes=[idb4])
            for h in range(8):
                k.dma("sp", id8f[:, h, :], cident_d[0:64, 0:64], writes=[id8f])
            k.op("dve", lambda e: e.memset(Hb[0][:], 0.0), writes=[Hb[0]])
            k.op("dve", lambda e: e.memset(Hb[1][:], 0.0), writes=[Hb[1]])

            def f32t(name):
                return sb(st, name, [128, 512], F32)
            sgm, a_t, g_t, r_t, k_t, v_t = [f32t(n) for n in ("sgm", "a_t", "g_t", "r_t", "k_t", "v_t")]
            kkn, kmod, bvec = f32t("kkn"), f32t("kmod"), f32t("bvec")
            S0 = f32t("S0")
            ogl_f = big.ap[:, 12304:16400]
            EinT = Buf("EinT", ogl_f[:, 0:512].rearrange("p (c t) -> p c t", c=4))
            EninT = Buf("EninT", ogl_f[:, 512:1024].rearrange("p (c t) -> p c t", c=4))
            EexT = Buf("EexT", ogl_f[:, 1024:1536].rearrange("p (c t) -> p c t", c=4))
            bon = Buf("bon", ogl_f[:, 1536:2048])
            S1 = Buf("S1", ogl_f[:, 2048:2560])
            S2 = Buf("S2", ogl_f[:, 2560:3072])
            Eex = Buf("Eex", ogl_f[:, 3072:3584])
            Erev = Buf("Erev", ogl_f[:, 3584:4096])
            v_bf = sb(st, "v_bf", [128, 512], BF16)
            twad = sb(st, "twad", [128, 128], BF16)
            sgT = sb(st, "sgT", [128, 128], BF16)
            small = sb(st, "small", [128, 6, 8], F32)
            X = sb(st, "X", [128, 8, 2, 64], BF16)
            Bh = sb(st, "Bh", [128, 512], BF16)
            Kh = sb(st, "Kh", [128, 512], BF16)
            AR = sb(st, "AR", [128, 4, 2, 128], BF16)
            BTz = sb(st, "BTz", [128, 4, 2, 128], BF16)
            KTz = sb(st, "KTz", [128, 4, 2, 128], BF16)
            k.op("dve", lambda e: e.memset(BTz[:], 0.0), writes=[BTz])
            k.op("dve", lambda e: e.memset(KTz[:], 0.0), writes=[KTz])
            gC = sb(st, "gC", [64, 8], F32)
            ArbT = [sb(st, f"ArbT{h}", [128, 4, 128], BF16) for h in range(2)]
            MakT = [sb(st, f"MakT{h}", [128, 4, 128], BF16) for h in range(2)]
            ArkT = [sb(st, f"ArkT{h}", [128, 4, 128], BF16) for h in range(2)]
            WU = sb(st, "WU", [128, 8, 2, 64], BF16)
            Dg = sb(st, "Dg", [64, 8, 64], F32)
            PTb = sb(st, "PTb", [128, 8, 64], BF16)
            QeT = sb(st, "QeT", [128, 8, 128], BF16)
            k.op("dve", lambda e: e.memset(PTb[:], 0.0), writes=[PTb])
            k.op("dve", lambda e: e.memset(QeT[:], 0.0), writes=[QeT])
            o_bf = sb(st, "o_bf", [128, 512], BF16)

            def v3(ap, a):
                return ap.rearrange("p (a b) -> p a b", a=a)

            def bfv(b_, half):
                return b_.ap.bitcast(BF16)[:, half * 512:(half + 1) * 512].rearrange("p (c t) -> p c t", c=4)
            alias3 = [(bfv(sgm, hf_), bfv(a_t, hf_), bfv(k_t, hf_)) for hf_ in range(2)]
            aliasb = [Buf(f"alias{hf_}", None) for hf_ in range(2)]

            def hv(b_, kind):
                out = []
                for hf_ in range(2):
                    if kind == "tok":
                        ap_ = b_.ap[:, hf_ * 256:(hf_ + 1) * 256]
                    elif kind == "ch":
                        ap_ = b_.ap[:, 2 * hf_:2 * hf_ + 2]
                    elif kind == "hd":
                        ap_ = b_.ap[:, 4 * hf_:4 * hf_ + 4]
                    else:
                        ap_ = b_.ap[:, :, 4 * hf_:4 * hf_ + 4]
                    out.append(Buf(f"{b_.name}_{hf_}", ap_))
                return out
            sgmH, a_tH, g_tH, r_tH, k_tH, v_tH = [hv(b_, "tok") for b_ in (sgm, a_t, g_t, r_t, k_t, v_t)]
            kknH, kmodH, bvecH, S0H, S1H, S2H = [hv(b_, "tok") for b_ in (kkn, kmod, bvec, S0, S1, S2)]
            EexH, ErevH, bonH, v_bfH, BhH, KhH, o_bfH = [hv(b_, "tok") for b_ in (Eex, Erev, bon, v_bf, Bh, Kh, o_bf)]
            EinTH, EninTH, EexTH, ARH, BTzH, KTzH = [hv(b_, "ch") for b_ in (EinT, EninT, EexT, AR, BTz, KTz)]
            XH, WUH, DgH, PTbH, QeTH, gCH = [hv(b_, "hd") for b_ in (X, WU, Dg, PTb, QeT, gC)]
            HbH = [hv(b_, "hd") for b_ in Hb]
            smallH = hv(small, "sm")
            orwTH = [Buf(f"orwT_{hf_}", None) for hf_ in range(2)]

            nch_run = NCH if "rw_short" not in taps else 2
            if "rw_cut0" in taps:
                nch_run = 0
            def PROJ(n):
                t0 = n * 128
                ucur = [uT_ap[:, c, t0 + 1:t0 + 129] for c in range(8)]
                uprv = [uT_ap[:, c, t0:t0 + 128] for c in range(8)]

                def proj_tok(pb_, c0, c1):
                    for c in range(8):
                        MM(pb_[:, 0:c1 - c0], ucur[c], W1[c][:, c0:c1], c == 0, False, [uTb[c], W1[c]], [pb_])
                        MM(pb_[:, 0:c1 - c0], uprv[c], W2[c][:, c0:c1], False, c == 7, [uTb[c], W2[c]], [pb_])

                def proj_ch(out_ap, pb_, c0, c1):
                    for c in range(8):
                        MM(out_ap, W1[c][:, c0:c1], ucur[c], c == 0, False, [uTb[c], W1[c]], [pb_])
                        MM(out_ap, W2[c][:, c0:c1], uprv[c], False, c == 7, [uTb[c], W2[c]], [pb_])

                pL = bank()
                pLv = v3(pL[:], 4)
                proj_ch(pLv[:, 0, :], pL, 1536, 1664)
                proj_ch(pLv[:, 2, :], pL, 1664, 1792)
                ACT(twad[0:64, :], pLv[0:64, 0, :], AF.Tanh, [pL], [twad])
                ACT(twad[64:128, :], pLv[64:128, 0, :], AF.Copy, [pL], [twad])
                ACT(sgT[:], pLv[:, 2, :], AF.Sigmoid, [pL], [sgT])
                pR, pK, pV = bank(True), bank(True), bank(True)
                proj_tok(pR, 0, 512)
                proj_tok(pK, 512, 1024)
                proj_tok(pV, 1024, 1536)
                pW, pA, pG = bank(True), bank(True), bank(True)
                MM(pW[:], twad[:], w2b[:], True, False, [twad, w2b], [pW])
                MM(pW[:], ones1[:], w0r[:], False, True, [ones1, w0r], [pW])
                MM(pA[:], twad[:], a2b[:], True, False, [twad, a2b], [pA])
                MM(pA[:], ones1[:], a0r[:], False, True, [ones1, a0r], [pA])
                MM(pG[:], sgT[:], g2b[:], True, True, [sgT, g2b], [pG])
                return pR, pK, pV, pW, pA, pG

            nxt = PROJ(0) if nch_run > 0 else None
            e1_done = False
            for n in range(nch_run):
                t0 = n * 128
                if not (n > 0 and e1_done):
                    pR, pK, pV, pW, pA, pG = nxt
                Hc, Hn = HbH[n % 2], HbH[(n + 1) % 2]
                pg = bank(True)
                pCs, pTs, pYs = [None, None], [None, None], [None, None]
                v4 = lambda ap: ap.rearrange("p (a b) -> p a b", a=4)
                cs_ = lambda hf: slice(hf * 256, hf * 256 + 256)

                def E1(hf):
                    if hf == 1:
                        return
                    ACT(sgm[:], pW[:], AF.Sigmoid, [pW], sgmH)
                    CP("dve", k_t[:], pK[:], [pK], k_tH)
                    ACT(a_t[:], pA[:], AF.Sigmoid, [pA], a_tH)
                    ACT(r_t[:], pR[:], AF.Copy, [pR], r_tH)
                    ACT(v_t[:], pV[:], AF.Copy, [pV], v_tH)
                    CP("pool", v_bf[:], v_t[:], v_tH, v_bfH)
                    unpin(pR, pK, pV, pW, pA)

                def Eg():
                    ACT(g_t[:], pG[:], AF.Copy, [pG], g_tH)
                    unpin(pG)

                def C1(hf):
                    s_ = sgmH[hf]
                    pC, pT = bank(True), bank(True)
                    MM(pC[:, 0:256], Mtri[:, 0, :], s_[:], True, True, [Mtri, s_], [pC], last=False)
                    MM(pC[:, 256:512], Mtri[:, 2, :], s_[:], True, True, [Mtri, s_], [pC])
                    for i in range(2):
                        MM(v3(pT[:], 4)[:, i, :], s_[:, i * 128:(i + 1) * 128], Mtri[:, 1, :], True, True, [Mtri, s_], [pT], last=False)
                        MM(v3(pT[:], 4)[:, 2 + i, :], s_[:, i * 128:(i + 1) * 128], Mtri[:, 0, :], True, True, [Mtri, s_], [pT], last=(i == 1))
                    for j in range(4):
                        MM(pg[0:64, 4 * hf + j:4 * hf + j + 1], s_[:, j * 64:(j + 1) * 64], negcol[:], True, True, [s_, negcol], [pg], last=(j == 3))
                    pCs[hf], pTs[hf] = pC, pT

                def X1(hf):
                    pC, pT = pCs[hf], pTs[hf]
                    ACT(EexH[hf][:], pC[:, 0:256], AF.Exp, [pC], [EexH[hf]])
                    yield
                    ACT(ErevH[hf][:], pC[:, 256:512], AF.Exp, [pC], [ErevH[hf]])
                    yield
                    ACT(EinTH[hf][:], v3(pT[:], 4)[:, 0:2, :], AF.Exp, [pT], [EinTH[hf]])
                    yield
                    ACT(EninTH[hf][:], v3(pT[:], 4)[:, 0:2, :], AF.Exp, [pT], [EninTH[hf]], scale=-1.0)
                    yield
                    ACT(EexTH[hf][:], v3(pT[:], 4)[:, 2:4, :], AF.Exp, [pT], [EexTH[hf]])
                    yield
                    ACT(gCH[hf][:], pg[0:64, 4 * hf:4 * hf + 4], AF.Exp, [pg], [gCH[hf]])
                    unpin(pC, pT)
                    if hf == 1:
                        unpin(pg)

                def K1(hf):
                    cs, sm = cs_(hf), smallH[hf]
                    TT("pool", S0H[hf][:], k_tH[hf][:], rows[:, 0, cs], ALU.mult, [k_tH[hf], rows], [S0H[hf]])
                    yield
                    TT("pool", S1H[hf][:], S0H[hf][:], S0H[hf][:], ALU.mult, [S0H[hf]], [S1H[hf]])
                    yield
                    k.op("dve", lambda e: e.tensor_reduce(sm[:, 0, :], v4(S1H[hf][:]), AX.X, ALU.add), [S1H[hf]], [sm])
                    yield
                    TS("dve", sm[:, 1, :], sm[:, 0, :], 1e-24, None, ALU.add, None, [sm], [sm])
                    yield
                    k.op("dve", lambda e: e.reciprocal(sm[:, 1, :], sm[:, 1, :]), [sm], [sm])
                    yield
                    ACT(sm[:, 1, :], sm[:, 1, :], AF.Sqrt, [sm], [sm])
                    yield
                    TT("dve", v4(kknH[hf][:]), v4(S0H[hf][:]), sm[:, 1, :].unsqueeze(2).to_broadcast([128, 4, 64]), ALU.mult, [S0H[hf], sm], [kknH[hf]])

                def A1(hf):
                    cs = cs_(hf)
                    STT("dve", S2H[hf][:], a_tH[hf][:], -1.0, rows[:, 1, cs], ALU.add, ALU.mult, [a_tH[hf], rows], [S2H[hf]])
                    yield
                    STT("dve", kmodH[hf][:], S2H[hf][:], 1.0, k_tH[hf][:], ALU.add, ALU.mult, [S2H[hf], k_tH[hf]], [kmodH[hf]])
                    yield
                    TT("pool", bvecH[hf][:], kknH[hf][:], a_tH[hf][:], ALU.mult, [kknH[hf], a_tH[hf]], [bvecH[hf]])

                def O1(hf):
                    STT("dve", XH[hf][:, :, 0, :], v4(kknH[hf][:]), -1.0, v4(EexH[hf][:]), ALU.mult, ALU.mult, [kknH[hf], EexH[hf]], [XH[hf]])
                    yield
                    TT("dve", BhH[hf][:], bvecH[hf][:], ErevH[hf][:], ALU.mult, [bvecH[hf], ErevH[hf]], [BhH[hf]])
                    yield
                    TT("pool", KhH[hf][:], kmodH[hf][:], ErevH[hf][:], ALU.mult, [kmodH[hf], ErevH[hf]], [KhH[hf]])

                def B1(hf):
                    cs, sm = cs_(hf), smallH[hf]
                    TT("pool", S1H[hf][:], r_tH[hf][:], kmodH[hf][:], ALU.mult, [r_tH[hf], kmodH[hf]], [S1H[hf]])
                    yield
                    TT("pool", S1H[hf][:], S1H[hf][:], rows[:, 2, cs], ALU.mult, [S1H[hf], rows], [S1H[hf]])
                    yield
                    k.op("dve", lambda e: e.tensor_reduce(sm[:, 2, :], v4(S1H[hf][:]), AX.X, ALU.add), [S1H[hf]], [sm])
                    yield
                    TT("dve", v4(bonH[hf][:]), v4(v_tH[hf][:]), sm[:, 2, :].unsqueeze(2).to_broadcast([128, 4, 64]), ALU.mult, [v_tH[hf], sm], [bonH[hf]])

                def T1(hf):
                    pA_, pB_ = bank(True), bank(True)
                    for i in range(2):
                        sl = slice(i * 128, (i + 1) * 128)
                        TR(v3(pA_[:], 4)[:, i, :], kknH[hf][:, sl], identf[:], [kknH[hf], identf], [pA_], last=False)
                        TR(v3(pA_[:], 4)[:, 2 + i, :], r_tH[hf][:, sl], identf[:], [r_tH[hf], identf], [pA_], last=(i == 1))
                        TR(v3(pB_[:], 4)[:, i, :], bvecH[hf][:, sl], identf[:], [bvecH[hf], identf], [pB_], last=False)
                        TR(v3(pB_[:], 4)[:, 2 + i, :], kmodH[hf][:, sl], identf[:], [kmodH[hf], identf], [pB_], last=(i == 1))
                    ARh = ARH[hf]
                    yield
                    STT("dve", ARh[:, :, 0, :], v3(pA_[:], 4)[:, 0:2, :], -1.0, EexTH[hf][:], ALU.mult, ALU.mult, [pA_, EexTH[hf]], [ARh])
                    yield
                    TT("dve", ARh[:, :, 1, :], v3(pA_[:], 4)[:, 2:4, :], EinTH[hf][:], ALU.mult, [pA_, EinTH[hf]], [ARh])
                    yield
                    for hh in range(2):
                        ps_ = slice(hh * 64, hh * 64 + 64)
                        TT("dve", BTzH[hf][ps_, :, hh, :], v3(pB_[:], 4)[ps_, 0:2, :], EninTH[hf][ps_], ALU.mult, [pB_, EninTH[hf]], [BTzH[hf]])
                        TT("dve", KTzH[hf][ps_, :, hh, :], v3(pB_[:], 4)[ps_, 2:4, :], EninTH[hf][ps_], ALU.mult, [pB_, EninTH[hf]], [KTzH[hf]])
                    unpin(pA_, pB_)

                def I1(hf):
                    pLA = [bank(), bank()]
                    pMA = [bank(), bank()]
                    pLL = bank()
                    for j in range(4):
                        c4l, hh = j // 2, j % 2
                        ar_rhs = ARH[hf][:, c4l, :, :].rearrange("p a t -> p (a t)")
                        o1 = pLA[j // 2][:, (j % 2) * 256:(j % 2) * 256 + 256]
                        o2 = pMA[j // 2][:, (j % 2) * 256:(j % 2) * 256 + 256]
                        MM(o1, BTzH[hf][:, c4l, hh, :], ar_rhs, True, True, [BTzH[hf], ARH[hf]], [pLA[j // 2]], last=(j == 3))
                        MM(o2, KTzH[hf][:, c4l, hh, :], ar_rhs, True, True, [KTzH[hf], ARH[hf]], [pMA[j // 2]], last=(j == 3))
                        MM(pLL[:, j * 128:(j + 1) * 128], ARH[hf][:, c4l, 0, :], BTzH[hf][:, c4l, hh, :], True, True, [ARH[hf], BTzH[hf]], [pLL], last=(j == 3))
                    N0, L0 = Nb[hf][0], Lb[hf][0]
                    for q in range(2):
                        src = pLA[q][:].rearrange("p (h a t) -> p h a t", h=2, a=2)
                        TT("dve", N0[:, 2 * q:2 * q + 2, :], src[:, :, 0, :], mND[:], ALU.mult, [pLA[q], mND], [N0])
                        TT("dve", ArbT[hf][:, 2 * q:2 * q + 2, :], src[:, :, 1, :], mSI[:, :, 1, :], ALU.mult, [pLA[q], mSI], [ArbT[hf]])
                        src2 = pMA[q][:].rearrange("p (h a t) -> p h a t", h=2, a=2)
                        TT("dve", MakT[hf][:, 2 * q:2 * q + 2, :], src2[:, :, 0, :], mSI[:, :, 0, :], ALU.mult, [pMA[q], mSI], [MakT[hf]])
                        TT("dve", ArkT[hf][:, 2 * q:2 * q + 2, :], src2[:, :, 1, :], mSI[:, :, 1, :], ALU.mult, [pMA[q], mSI], [ArkT[hf]])
                    TT("dve", L0[:], v3(pLL[:], 4), mSL4[:, 0], ALU.mult, [pLL, mSL4], [L0])
                    TT("dve", bfv(sgm, hf), v3(pLL[:], 4), mSL4[:, 1], ALU.mult, [pLL, mSL4], [sgmH[hf]])
                    TT("pool", Sm[hf][:], N0[:], idb4[:], ALU.add, [N0, idb4], [Sm[hf]])

                def NLa(hf, lev):
                    cur = lev % 2
                    Nc, Lc = Nb[hf][cur], Lb[hf][cur]
                    Nn, Ln = Nb[hf][1 - cur], Lb[hf][1 - cur]
                    pL2 = bank()
                    for j in range(4):
                        MM(v3(pL2[:], 4)[:, j, :], Nc[:, j, :], Lc[:, j, :], True, True, [Nc, Lc], [pL2], last=(j == 3))
                    if lev < 4:
                        pN2 = bank()
                        for j in range(4):
                            MM(v3(pN2[:], 4)[:, j, :], Lc[:, j, :], Nc[:, j, :], True, True, [Nc, Lc], [pN2], last=(j == 3))
                    ACT(Ln[:], v3(pL2[:], 4), AF.Copy, [pL2], [Ln])
                    if lev < 4:
                        CP("dve", Nn[:], v3(pN2[:], 4), [pN2], [Nn])

                def NLb(hf, lev):
                    Ln = Lb[hf][1 - (lev % 2)]
                    pS = bank()
                    for j in range(4):
                        MM(v3(pS[:], 4)[:, j, :], Ln[:, j, :], Sm[hf][:, j, :], True, False, [Ln, Sm[hf]], [pS])
                        MM(v3(pS[:], 4)[:, j, :], identb[:], Sm[hf][:, j, :], False, True, [identb, Sm[hf]], [pS], last=(j == 3))
                    CP("act" if hf == 0 else "dve", Sm[hf][:], v3(pS[:], 4), [pS], [Sm[hf]])

                def MGa(hf):
                    Lo_ap, Tm_ap, Zb_ap = bfv(sgm, hf), bfv(a_t, hf), bfv(k_t, hf)
                    pTt = bank()
                    pTtb = pTt[:].bitcast(BF16)[:, 0:512].rearrange("p (c t) -> p c t", c=4)
                    for j in range(4):
                        TR(pTtb[:, j, :], Sm[hf][:, j, :], identb[:], [Sm[hf], identb], [pTt], last=(j == 3))
                    ACT(Tm_ap, pTtb, AF.Copy, [pTt], [a_tH[hf]])
                    pZ_ = bank()
                    for j in range(4):
                        MM(v3(pZ_[:], 4)[:, j, :], Lo_ap[:, j, :], Sm[hf][:, j, :], True, True, [sgmH[hf], Sm[hf]], [pZ_], last=(j == 3))
                    CP("dve", Zb_ap, v3(pZ_[:], 4), [pZ_], [k_tH[hf]])

                def MGb(hf):
                    Tm_ap, Zb_ap = bfv(a_t, hf), bfv(k_t, hf)
                    pS = bank()
                    for j in range(4):
                        MM(v3(pS[:], 4)[:, j, :], Tm_ap[:, j, :], Zb_ap[:, j, :], True, False, [a_tH[hf], k_tH[hf]], [pS])
                        MM(v3(pS[:], 4)[:, j, :], identb[:], Sm[hf][:, j, :], False, True, [identb, Sm[hf]], [pS], last=(j == 3))
                    ACT(Sm[hf][:], v3(pS[:], 4), AF.Copy, [pS], [Sm[hf]])

                def P1a(hf):
                    pMV = bank()
                    for j in range(4):
                        MM(pMV[:, j * 64:(j + 1) * 64], MakT[hf][:, j, :], v_bfH[hf][:, j * 64:(j + 1) * 64], True, True, [MakT[hf], v_bfH[hf]], [pMV], last=(j == 3))
                    ACT(XH[hf][:, :, 1, :], v4(pMV[:, 0:256]), AF.Copy, [pMV], [XH[hf]])

                def P1b(hf):
                    pWU = bank()
                    for j in range(4):
                        MM(pWU[:, j * 128:(j + 1) * 128], Sm[hf][:, j, :], XH[hf][:, j, :, :].rearrange("p a b -> p (a b)"), True, True, [Sm[hf], XH[hf]], [pWU], last=(j == 3))
                    ACT(WUH[hf][:].rearrange("p h a b -> p (h a b)"), pWU[:], AF.Copy, [pWU], [WUH[hf]])

                def P1c(hf):
                    pP = bank(True)
                    yield
                    for j in range(4):
                        MM(pP[0:64, j * 64:(j + 1) * 64], WUH[hf][:, j, 0, :], BhH[hf][:, j * 64:(j + 1) * 64], True, True, [WUH[hf], BhH[hf]], [pP], last=(j == 3))
                    yield
                    TT("pool", DgH[hf][:], id8f[:, 0:4, :], gCH[hf][:].unsqueeze(2).to_broadcast([64, 4, 64]), ALU.mult, [id8f, gCH[hf]], [DgH[hf]])
                    yield
                    TT("dve", PTbH[hf][0:64], v4(pP[0:64, 0:256]), DgH[hf][:], ALU.add, [pP, DgH[hf]], [PTbH[hf]])
                    yield
                    pQ = bank(True)
                    yield
                    for j in range(4):
                        p0 = (j % 2) * 64
                        oq = pQ[0:64, j * 128:(j + 1) * 128]
                        MM(oq, WUH[hf][:, j, 0, :], ArbT[hf][:, j, :], True, False, [WUH[hf], ArbT[hf]], [pQ])
                        MM(oq, identb[:, p0:p0 + 64], ARH[hf][:, j // 2, 1, :], False, True, [identb, ARH[hf]], [pQ], last=(j == 3))
                    yield
                    ACT(QeTH[hf][0:64], v3(pQ[0:64, :], 4), AF.Copy, [pQ], [QeTH[hf]])
                    unpin(pP, pQ)

                def P1d(hf):
                    pY = bank(True)
                    yield
                    for j in range(4):
                        oy = pY[:, j * 64:(j + 1) * 64]
                        vj = v_bfH[hf][:, j * 64:(j + 1) * 64]
                        MM(oy, QeTH[hf][:, j, :], Hc[hf][:, j, :], True, False, [QeTH[hf], Hc[hf]], [pY])
                        MM(oy, ArbT[hf][:, j, :], WUH[hf][:, j, 1, :], False, False, [ArbT[hf], WUH[hf]], [pY])
                        MM(oy, ArkT[hf][:, j, :], vj, False, True, [ArkT[hf], v_bfH[hf]], [pY], last=(j == 3))
                    yield
                    pH = bank(True)
                    yield
                    for j in range(4):
                        oh = pH[0:64, j * 64:(j + 1) * 64]
                        vj = v_bfH[hf][:, j * 64:(j + 1) * 64]
                        MM(oh, PTbH[hf][:, j, :], Hc[hf][:, j, :], True, False, [PTbH[hf], Hc[hf]], [pH])
                        MM(oh, BhH[hf][:, j * 64:(j + 1) * 64], WUH[hf][:, j, 1, :], False, False, [BhH[hf], WUH[hf]], [pH])
                        MM(oh, KhH[hf][:, j * 64:(j + 1) * 64], vj, False, True, [KhH[hf], v_bfH[hf]], [pH], last=(j == 3))
                    yield
                    ACT(Hn[hf][0:64], v4(pH[0:64, 0:256]), AF.Copy, [pH], [Hn[hf]])
                    unpin(pH)
                    pYs[hf] = pY

                def F1(hf):
                    cs, sm, pY = cs_(hf), smallH[hf], pYs[hf]
                    y_, q_ = S0H[hf], S2H[hf]
                    bc = lambda r_: sm[:, r_, :].unsqueeze(2).to_broadcast([128, 4, 64])
                    ACT(y_[:], pY[:, 0:256], AF.Copy, [pY], [y_])
                    unpin(pY)
                    yield
                    k.op("dve", lambda e: e.tensor_reduce(sm[:, 3, :], v4(y_[:]), AX.X, ALU.add), [y_], [sm])
                    yield
                    TT("pool", q_[:], y_[:], y_[:], ALU.mult, [y_], [q_])
                    yield
                    k.op("dve", lambda e: e.tensor_reduce(sm[:, 4, :], v4(q_[:]), AX.X, ALU.add), [q_], [sm])
                    yield
                    TS("dve", sm[:, 3, :], sm[:, 3, :], 1.0 / 64, None, ALU.mult, None, [sm], [sm])
                    yield
                    TT("dve", sm[:, 5, :], sm[:, 3, :], sm[:, 3, :], ALU.mult, [sm], [sm])
                    yield
                    STT("dve", sm[:, 4, :], sm[:, 4, :], 1.0 / 64, sm[:, 5, :], ALU.mult, ALU.subtract, [sm], [sm])
                    yield
                    TS("dve", sm[:, 4, :], sm[:, 4, :], 64e-5, None, ALU.add, None, [sm], [sm])
                    yield
                    k.op("dve", lambda e: e.reciprocal(sm[:, 4, :], sm[:, 4, :]), [sm], [sm])
                    yield
                    ACT(sm[:, 4, :], sm[:, 4, :], AF.Sqrt, [sm], [sm])

                def F1b(hf):
                    cs, sm = cs_(hf), smallH[hf]
                    y_ = S0H[hf]
                    bc = lambda r_: sm[:, r_, :].unsqueeze(2).to_broadcast([128, 4, 64])
                    TT("dve", v4(y_[:]), v4(y_[:]), bc(3), ALU.subtract, [y_, sm], [y_])
                    yield
                    TT("dve", v4(y_[:]), v4(y_[:]), bc(4), ALU.mult, [y_, sm], [y_])
                    yield
                    TT("pool", y_[:], y_[:], rows[:, 3, cs], ALU.mult, [y_, rows], [y_])
                    yield
                    TT("pool", y_[:], y_[:], rows[:, 4, cs], ALU.add, [y_, rows], [y_])
                    yield
                    TT("pool", y_[:], y_[:], bonH[hf][:], ALU.add, [y_, bonH[hf]], [y_])
                    yield
                    TT("pool", o_bfH[hf][:], y_[:], g_tH[hf][:], ALU.mult, [y_, g_tH[hf]], [o_bfH[hf]])
                    yield
                    pO = bank(True)
                    pOb = pO[:].bitcast(BF16)[:, 0:256].rearrange("p (c t) -> p c t", c=2)
                    yield
                    for i in range(2):
                        TR(pOb[:, i, :], o_bfH[hf][:, i * 128:(i + 1) * 128], identb[:], [o_bfH[hf], identb], [pO], last=(i == 1))
                    yield
                    ACT(orwT_ap[:, 2 * hf:2 * hf + 2, t0:t0 + 128], pOb, AF.Copy, [pO], [orwTH[hf]])
                    unpin(pO)

                def both(fn, *a):
                    gens = [fn(hf_, *a) for hf_ in range(2)]
                    gens = [g for g in gens if g is not None]
                    while gens:
                        for g in list(gens):
                            try:
                                next(g)
                            except StopIteration:
                                gens.remove(g)
                if not (n > 0 and e1_done):
                    both(E1)
                Eg()
                for fn in (C1, K1, X1, A1, O1, B1, T1, I1):
                    both(fn)
                for lev in range(5):
                    both(NLa, lev)
                    both(NLb, lev)
                for fn in (MGa, MGb, P1a, P1b, P1c, P1d):
                    both(fn)
                if n + 1 < nch_run:
                    nxt = PROJ(n + 1)
                both(F1)
                if n + 1 < nch_run:
                    pR, pK, pV, pW, pA, pG = nxt
                    both(E1)
                    e1_done = True
                else:
                    e1_done = False
                both(F1b)
            if "rw_short" in taps:
                tap("orwT", orwT_ap[:, :, 0:256], [128, 4, 256], orwTH)
            else:
                tap("orwT", orwT_ap, [128, 4, T], orwTH)
            k.barrier()
        if stop_after <= 2:
            k.finish()
            return nc, tap_d

        with ExitStack() as st:
            WG = [sb(st, f"WG{c}", [128, GL], BF16) for c in range(8)]
            for c in range(8):
                k.dma("pool", WG[c][:], win_v[:, c, RW:RW + GL], writes=[WG[c]])
            ga2 = sb(st, "ga2", [128, 256], F32)
            ngr = sb(st, "ngr", [128, 4, 128], F32)
            Gtri = sb(st, "Gtri", [128, 3, 128], F32)
            c16 = sb(st, "c16", [128, 1], F32)
            mI4 = sb(st, "mI4", [128, 4, 128], F32)
            k.op("dve", lambda e: e.memset(ga2[:], 0.0), writes=[ga2])
            k.dma("sp", ga2[0:16, :], ga2_d, writes=[ga2])
            gabr = sb(st, "gabr", [128, 256], F32)
            k.dma("sp", gabr[:], _bc_rows(gab_d, 128, 256), writes=[gabr])
            k.dma("sp", ngr[:], bass.AP(gng2_d.tensor, gng2_d.offset, [[0, 128], [0, 4], [1, 128]]), writes=[ngr])
            k.dma("sp", Gtri[:], ctri_d[:, 0:3, :], writes=[Gtri])
            TS("dve", Gtri[:], Gtri[:], -1.0 / 16, None, ALU.mult, None, [Gtri], [Gtri])
            k.op("dve", lambda e: e.memset(c16[:], -1.0 / 16), writes=[c16])
            for h4 in range(4):
                k.dma("sp", mI4[:, h4, :], ctri_d[:, 1, :], writes=[mI4])
            Sst = sb(st, "Sst", [128, 4, 128], F32)
            Sbf = sb(st, "Sbf", [128, 4, 128], BF16)
            k.op("dve", lambda e: e.memset(Sst[:], 0.0), writes=[Sst])
            k.op("dve", lambda e: e.memset(Sbf[:], 0.0), writes=[Sbf])

            def v3(ap, a):
                return ap.rearrange("p (a b) -> p a b", a=a)

            def gset(i):
                B = {}
                B["adT"] = sb(st, f"g_adT{i}", [128, 128], F32)
                k.op("dve", lambda e: e.memset(B["adT"][:], 0.0), writes=[B["adT"]])

                B["qkT"] = sb(st, f"g_qkT{i}", [128, 4, 128], F32)
                B["gk"] = sb(st, f"g_gk{i}", [128, 256], F32)
                B["ez"] = sb(st, f"g_ez{i}", [128, 256], F32)
                B["lz"] = sb(st, f"g_lz{i}", [128, 256], F32)
                B["Erev"] = sb(st, f"g_Erev{i}", [128, 256], F32)
                B["Ein"] = sb(st, f"g_Ein{i}", [128, 2, 128], F32)
                B["Enin"] = sb(st, f"g_Enin{i}", [128, 2, 128], F32)
                B["decs"] = sb(st, f"g_decs{i}", [128, 2], F32)
                B["kdec"] = sb(st, f"g_kdec{i}", [128, 256], BF16)
                B["gv"] = sb(st, f"g_v{i}", [128, 512], BF16)
                B["sgg"] = sb(st, f"g_sgg{i}", [128, 512], F32)
                B["QsT"] = sb(st, f"g_QsT{i}", [128, 2, 128], BF16)
                B["KsTz"] = sb(st, f"g_KsTz{i}", [128, 2, 2, 128], BF16)
                k.op("dve", lambda e: e.memset(B["KsTz"][:], 0.0), writes=[B["KsTz"]])
                B["attT"] = sb(st, f"g_attT{i}", [128, 4, 128], BF16)
                B["osb"] = sb(st, f"g_osb{i}", [128, 512], F32)
                B["osq"] = sb(st, f"g_osq{i}", [128, 512], F32)
                B["gsm"] = sb(st, f"g_sm{i}", [128, 2, 4], F32)
                B["of"] = sb(st, f"g_of{i}", [128, 512], BF16)
                return B
            GS = [gset(0), gset(1)]

            def G1(n, B):
                ucur = [uT_ap[:, c, n * 128 + 1:n * 128 + 129] for c in range(8)]
                pC, pD = bank(), bank()
                pCv = v3(pC[:], 4)
                for i, c0 in enumerate((0, 128, 256, 384)):
                    for c in range(8):
                        MM(pCv[:, i, :], WG[c][:, c0:c0 + 128], ucur[c], c == 0, c == 7, [WG[c], uTb[c]], [pC])
                for c in range(8):
                    MM(pD[0:16, 0:128], WG[c][:, 1536:1552], ucur[c], c == 0, c == 7, [WG[c], uTb[c]], [pD])
                CP("dve", B["adT"][0:16, :], pD[0:16, 0:128], [pD], [B["adT"]])
                ACT(B["qkT"][:], pCv, AF.Copy, [pC], [B["qkT"]])

            def G2(n, B):
                ucur = [uT_ap[:, c, n * 128 + 1:n * 128 + 129] for c in range(8)]
                pK, pV, pGg = bank(), bank(), bank()
                for c in range(8):
                    MM(pK[:, 0:256], ucur[c], WG[c][:, 256:512], c == 0, c == 7, [WG[c], uTb[c]], [pK])
                for c in range(8):
                    MM(pV[:], ucur[c], WG[c][:, 512:1024], c == 0, c == 7, [WG[c], uTb[c]], [pV])
                for c in range(8):
                    MM(pGg[:], ucur[c], WG[c][:, 1024:1536], c == 0, c == 7, [WG[c], uTb[c]], [pGg])
                CP("dve", B["gk"][:], pK[:, 0:256], [pK], [B["gk"]])
                ACT(B["gv"][:], pV[:], AF.Copy, [pV], [B["gv"]])
                ACT(B["sgg"][:], pGg[:], AF.Silu, [pGg], [B["sgg"]])

            def G3(n, B):
                pZ = bank()
                MM(pZ[:, 0:256], B["adT"][:], ga2[:], True, True, [B["adT"], ga2], [pZ])
                TT("dve", B["ez"][:], pZ[:, 0:256], gabr[:], ALU.add, [pZ, gabr], [B["ez"]])
                ACT(B["ez"][:], B["ez"][:], AF.Exp, [B["ez"]], [B["ez"]], scale=-1.0)
                ACT(B["lz"][:], B["ez"][:], AF.Ln, [B["ez"]], [B["lz"]], bias=1.0)

            def G4(n, B):
                lz = B["lz"]
                pB, pDc = bank(), bank()
                MM(pB[:, 0:256], Gtri[:, 2, :], lz[:], True, True, [Gtri, lz], [pB], last=False)
                for c2 in range(2):
                    MM(pB[:, 256 + c2 * 128:384 + c2 * 128], lz[:, c2 * 128:(c2 + 1) * 128], Gtri[:, 1, :], True, True, [Gtri, lz], [pB], last=(c2 == 1))
                for c2 in range(2):
                    MM(pDc[:, c2:c2 + 1], lz[:, c2 * 128:(c2 + 1) * 128], c16[:], True, True, [lz, c16], [pDc], last=(c2 == 1))
                ACT(B["Erev"][:], pB[:, 0:256], AF.Exp, [pB], [B["Erev"]])
                ACT(B["Ein"][:], v3(pB[:, 256:512], 2), AF.Exp, [pB], [B["Ein"]])
                ACT(B["Enin"][:], v3(pB[:, 256:512], 2), AF.Exp, [pB], [B["Enin"]], scale=-1.0)
                ACT(B["decs"][:], pDc[:, 0:2], AF.Exp, [pDc], [B["decs"]])
                TT("dve", B["kdec"][:], B["gk"][:], B["Erev"][:], ALU.mult, [B["gk"], B["Erev"]], [B["kdec"]])
                STT("dve", B["QsT"][:], B["qkT"][:, 0:2, :], 0.125, B["Ein"][:], ALU.mult, ALU.mult, [B["qkT"], B["Ein"]], [B["QsT"]])
                for hh in range(2):
                    ps_ = slice(hh * 64, hh * 64 + 64)
                    TT("pool", B["KsTz"][ps_, :, hh, :], B["qkT"][ps_, 2:4, :], B["Enin"][ps_], ALU.mult, [B["qkT"], B["Enin"]], [B["KsTz"]])

            def G5(n, B):
                pA = bank()
                for h in range(4):
                    MM(v3(pA[:], 4)[:, h, :], B["KsTz"][:, h // 2, h % 2, :], B["QsT"][:, h // 2, :], True, True, [B["KsTz"], B["QsT"]], [pA], last=(h == 3))
                TT("dve", B["attT"][:], v3(pA[:], 4), mI4[:], ALU.mult, [pA, mI4], [B["attT"]])

            def G6(n, B):
                pOo = bank(True)
                for h in range(4):
                    oo = v3(pOo[:], 4)[:, h, :]
                    MM(oo, B["attT"][:, h, :], B["gv"][:, h * 128:(h + 1) * 128], True, False, [B["attT"], B["gv"]], [pOo])
                    MM(oo, B["QsT"][:, h // 2, :], Sbf[:, h, :], False, True, [B["QsT"], Sbf], [pOo], last=(h == 3))
                pKV = bank()
                for h in range(4):
                    c2 = h // 2
                    MM(v3(pKV[:], 4)[:, h, :], B["kdec"][:, c2 * 128:(c2 + 1) * 128], B["gv"][:, h * 128:(h + 1) * 128], True, True, [B["kdec"], B["gv"]], [pKV], last=(h == 3))
                for h in range(4):
                    c2, p0 = h // 2, (h % 2) * 64
                    STT("dve", Sst[p0:p0 + 64, h, :], Sst[p0:p0 + 64, h, :], B["decs"][p0:p0 + 64, c2:c2 + 1],
                        v3(pKV[:], 4)[p0:p0 + 64, h, :], ALU.mult, ALU.add, [Sst, B["decs"], pKV], [Sst])
                CP("pool", Sbf[:], Sst[:], [Sst], [Sbf])
                B["pOo"] = pOo

            def G7(n, B):
                t0 = n * 128
                pOo, osb, osq, gsm = B["pOo"], B["osb"], B["osq"], B["gsm"]
                ACT(osb[:], pOo[:], AF.Copy, [pOo], [osb])
                unpin(pOo)
                TT("pool", osq[:], osb[:], osb[:], ALU.mult, [osb], [osq])
                k.op("dve", lambda e: e.tensor_reduce(gsm[:, 0, :], v3(osq[:], 4), AX.X, ALU.add), [osq], [gsm])
                TS("dve", gsm[:, 1, :], gsm[:, 0, :], 1.0 / 128, 1e-5, ALU.mult, ALU.add, [gsm], [gsm])
                k.op("dve", lambda e: e.reciprocal(gsm[:, 1, :], gsm[:, 1, :]), [gsm], [gsm])
                ACT(gsm[:, 1, :], gsm[:, 1, :], AF.Sqrt, [gsm], [gsm])
                TT("dve", v3(osb[:], 4), v3(osb[:], 4), gsm[:, 1, :].unsqueeze(2).to_broadcast([128, 4, 128]), ALU.mult, [osb, gsm], [osb])
                TT("pool", v3(osb[:], 4), v3(osb[:], 4), ngr[:], ALU.mult, [osb, ngr], [osb])
                TT("pool", B["of"][:], osb[:], B["sgg"][:], ALU.mult, [osb, B["sgg"]], [B["of"]])
                pO = bank()
                pOb = pO[:].bitcast(BF16)[:, 0:512].rearrange("p (c t) -> p c t", c=4)
                for c4 in range(4):
                    TR(pOb[:, c4, :], B["of"][:, c4 * 128:(c4 + 1) * 128], identb[:], [B["of"], identb], [pO], last=(c4 == 3))
                ACT(oglaT_ap[:, :, t0:t0 + 128], pOb, AF.Copy, [pO], [oglaTb])

            nch_run = NCH if "gla_short" not in taps else 2
            for n in range(0, nch_run, 2):
                for fn in (G1, G2, G3, G4, G5, G6, G7):
                    fn(n, GS[0])
                    fn(n + 1, GS[1])
            if "gla_short" in taps:
                tap("oglaT", oglaT_ap[:, :, 0:256], [128, 4, 256], [oglaTb])
            else:
                tap("oglaT", oglaT_ap, [128, 4, T], [oglaTb])
            k.barrier()
        if stop_after <= 3:
            k.finish()
            return nc, tap_d

        with ExitStack() as st3:
            mT = sb(st3, "mT", [128, 8, T], BF16)
            mTb = [Buf(f"mT{d}", None) for d in range(8)]
            with ExitStack() as st:
                Wgt = [sb(st, f"Wgt{c}", [128, 2048], BF16) for c in range(8)]
                Wbr = sb(st, "Wbr", [128, 4, D], BF16)
                Wbg = sb(st, "Wbg", [128, 4, D], BF16)
                for c in range(8):
                    k.dma("pool", Wgt[c][:], win_v[:, c, RW + GL:NIN], writes=[Wgt[c]])
                k.dma("pool", Wbr[:], wbr_d.rearrange("(c p) n -> p c n", p=128), writes=[Wbr])
                k.dma("pool", Wbg[:], wbg_d.rearrange("(c p) n -> p c n", p=128), writes=[Wbg])
                sr = [sb(st, f"m_sr{i}", [128, 512], F32) for i in range(2)]
                sg_ = [sb(st, f"m_sg{i}", [128, 512], F32) for i in range(2)]
                it = 0
                for sbk in range(4):
                    tok = slice(sbk * 512, (sbk + 1) * 512)
                    for dt in range(8):
                        dsl = slice(dt * 128, (dt + 1) * 128)
                        p1, p2, p3, p4 = bank(), bank(), bank(), bank()
                        for c in range(8):
                            MM(p1[:], Wgt[c][:, dsl], uT_ap[:, c, sbk * 512 + 1:sbk * 512 + 513], c == 0, c == 7, [Wgt[c], uTb[c]], [p1])
                        for c in range(8):
                            MM(p2[:], Wgt[c][:, 1024 + dt * 128:1024 + (dt + 1) * 128], uT_ap[:, c, sbk * 512 + 1:sbk * 512 + 513],
                               c == 0, c == 7, [Wgt[c], uTb[c]], [p2])
                        for c in range(4):
                            MM(p3[:], Wbr[:, c, dsl], orwT_ap[:, c, tok], c == 0, c == 3, [Wbr] + orwTH, [p3])
                        for c in range(4):
                            MM(p4[:], Wbg[:, c, dsl], oglaT_ap[:, c, tok], c == 0, c == 3, [Wbg, oglaTb], [p4])
                        a_, b_ = sr[it % 2], sg_[it % 2]
                        it += 1
                        ACT(a_[:], p1[:], AF.Sigmoid, [p1], [a_])
                        ACT(b_[:], p2[:], AF.Sigmoid, [p2], [b_])
                        TT("dve", a_[:], a_[:], p3[:], ALU.mult, [a_, p3], [a_])
                        TT("dve", b_[:], b_[:], p4[:], ALU.mult, [b_, p4], [b_])
                        TT("pool", mT[:, dt, tok], a_[:], b_[:], ALU.add, [a_, b_], [mTb[dt]])
                tap("mT", mT[:], [128, 8, T], mTb)
                k.barrier()
            if stop_after <= 4:
                k.finish()
                return nc, tap_d

            with ExitStack() as st:
                Wmx = sb(st, "Wmx", [128, 8, D], BF16)
                k.dma("pool", Wmx[:], wmix_d.rearrange("(c p) n -> p c n", p=128), writes=[Wmx])
                g1row = sb(st, "g1row", [128, D], F32)
                lnr = sb(st, "ln1r", [128, 2, D], F32)
                k.dma("sp", lnr[:, 0, :], _bc_rows(ln1g_d, 128, D), writes=[lnr])
                k.dma("sp", lnr[:, 1, :], _bc_rows(ln1b_d, 128, D), writes=[lnr])
                screp = sb(st, "screp", [128, 8, 128], BF16)
                CP("dve", screp[:], scb[:].unsqueeze(2).to_broadcast([128, 8, 128]), [scb], [screp])
                brow1 = sb(st, "brow1", [1, D], F32)
                k.dma("sp", brow1[:], brow_d[:, 2 * D:3 * D], writes=[brow1])
                with ExitStack() as stw:
                    wg1 = sb(stw, "wg1", [128, 8, D], BF16)
                    k.dma("pool", wg1[:], wada_d.rearrange("(c p) n -> p c n", p=128)[:, :, 2 * D:3 * D], writes=[wg1])
                    for nh in range(2):
                        pg_ = bank()
                        for c in range(8):
                            MM(pg_[:], screp[:, c, :], wg1[:, c, nh * 512:(nh + 1) * 512], c == 0, False, [screp, wg1], [pg_])
                        MM(pg_[:], ones1[:], brow1[:, nh * 512:(nh + 1) * 512], False, True, [ones1, brow1], [pg_])
                        ACT(g1row[:, nh * 512:(nh + 1) * 512], pg_[:], AF.Copy, [pg_], [g1row])
                    k.barrier()
                xin = [sb(st, f"xin{i}", [128, D], F32) for i in range(2)]
                ybuf = [sb(st, f"ybuf{i}", [128, D], F32) for i in range(2)]
                stat = [sb(st, f"l1stat{i}", [128, 2, 6], F32) for i in range(2)]
                mv = [sb(st, f"l1mv{i}", [128, 4], F32) for i in range(2)]
                pms = {}

                def M0(tb):
                    xi = xin[tb % 2]
                    k.dma("sp", xi[:], x_d[tb * 128:(tb + 1) * 128, :], writes=[xi])
                    pm_ = [bank(True), bank(True)]
                    for nh in range(2):
                        for dt in range(8):
                            MM(pm_[nh][:], mT[:, dt, tb * 128:(tb + 1) * 128], Wmx[:, dt, nh * 512:(nh + 1) * 512], dt == 0, dt == 7, [mTb[dt], Wmx], [pm_[nh]])
                    pms[tb] = pm_

                def M1(tb):
                    yb, pm_ = ybuf[tb % 2], pms[tb]
                    for nh in range(2):
                        hs = slice(nh * 512, (nh + 1) * 512)
                        TT("dve", yb[:, hs], pm_[nh][:], g1row[:, hs], ALU.mult, [pm_[nh], g1row], [yb])
                    unpin(*pm_)
                    STT("dve", yb[:], xin[tb % 2][:], ALPHA, yb[:], ALU.mult, ALU.add, [xin[tb % 2], yb], [yb])

                def M2(tb):
                    yb, st_, mv_ = ybuf[tb % 2], stat[tb % 2], mv[tb % 2]
                    for nh in range(2):
                        k.op("dve", lambda e, nh=nh: e.bn_stats(st_[:, nh, :], yb[:, nh * 512:(nh + 1) * 512]), [yb], [st_])
                    k.op("dve", lambda e: e.bn_aggr(mv_[:, 0:2], st_[:]), [st_], [mv_])
                    TS("dve", mv_[:, 2:3], mv_[:, 1:2], 1e-5, None, ALU.add, None, [mv_], [mv_])
                    k.op("dve", lambda e: e.reciprocal(mv_[:, 2:3], mv_[:, 2:3]), [mv_], [mv_])
                    ACT(mv_[:, 2:3], mv_[:, 2:3], AF.Sqrt, [mv_], [mv_])

                def M3(tb):
                    yb, mv_ = ybuf[tb % 2], mv[tb % 2]
                    TS("dve", yb[:], yb[:], mv_[:, 0:1], mv_[:, 2:3], ALU.subtract, ALU.mult, [yb, mv_], [yb])
                    TT("pool", yb[:], yb[:], lnr[:, 0, :], ALU.mult, [yb, lnr], [yb])
                    TT("pool", x1_ap[:, tb, :], yb[:], lnr[:, 1, :], ALU.add, [yb, lnr], [x1b[tb]])

                for tb0 in range(0, 16, 2):
                    for fn in (M0, M1, M2, M3):
                        fn(tb0)
                        fn(tb0 + 1)
                tap("x1", x1_ap, [128, 16, D], x1b)
                k.barrier()
        if stop_after <= 5:
            k.finish()
            return nc, tap_d

        with ExitStack() as st:
            Wo = sb(st, "Wo", [128, 22, D], BF16)
            Wob = [Buf(f"Wo{f}", None) for f in range(22)]
            wfo_v = wfo_d.rearrange("(f p) n -> p f n", p=128)
            lnr = sb(st, "ln2r", [128, 2, D], F32)
            k.dma("sp", lnr[:, 0, :], _bc_rows(ln2g_d, 128, D), writes=[lnr])
            k.dma("sp", lnr[:, 1, :], _bc_rows(ln2b_d, 128, D), writes=[lnr])
            g2row = sb(st, "g2row", [128, D], BF16)
            screp = sb(st, "screp2", [128, 8, 128], BF16)
            CP("dve", screp[:], scb[:].unsqueeze(2).to_broadcast([128, 8, 128]), [scb], [screp])
            u2T = sb(st, "u2T", [128, 8, 1024], BF16)
            actT = sb(st, "actT", [128, 22, 1024], BF16)
            actb = [Buf(f"act{f}", None) for f in range(22)]
            Wgu = [sb(st, f"Wgu{i}", [128, 8, 2, 128], BF16) for i in range(3)]
            sgf = [sb(st, f"f_sg{i}", [128, 512], F32) for i in range(2)]
            ybuf = [sb(st, f"f_y{i}", [128, D], F32) for i in range(2)]
            stat = sb(st, "l2stat", [128, 2, 6], F32)
            mv = sb(st, "l2mv", [128, 4], F32)
            brow2 = ybuf[0]
            k.dma("sp", brow2[0:1, :], brow_d[:, 5 * D:6 * D], writes=[brow2])
            wg2 = actT[:, 14:22, :]
            wg2b = actb[14:22]

            def side_work(f):
                if f == 0:
                    k.dma("pool", wg2, wada_d.rearrange("(c p) n -> p c n", p=128)[:, :, 5 * D:6 * D], writes=wg2b)
                if f == 3:
                    for nh in range(2):
                        pg_ = bank()
                        for c in range(8):
                            MM(pg_[:], screp[:, c, :], wg2[:, c, nh * 512:(nh + 1) * 512], c == 0, False, [screp] + wg2b, [pg_])
                        MM(pg_[:], ones1[:], brow2[0:1, nh * 512:(nh + 1) * 512], False, True, [ones1, brow2], [pg_])
                        ACT(g2row[:, nh * 512:(nh + 1) * 512], pg_[:], AF.Copy, [pg_], [g2row])
                if 2 <= f < 13:
                    f0 = 2 * (f - 2)
                    k.dma("pool", Wo[:, f0:f0 + 2, :], wfo_v[:, f0:f0 + 2, :], writes=Wob[f0:f0 + 2])
                if 6 <= f < 17:
                    f0 = 2 * (f - 6)
                    TT("dve", Wo[:, f0, :], Wo[:, f0, :], g2row[:], ALU.mult, [Wob[f0], g2row], [Wob[f0]])
                    TT("pool", Wo[:, f0 + 1, :], Wo[:, f0 + 1, :], g2row[:], ALU.mult, [Wob[f0 + 1], g2row], [Wob[f0 + 1]])
            wfi_v = wfi_d.rearrange("(c p) (g n) -> p c g n", p=128, g=2)
            wi = 0
            it = 0
            for sbk in range(2):
                for c in range(8):
                    for jb in range(2):
                        pt_ = bank()
                        for j in range(4):
                            tb = sbk * 8 + jb * 4 + j
                            TR(pt_[:, j * 128:(j + 1) * 128], x1_ap[:, tb, c * 128:(c + 1) * 128], identf[:], [x1b[tb], identf], [pt_], last=(j == 3))
                        ACT(u2T[:, c, jb * 512:(jb + 1) * 512], pt_[:], AF.Identity, [pt_, modp], [u2T],
                            bias=modp[:, 16 + c:17 + c], scale=modp[:, 24 + c:25 + c])
                for f in range(22):
                    w_ = Wgu[wi % 3]
                    wi += 1
                    k.dma("pool", w_[:, :, 0, :], wfi_v[:, :, 0, f * 128:(f + 1) * 128], writes=[w_])
                    k.dma("pool", w_[:, :, 1, :], wfi_v[:, :, 1, f * 128:(f + 1) * 128], writes=[w_])
                    if sbk == 0:
                        side_work(f)
                    for hh in range(2):
                        ph, pu = bank(), bank()
                        for c in range(8):
                            MM(ph[:], w_[:, c, 0, :], u2T[:, c, hh * 512:(hh + 1) * 512], c == 0, c == 7, [w_, u2T], [ph])
                        for c in range(8):
                            MM(pu[:], w_[:, c, 1, :], u2T[:, c, hh * 512:(hh + 1) * 512], c == 0, c == 7, [w_, u2T], [pu])
                        s_ = sgf[it % 2]
                        it += 1
                        ACT(s_[:], ph[:], AF.Silu, [ph], [s_])
                        TT("dve", actT[:, f, hh * 512:(hh + 1) * 512], s_[:], pu[:], ALU.mult, [s_, pu], [actb[f]])
                for j in range(8):
                    tb = sbk * 8 + j
                    yb = ybuf[tb % 2]
                    po = [bank(), bank()]
                    for nh in range(2):
                        for f in range(22):
                            MM(po[nh][:], actT[:, f, j * 128:(j + 1) * 128], Wo[:, f, nh * 512:(nh + 1) * 512], f == 0, f == 21, [actb[f], Wob[f]], [po[nh]])
                    for nh in range(2):
                        hs = slice(nh * 512, (nh + 1) * 512)
                        STT("dve", yb[:, hs], x1_ap[:, tb, hs], ALPHA, po[nh][:], ALU.mult, ALU.add, [x1b[tb], po[nh]], [yb])
                    for nh in range(2):
                        k.op("dve", lambda e, nh=nh, yb=yb: e.bn_stats(stat[:, nh, :], yb[:, nh * 512:(nh + 1) * 512]), [yb], [stat])
                    k.op("dve", lambda e: e.bn_aggr(mv[:, 0:2], stat[:]), [stat], [mv])
                    TS("dve", mv[:, 2:3], mv[:, 1:2], 1e-5, None, ALU.add, None, [mv], [mv])
                    k.op("dve", lambda e: e.reciprocal(mv[:, 2:3], mv[:, 2:3]), [mv], [mv])
                    ACT(mv[:, 2:3], mv[:, 2:3], AF.Sqrt, [mv], [mv])
                    TS("dve", yb[:], yb[:], mv[:, 0:1], mv[:, 2:3], ALU.subtract, ALU.mult, [yb, mv], [yb])
                    TT("pool", yb[:], yb[:], lnr[:, 0, :], ALU.mult, [yb, lnr], [yb])
                    TT("pool", yb[:], yb[:], lnr[:, 1, :], ALU.add, [yb, lnr], [yb])
                    k.dma("sp", y_d[tb * 128:(tb + 1) * 128, :], yb[:], reads=[yb], is_output=True)
        k.finish()
    return nc, tap_d


def _host_inputs(inputs):
    f = lambda a: np.ascontiguousarray(a, dtype=np.float32)
    sh = {}
    b = inputs["b_ada"][0]
    sh["w_ada"] = f(inputs["w_ada"][0])
    sh["b_pp"] = f(b.reshape(6, 8, 128)[[0, 1, 3, 4]].transpose(2, 0, 1).reshape(128, 32))
    sh["b_row"] = f(b.reshape(1, -1))
    sh["w_in"] = f(inputs["w_in"][0])
    sh["mu"] = f(inputs["mu_rw"][0].reshape(1, -1))
    for nm in ("rw_w0", "rw_a0", "rw_k_k", "rw_k_a", "rw_r_k", "rw_gn_g", "rw_gn_b", "gla_a_b", "gla_norm_g",
               "ln1_g", "ln1_b", "ln2_g", "ln2_b"):
        sh[nm] = f(inputs[nm][0].reshape(1, -1))
    for nm in ("rw_w2", "rw_a2", "rw_g2", "gla_a2", "w_rw_branch", "w_gla_branch", "w_mix_out", "w_ffn_in", "w_ffn_out"):
        sh[nm] = f(inputs[nm][0])
    sh["c_ident"] = np.eye(128, dtype=np.float32)
    s = np.arange(128)[:, None]
    t = np.arange(128)[None, :]
    bd = (s // 64) == (t // 64)
    sh["c_tri"] = f(np.stack([(s < t), (s <= t), (s > t), (s < t) & bd, (s > t) & bd, (s >= 64) & (t < 64)], axis=1).astype(np.float32))
    maps = []
    x = inputs["x"]
    c = inputs["c"]
    for bi in range(x.shape[0]):
        m = dict(sh)
        m["xT"] = f(x[bi].T)
        m["x"] = f(x[bi])
        m["cpp"] = f(c[bi].reshape(8, 128).T)
        maps.append(m)
    return maps


def kernel(**inputs):
    maps = _host_inputs(inputs)
    nc, _ = build_nc()
    res = run_bass_kernel_spmd(nc, maps, core_ids=list(range(len(maps))))
    return np.stack([np.asarray(r["y"], dtype=np.float32) for r in res.results], axis=0)
```

```python
import math
from contextlib import ExitStack

import numpy as np
import concourse.bass as bass
import concourse.mybir as mybir
from concourse.bass_utils import run_bass_kernel_spmd

F32 = mybir.dt.float32
BF16 = mybir.dt.bfloat16
AF = mybir.ActivationFunctionType
ALU = mybir.AluOpType
AX = mybir.AxisListType

D = 1024
T = 2048
NCH = T // 128
RW = 1792
GL = 1552
NIN = 5392
DFF = 2816
ALPHA = 2.0 ** 0.25
EM05 = math.exp(-0.5)


class Buf:
    __slots__ = ("name", "ap", "writer", "readers")

    def __init__(self, name, ap):
        self.name = name
        self.ap = ap
        self.writer = None
        self.readers = {}

    def __getitem__(self, key):
        return self.ap[key]


class KB:
    def __init__(self, nc, stack, n_dma_sems=8):
        self.nc = nc
        self.engs = {"pe": nc.tensor, "act": nc.scalar, "dve": nc.vector, "pool": nc.gpsimd, "sp": nc.sync}
        self.sem, self.cnt, self.waited = {}, {}, {}
        for e in self.engs:
            self.sem[e] = stack.enter_context(nc.semaphore("s_" + e))
            self.cnt[e] = 0
            self.waited[e] = {}
        self.dma_sems, self.dma_val, self.dma_rr = {}, {}, {}
        for q in ("sp", "act", "pool"):
            self.dma_sems[q] = [stack.enter_context(nc.semaphore(f"d_{q}{i}")) for i in range(n_dma_sems)]
            self.dma_val[q] = [0] * n_dma_sems
            self.dma_rr[q] = 0
        self.out_events = []
        self.pending = {}

    def _wait(self, eng, ev):
        sem, val, _ = ev
        if self.waited[eng].get(sem.name, 0) >= val:
            return
        self.engs[eng].wait_ge(sem, val)
        self.waited[eng][sem.name] = val

    def _collect(self, eng, reads, writes):
        evs = {}

        def add(ev, kind):
            if ev is None:
                return
            sem, val, src = ev
            if src == eng and (eng == "pe" or (kind == "war" and eng != "pool")):
                return
            if sem.name not in evs or evs[sem.name][1] < val:
                evs[sem.name] = ev
        for b in reads:
            add(b.writer, "raw")
        for b in writes:
            add(b.writer, "waw")
            for ev in b.readers.values():
                add(ev, "war")
        return evs

    def _record(self, ev, reads, writes):
        for b in reads:
            b.readers[ev[0].name] = ev
        for b in writes:
            b.writer = ev
            b.readers = {}

    def op(self, eng, fn, reads=(), writes=(), inc=True):
        for ev in self._collect(eng, reads, writes).values():
            self._wait(eng, ev)
        ins = fn(self.engs[eng])
        pend = self.pending.setdefault(eng, [])
        if not inc:
            pend.append((tuple(reads), tuple(writes)))
            return
        self.cnt[eng] += 1
        ins.then_inc(self.sem[eng], 1)
        ev = (self.sem[eng], self.cnt[eng], eng)
        for r_, w_ in pend:
            self._record(ev, r_, w_)
        pend.clear()
        self._record(ev, reads, writes)

    def dma(self, q, out, in_, reads=(), writes=(), is_output=False):
        for ev in self._collect(q, reads, writes).values():
            self._wait(q, ev)
        i = self.dma_rr[q]
        self.dma_rr[q] = (i + 1) % len(self.dma_sems[q])
        sem = self.dma_sems[q][i]
        prev = self.dma_val[q][i]
        if prev > 0:
            self._wait(q, (sem, prev, "dma"))
        ins = self.engs[q].dma_start(out=out, in_=in_)
        ins.then_inc(sem, 16)
        self.dma_val[q][i] = prev + 16
        ev = (sem, prev + 16, "dma")
        self._record(ev, reads, writes)
        if is_output:
            self.out_events.append(ev)

    def barrier(self):
        assert not any(self.pending.values()), "pending non-incrementing ops at barrier"
        evs = [(self.sem[e], self.cnt[e], e) for e in self.engs if self.cnt[e] > 0]
        for q in self.dma_sems:
            for s, v in zip(self.dma_sems[q], self.dma_val[q]):
                if v > 0:
                    evs.append((s, v, "dma"))
        for e in self.engs:
            for ev in evs:
                if ev[2] != e or e != "pe":
                    self._wait(e, ev)

    def finish(self):
        for ev in self.out_events:
            self._wait("sp", ev)
        self.final_counts = dict(self.cnt)
        KB.last = self


def _bc_rows(ap, nparts, n):
    return bass.AP(ap.tensor, ap.offset, [[0, nparts], [1, n]])


def build_nc(stop_after=99, taps=()):
    nc = bass.Bass("TRN2", target_bir_lowering=False)
    din = {}

    def inp(name, shape):
        din[name] = nc.dram_tensor(name, list(shape), F32, kind="ExternalInput").ap()
        return din[name]

    xT_d = inp("xT", [D, T])
    x_d = inp("x", [T, D])
    cpp_d = inp("cpp", [128, 8])
    wada_d = inp("w_ada", [D, 6 * D])
    bpp_d = inp("b_pp", [128, 32])
    brow_d = inp("b_row", [1, 6 * D])
    win_d = inp("w_in", [D, NIN])
    mu_d = inp("mu", [1, RW])
    w0_d = inp("rw_w0", [1, 512])
    a0_d = inp("rw_a0", [1, 512])
    w2_d = inp("rw_w2", [64, 512])
    a2_d = inp("rw_a2", [64, 512])
    g2_d = inp("rw_g2", [128, 512])
    kk_d = inp("rw_k_k", [1, 512])
    ka_d = inp("rw_k_a", [1, 512])
    rk_d = inp("rw_r_k", [1, 512])
    gng_d = inp("rw_gn_g", [1, 512])
    gnb_d = inp("rw_gn_b", [1, 512])
    ga2_d = inp("gla_a2", [16, 256])
    gab_d = inp("gla_a_b", [1, 256])
    gng2_d = inp("gla_norm_g", [1, 128])
    wbr_d = inp("w_rw_branch", [512, D])
    wbg_d = inp("w_gla_branch", [512, D])
    wmix_d = inp("w_mix_out", [D, D])
    ln1g_d = inp("ln1_g", [1, D])
    ln1b_d = inp("ln1_b", [1, D])
    wfi_d = inp("w_ffn_in", [D, 2 * DFF])
    wfo_d = inp("w_ffn_out", [DFF, D])
    ln2g_d = inp("ln2_g", [1, D])
    ln2b_d = inp("ln2_b", [1, D])
    cident_d = inp("c_ident", [128, 128])
    ctri_d = inp("c_tri", [128, 6, 128])
    y_d = nc.dram_tensor("y", [T, D], F32, kind="ExternalOutput").ap()
    tap_d = {}

    with ExitStack() as st0:
        k = KB(nc, st0)

        def sb(stack, name, shape, dt):
            t = stack.enter_context(nc.sbuf_tensor("sb_" + name, list(shape), dt))
            return Buf(name, t[:])

        def MM(out, lhsT, rhs, st, sp, R, W, last=None):
            k.op("pe", lambda e: e.matmul(out, lhsT, rhs, start=st, stop=sp), R, W, inc=(sp if last is None else last))

        def TR(out, in_, idn, R, W, last=True):
            k.op("pe", lambda e: e.transpose(out, in_, idn), R, W, inc=last)

        def ACT(out, in_, fn, R, W, bias=None, scale=None):
            kw = {}
            if bias is not None:
                kw["bias"] = bias
            if scale is not None:
                kw["scale"] = scale
            k.op("act", lambda e: e.activation(out, in_, fn, **kw), R, W)

        def TT(eng, out, a, b, op, R, W):
            if "nopool" in taps and eng == "pool":
                eng = "dve"
            k.op(eng, lambda e: e.tensor_tensor(out, a, b, op), R, W)

        def STT(eng, out, a, s, b, op0, op1, R, W):
            k.op(eng, lambda e: e.scalar_tensor_tensor(out, a, s, b, op0, op1), R, W)

        def TS(eng, out, a, s1, s2, op0, op1, R, W):
            if op1 is None:
                k.op(eng, lambda e: e.tensor_scalar(out, a, s1, None, op0), R, W)
            else:
                k.op(eng, lambda e: e.tensor_scalar(out, a, s1, s2, op0, op1), R, W)

        def CP(eng, out, in_, R, W):
            if "nopool" in taps and eng == "pool":
                eng = "dve"
            if eng == "act":
                ACT(out, in_, AF.Copy, R, W)
            else:
                k.op(eng, lambda e: e.tensor_copy(out, in_), R, W)

        def tap(name, ap, shape, reads):
            if name not in taps:
                return
            tap_d[name] = nc.dram_tensor("tap_" + name, list(shape), F32, kind="ExternalOutput").ap()
            k.dma("pool", tap_d[name], ap, reads=reads, is_output=True)

        banks = []
        for i in range(8):
            t = st0.enter_context(nc.psum_tensor(f"pb{i}", [128, 512], F32))
            banks.append(Buf(f"pb{i}", t[:]))
        bank_rr = [0]

        pinned = set()

        def bank(pin=False):
            for _ in range(8):
                b = banks[bank_rr[0]]
                bank_rr[0] = (bank_rr[0] + 1) % 8
                if b.name not in pinned:
                    if pin:
                        pinned.add(b.name)
                    return b
            raise RuntimeError("all PSUM banks pinned")

        def unpin(*bs):
            for b in bs:
                pinned.discard(b.name)

        big = sb(st0, "big", [128, 16400], F32)
        bigb = big.ap.bitcast(BF16)
        uT_ap = bigb[:, 0:16416].rearrange("p (c t) -> p c t", c=8)
        orwT_ap = bigb[:, 16416:24608].rearrange("p (c t) -> p c t", c=4)
        oglaT_ap = bigb[:, 24608:32800].rearrange("p (c t) -> p c t", c=4)
        x1_ap = big.ap[:, 0:16384].rearrange("p (b d) -> p b d", b=16)
        uTb = [Buf(f"uT{c}", None) for c in range(8)]
        orwTb = Buf("orwT", None)
        oglaTb = Buf("oglaT", None)
        x1b = [Buf(f"x1_{b}", None) for b in range(16)]

        identf = sb(st0, "identf", [128, 128], F32)
        identb = sb(st0, "identb", [128, 128], BF16)
        modp = sb(st0, "modp", [128, 32], F32)
        ones1 = sb(st0, "ones1", [1, 128], F32)
        scb = sb(st0, "scb", [128, 8], BF16)
        k.dma("sp", identf[:], cident_d, writes=[identf])
        k.dma("pool", identb[:], cident_d, writes=[identb])
        k.op("dve", lambda e: e.memset(ones1[:], 1.0), writes=[ones1])

        win_v = win_d.rearrange("(c p) n -> p c n", p=128)

        with ExitStack() as st:
            Nb = [[sb(st, f"N{h}{i}", [128, 4, 128], BF16) for i in range(2)] for h in range(2)]
            Lb = [[sb(st, f"L{h}{i}", [128, 4, 128], BF16) for i in range(2)] for h in range(2)]
            Sm = [sb(st, f"Sm{h}", [128, 4, 128], BF16) for h in range(2)]
            W1 = [sb(st, f"W1_{c}", [128, RW], BF16) for c in range(8)]
            W2 = [sb(st, f"W2_{c}", [128, RW], BF16) for c in range(8)]
            with ExitStack() as stp:
                cpp = sb(stp, "cpp", [128, 8], F32)
                bpp = sb(stp, "bpp", [128, 32], F32)
                wa = [sb(stp, f"wa{i}", [128, 8, 1024], BF16) for i in range(2)]
                mur = sb(stp, "mur", [128, RW], F32)
                omr = sb(stp, "omr", [128, RW], F32)
                stg = [sb(stp, f"stg{i}", [128, T], F32) for i in range(2)]
                k.dma("sp", cpp[:], cpp_d, writes=[cpp])
                k.dma("sp", bpp[:], bpp_d, writes=[bpp])
                ACT(scb[:], cpp[:], AF.Silu, [cpp], [scb])
                wada_v = wada_d.rearrange("(c p) n -> p c n", p=128)
                parts = (0, 1, 3, 4)
                for pi in range(2):
                    k.dma("pool", wa[pi][:], wada_v[:, :, parts[pi] * 1024:(parts[pi] + 1) * 1024], writes=[wa[pi]])
                k.dma("sp", mur[:], _bc_rows(mu_d, 128, RW), writes=[mur])
                TS("dve", omr[:], mur[:], -1.0, 1.0, ALU.mult, ALU.add, [mur], [omr])
                pm = bank(True)

                def p0_mm(pi):
                    w = wa[pi % 2]
                    for m in range(8):
                        col = pi * 8 + m
                        for c in range(8):
                            MM(pm[:, col:col + 1], w[:, c, m * 128:(m + 1) * 128], scb[:, c:c + 1], c == 0, c == 7, [w, scb], [pm], last=(m == 7 and c == 7))

                def prep(c):
                    w_ = stg[c % 2]
                    k.dma("sp", w_[:, 0:RW], win_v[:, c, 0:RW], writes=[w_])
                    TT("dve", W1[c][:], w_[:, 0:RW], omr[:], ALU.mult, [w_, omr], [W1[c]])
                    TT("pool", W2[c][:], w_[:, 0:RW], mur[:], ALU.mult, [w_, mur], [W2[c]])
                for c in range(4):
                    prep(c)
                p0_mm(0)
                p0_mm(1)
                for pi in range(2, 4):
                    k.dma("pool", wa[pi % 2][:], wada_v[:, :, parts[pi] * 1024:(parts[pi] + 1) * 1024], writes=[wa[pi % 2]])
                for c in range(4, 8):
                    prep(c)
                p0_mm(2)
                p0_mm(3)
                TT("dve", modp[:], pm[:, 0:32], bpp[:], ALU.add, [pm, bpp], [modp])
                unpin(pm)
                TS("dve", modp[:, 8:16], modp[:, 8:16], 1.0, None, ALU.add, None, [modp], [modp])
                TS("dve", modp[:, 24:32], modp[:, 24:32], 1.0, None, ALU.add, None, [modp], [modp])
                tap("modp", modp[:], [128, 32], [modp])
                for c in range(8):
                    s_ = stg[c % 2]
                    k.dma("sp", s_[:], xT_d[c * 128:(c + 1) * 128, :], writes=[s_])
                    k.op("dve", lambda e, c=c: e.memset(uT_ap[:, c, 0:1], 0.0), writes=[uTb[c]])
                    ACT(uT_ap[:, c, 1:T + 1], s_[:], AF.Identity, [s_, modp], [uTb[c]],
                        bias=modp[:, c:c + 1], scale=modp[:, 8 + c:9 + c])
                tap("uT", uT_ap[:, :, 1:T + 1], [128, 8, T], uTb)
                k.barrier()

            rows = sb(st, "rwrows", [128, 5, 512], F32)
            w0r = sb(st, "w0r", [1, 512], F32)
            a0r = sb(st, "a0r", [1, 512], F32)
            w2b = sb(st, "w2b", [128, 512], BF16)
            a2b = sb(st, "a2b", [128, 512], BF16)
            g2b = sb(st, "g2b", [128, 512], BF16)
            Mtri = sb(st, "Mtri", [128, 3, 128], F32)
            negcol = sb(st, "negcol", [128, 1], F32)
            mSI = sb(st, "mSI", [128, 2, 2, 128], F32)
            mND = sb(st, "mND", [128, 2, 128], F32)
            mSL4 = sb(st, "mSL4", [128, 2, 4, 128], F32)
            idb4 = sb(st, "idb4", [128, 4, 128], BF16)
            id8f = sb(st, "id8f", [64, 8, 64], F32)
            Hb = [sb(st, f"Hb{i}", [128, 8, 64], BF16) for i in range(2)]
            for i, d_ in enumerate((kk_d, ka_d, rk_d, gng_d, gnb_d)):
                k.dma("sp", rows[:, i, :], _bc_rows(d_, 128, 512), writes=[rows])
            k.dma("sp", w0r[:], w0_d, writes=[w0r])
            k.dma("sp", a0r[:], a0_d, writes=[a0r])
            k.op("dve", lambda e: e.memset(w2b[:], 0.0), writes=[w2b])
            k.op("dve", lambda e: e.memset(a2b[:], 0.0), writes=[a2b])
            k.dma("pool", w2b[0:64, :], w2_d, writes=[w2b])
            k.dma("pool", a2b[64:128, :], a2_d, writes=[a2b])
            k.dma("pool", g2b[:], g2_d, writes=[g2b])
            k.dma("sp", Mtri[:], ctri_d[:, 0:3, :], writes=[Mtri])
            TS("dve", Mtri[:], Mtri[:], -EM05, None, ALU.mult, None, [Mtri], [Mtri])
            k.op("dve", lambda e: e.memset(negcol[:], -EM05), writes=[negcol])
            for h2 in range(2):
                k.dma("sp", mSI[:, h2, 0, :], ctri_d[:, 0, :], writes=[mSI])
                k.dma("sp", mND[:, h2, :], ctri_d[:, 3, :], writes=[mND])
                k.dma("sp", mSI[:, h2, 1, :], ctri_d[:, 1, :], writes=[mSI])
            for h4 in range(4):
                k.dma("sp", mSL4[:, 0, h4, :], ctri_d[:, 4, :], writes=[mSL4])
                k.dma("sp", mSL4[:, 1, h4, :], ctri_d[:, 5, :], writes=[mSL4])
                k.dma("pool", idb4[:, h4, :], cident_d, writes=[idb4])
            for h in range(8):
                k.dma("sp", id8f[:, h, :], cident_d[0:64, 0:64], writes=[id8f])
            k.op("dve", lambda e: e.memset(Hb[0][:], 0.0), writes=[Hb[0]])
            k.op("dve", lambda e: e.memset(Hb[1][:], 0.0), writes=[Hb[1]])

            def f32t(name):
                return sb(st, name, [128, 512], F32)
            sgm, a_t, g_t, r_t, k_t, v_t = [f32t(n) for n in ("sgm", "a_t", "g_t", "r_t", "k_t", "v_t")]
            kkn, kmod, bvec = f32t("kkn"), f32t("kmod"), f32t("bvec")
            S0 = f32t("S0")
            ogl_f = big.ap[:, 12304:16400]
            EinT = Buf("EinT", ogl_f[:, 0:512].rearrange("p (c t) -> p c t", c=4))
            EninT = Buf("EninT", ogl_f[:, 512:1024].rearrange("p (c t) -> p c t", c=4))
            EexT = Buf("EexT", ogl_f[:, 1024:1536].rearrange("p (c t) -> p c t", c=4))
            bon = Buf("bon", ogl_f[:, 1536:2048])
            S1 = Buf("S1", ogl_f[:, 2048:2560])
            S2 = Buf("S2", ogl_f[:, 2560:3072])
            Eex = Buf("Eex", ogl_f[:, 3072:3584])
            Erev = Buf("Erev", ogl_f[:, 3584:4096])
            v_bf = sb(st, "v_bf", [128, 512], BF16)
            twad = sb(st, "twad", [128, 128], BF16)
            sgT = sb(st, "sgT", [128, 128], BF16)
            small = sb(st, "small", [128, 6, 8], F32)
            X = sb(st, "X", [128, 8, 2, 64], BF16)
            Bh = sb(st, "Bh", [128, 512], BF16)
            Kh = sb(st, "Kh", [128, 512], BF16)
            AR = sb(st, "AR", [128, 4, 2, 128], BF16)
            BTz = sb(st, "BTz", [128, 4, 2, 128], BF16)
            KTz = sb(st, "KTz", [128, 4, 2, 128], BF16)
            k.op("dve", lambda e: e.memset(BTz[:], 0.0), writes=[BTz])
            k.op("dve", lambda e: e.memset(KTz[:], 0.0), writes=[KTz])
            gC = sb(st, "gC", [64, 8], F32)
            ArbT = [sb(st, f"ArbT{h}", [128, 4, 128], BF16) for h in range(2)]
            MakT = [sb(st, f"MakT{h}", [128, 4, 128], BF16) for h in range(2)]
            ArkT = [sb(st, f"ArkT{h}", [128, 4, 128], BF16) for h in range(2)]
            WU = sb(st, "WU", [128, 8, 2, 64], BF16)
            Dg = sb(st, "Dg", [64, 8, 64], F32)
            PTb = sb(st, "PTb", [128, 8, 64], BF16)
            QeT = sb(st, "QeT", [128, 8, 128], BF16)
            k.op("dve", lambda e: e.memset(PTb[:], 0.0), writes=[PTb])
            k.op("dve", lambda e: e.memset(QeT[:], 0.0), writes=[QeT])
            o_bf = sb(st, "o_bf", [128, 512], BF16)

            def v3(ap, a):
                return ap.rearrange("p (a b) -> p a b", a=a)

            def bfv(b_, half):
                return b_.ap.bitcast(BF16)[:, half * 512:(half + 1) * 512].rearrange("p (c t) -> p c t", c=4)
            alias3 = [(bfv(sgm, hf_), bfv(a_t, hf_), bfv(k_t, hf_)) for hf_ in range(2)]
            aliasb = [Buf(f"alias{hf_}", None) for hf_ in range(2)]

            def hv(b_, kind):
                out = []
                for hf_ in range(2):
                    if kind == "tok":
                        ap_ = b_.ap[:, hf_ * 256:(hf_ + 1) * 256]
                    elif kind == "ch":
                        ap_ = b_.ap[:, 2 * hf_:2 * hf_ + 2]
                    elif kind == "hd":
                        ap_ = b_.ap[:, 4 * hf_:4 * hf_ + 4]
                    else:
                        ap_ = b_.ap[:, :, 4 * hf_:4 * hf_ + 4]
                    out.append(Buf(f"{b_.name}_{hf_}", ap_))
                return out
            sgmH, a_tH, g_tH, r_tH, k_tH, v_tH = [hv(b_, "tok") for b_ in (sgm, a_t, g_t, r_t, k_t, v_t)]
            kknH, kmodH, bvecH, S0H, S1H, S2H = [hv(b_, "tok") for b_ in (kkn, kmod, bvec, S0, S1, S2)]
            EexH, ErevH, bonH, v_bfH, BhH, KhH, o_bfH = [hv(b_, "tok") for b_ in (Eex, Erev, bon, v_bf, Bh, Kh, o_bf)]
            EinTH, EninTH, EexTH, ARH, BTzH, KTzH = [hv(b_, "ch") for b_ in (EinT, EninT, EexT, AR, BTz, KTz)]
            XH, WUH, DgH, PTbH, QeTH, gCH = [hv(b_, "hd") for b_ in (X, WU, Dg, PTb, QeT, gC)]
            HbH = [hv(b_, "hd") for b_ in Hb]
            smallH = hv(small, "sm")
            orwTH = [Buf(f"orwT_{hf_}", None) for hf_ in range(2)]

            nch_run = NCH if "rw_short" not in taps else 2
            if "rw_cut0" in taps:
                nch_run = 0
            def PROJ(n):
                t0 = n * 128
                ucur = [uT_ap[:, c, t0 + 1:t0 + 129] for c in range(8)]
                uprv = [uT_ap[:, c, t0:t0 + 128] for c in range(8)]

                def proj_tok(pb_, c0, c1):
                    for c in range(8):
                        MM(pb_[:, 0:c1 - c0], ucur[c], W1[c][:, c0:c1], c == 0, False, [uTb[c], W1[c]], [pb_])
                        MM(pb_[:, 0:c1 - c0], uprv[c], W2[c][:, c0:c1], False, c == 7, [uTb[c], W2[c]], [pb_])

                def proj_ch(out_ap, pb_, c0, c1):
                    for c in range(8):
                        MM(out_ap, W1[c][:, c0:c1], ucur[c], c == 0, False, [uTb[c], W1[c]], [pb_])
                        MM(out_ap, W2[c][:, c0:c1], uprv[c], False, c == 7, [uTb[c], W2[c]], [pb_])

                pL = bank()
                pLv = v3(pL[:], 4)
                proj_ch(pLv[:, 0, :], pL, 1536, 1664)
                proj_ch(pLv[:, 2, :], pL, 1664, 1792)
                ACT(twad[0:64, :], pLv[0:64, 0, :], AF.Tanh, [pL], [twad])
                ACT(twad[64:128, :], pLv[64:128, 0, :], AF.Copy, [pL], [twad])
                ACT(sgT[:], pLv[:, 2, :], AF.Sigmoid, [pL], [sgT])
                pR, pK, pV = bank(True), bank(True), bank(True)
                proj_tok(pR, 0, 512)
                proj_tok(pK, 512, 1024)
                proj_tok(pV, 1024, 1536)
                pW, pA, pG = bank(True), bank(True), bank(True)
                MM(pW[:], twad[:], w2b[:], True, False, [twad, w2b], [pW])
                MM(pW[:], ones1[:], w0r[:], False, True, [ones1, w0r], [pW])
                MM(pA[:], twad[:], a2b[:], True, False, [twad, a2b], [pA])
                MM(pA[:], ones1[:], a0r[:], False, True, [ones1, a0r], [pA])
                MM(pG[:], sgT[:], g2b[:], True, True, [sgT, g2b], [pG])
                return pR, pK, pV, pW, pA, pG

            nxt = PROJ(0) if nch_run > 0 else None
            e1_done = False
            for n in range(nch_run):
                t0 = n * 128
                if not (n > 0 and e1_done):
                    pR, pK, pV, pW, pA, pG = nxt
                Hc, Hn = HbH[n % 2], HbH[(n + 1) % 2]
                pg = bank(True)
                pCs, pTs, pYs = [None, None], [None, None], [None, None]
                v4 = lambda ap: ap.rearrange("p (a b) -> p a b", a=4)
                cs_ = lambda hf: slice(hf * 256, hf * 256 + 256)

                def E1(hf):
                    if hf == 1:
                        return
                    ACT(sgm[:], pW[:], AF.Sigmoid, [pW], sgmH)
                    CP("dve", k_t[:], pK[:], [pK], k_tH)
                    ACT(a_t[:], pA[:], AF.Sigmoid, [pA], a_tH)
                    ACT(r_t[:], pR[:], AF.Copy, [pR], r_tH)
                    ACT(v_t[:], pV[:], AF.Copy, [pV], v_tH)
                    CP("pool", v_bf[:], v_t[:], v_tH, v_bfH)
                    unpin(pR, pK, pV, pW, pA)

                def Eg():
                    ACT(g_t[:], pG[:], AF.Copy, [pG], g_tH)
                    unpin(pG)

                def C1(hf):
                    s_ = sgmH[hf]
                    pC, pT = bank(True), bank(True)
                    MM(pC[:, 0:256], Mtri[:, 0, :], s_[:], True, True, [Mtri, s_], [pC], last=False)
                    MM(pC[:, 256:512], Mtri[:, 2, :], s_[:], True, True, [Mtri, s_], [pC])
                    for i in range(2):
                        MM(v3(pT[:], 4)[:, i, :], s_[:, i * 128:(i + 1) * 128], Mtri[:, 1, :], True, True, [Mtri, s_], [pT], last=False)
                        MM(v3(pT[:], 4)[:, 2 + i, :], s_[:, i * 128:(i + 1) * 128], Mtri[:, 0, :], True, True, [Mtri, s_], [pT], last=(i == 1))
                    for j in range(4):
                        MM(pg[0:64, 4 * hf + j:4 * hf + j + 1], s_[:, j * 64:(j + 1) * 64], negcol[:], True, True, [s_, negcol], [pg], last=(j == 3))
                    pCs[hf], pTs[hf] = pC, pT

                def X1(hf):
                    pC, pT = pCs[hf], pTs[hf]
                    ACT(EexH[hf][:], pC[:, 0:256], AF.Exp, [pC], [EexH[hf]])
                    yield
                    ACT(ErevH[hf][:], pC[:, 256:512], AF.Exp, [pC], [ErevH[hf]])
                    yield
                    ACT(EinTH[hf][:], v3(pT[:], 4)[:, 0:2, :], AF.Exp, [pT], [EinTH[hf]])
                    yield
                    ACT(EninTH[hf][:], v3(pT[:], 4)[:, 0:2, :], AF.Exp, [pT], [EninTH[hf]], scale=-1.0)
                    yield
                    ACT(EexTH[hf][:], v3(pT[:], 4)[:, 2:4, :], AF.Exp, [pT], [EexTH[hf]])
                    yield
                    ACT(gCH[hf][:], pg[0:64, 4 * hf:4 * hf + 4], AF.Exp, [pg], [gCH[hf]])
                    unpin(pC, pT)
                    if hf == 1:
                        unpin(pg)

                def K1(hf):
                    cs, sm = cs_(hf), smallH[hf]
                    TT("pool", S0H[hf][:], k_tH[hf][:], rows[:, 0, cs], ALU.mult, [k_tH[hf], rows], [S0H[hf]])
                    yield
                    TT("pool", S1H[hf][:], S0H[hf][:], S0H[hf][:], ALU.mult, [S0H[hf]], [S1H[hf]])
                    yield
                    k.op("dve", lambda e: e.tensor_reduce(sm[:, 0, :], v4(S1H[hf][:]), AX.X, ALU.add), [S1H[hf]], [sm])
                    yield
                    TS("dve", sm[:, 1, :], sm[:, 0, :], 1e-24, None, ALU.add, None, [sm], [sm])
                    yield
                    k.op("dve", lambda e: e.reciprocal(sm[:, 1, :], sm[:, 1, :]), [sm], [sm])
                    yield
                    ACT(sm[:, 1, :], sm[:, 1, :], AF.Sqrt, [sm], [sm])
                    yield
                    TT("dve", v4(kknH[hf][:]), v4(S0H[hf][:]), sm[:, 1, :].unsqueeze(2).to_broadcast([128, 4, 64]), ALU.mult, [S0H[hf], sm], [kknH[hf]])

                def A1(hf):
                    cs = cs_(hf)
                    STT("dve", S2H[hf][:], a_tH[hf][:], -1.0, rows[:, 1, cs], ALU.add, ALU.mult, [a_tH[hf], rows], [S2H[hf]])
                    yield
                    STT("dve", kmodH[hf][:], S2H[hf][:], 1.0, k_tH[hf][:], ALU.add, ALU.mult, [S2H[hf], k_tH[hf]], [kmodH[hf]])
                    yield
                    TT("pool", bvecH[hf][:], kknH[hf][:], a_tH[hf][:], ALU.mult, [kknH[hf], a_tH[hf]], [bvecH[hf]])

                def O1(hf):
                    STT("dve", XH[hf][:, :, 0, :], v4(kknH[hf][:]), -1.0, v4(EexH[hf][:]), ALU.mult, ALU.mult, [kknH[hf], EexH[hf]], [XH[hf]])
                    yield
                    TT("dve", BhH[hf][:], bvecH[hf][:], ErevH[hf][:], ALU.mult, [bvecH[hf], ErevH[hf]], [BhH[hf]])
                    yield
                    TT("pool", KhH[hf][:], kmodH[hf][:], ErevH[hf][:], ALU.mult, [kmodH[hf], ErevH[hf]], [KhH[hf]])

                def B1(hf):
                    cs, sm = cs_(hf), smallH[hf]
                    TT("pool", S1H[hf][:], r_tH[hf][:], kmodH[hf][:], ALU.mult, [r_tH[hf], kmodH[hf]], [S1H[hf]])
                    yield
                    TT("pool", S1H[hf][:], S1H[hf][:], rows[:, 2, cs], ALU.mult, [S1H[hf], rows], [S1H[hf]])
                    yield
                    k.op("dve", lambda e: e.tensor_reduce(sm[:, 2, :], v4(S1H[hf][:]), AX.X, ALU.add), [S1H[hf]], [sm])
                    yield
                    TT("dve", v4(bonH[hf][:]), v4(v_tH[hf][:]), sm[:, 2, :].unsqueeze(2).to_broadcast([128, 4, 64]), ALU.mult, [v_tH[hf], sm], [bonH[hf]])

                def T1(hf):
                    pA_, pB_ = bank(True), bank(True)
                    for i in range(2):
                        sl = slice(i * 128, (i + 1) * 128)
                        TR(v3(pA_[:], 4)[:, i, :], kknH[hf][:, sl], identf[:], [kknH[hf], identf], [pA_], last=False)
                        TR(v3(pA_[:], 4)[:, 2 + i, :], r_tH[hf][:, sl], identf[:], [r_tH[hf], identf], [pA_], last=(i == 1))
                        TR(v3(pB_[:], 4)[:, i, :], bvecH[hf][:, sl], identf[:], [bvecH[hf], identf], [pB_], last=False)
                        TR(v3(pB_[:], 4)[:, 2 + i, :], kmodH[hf][:, sl], identf[:], [kmodH[hf], identf], [pB_], last=(i == 1))
                    ARh = ARH[hf]
                    yield
                    STT("dve", ARh[:, :, 0, :], v3(pA_[:], 4)[:, 0:2, :], -1.0, EexTH[hf][:], ALU.mult, ALU.mult, [pA_, EexTH[hf]], [ARh])
                    yield
                    TT("dve", ARh[:, :, 1, :], v3(pA_[:], 4)[:, 2:4, :], EinTH[hf][:], ALU.mult, [pA_, EinTH[hf]], [ARh])
                    yield
                    for hh in range(2):
                        ps_ = slice(hh * 64, hh * 64 + 64)
                        TT("dve", BTzH[hf][ps_, :, hh, :], v3(pB_[:], 4)[ps_, 0:2, :], EninTH[hf][ps_], ALU.mult, [pB_, EninTH[hf]], [BTzH[hf]])
                        TT("dve", KTzH[hf][ps_, :, hh, :], v3(pB_[:], 4)[ps_, 2:4, :], EninTH[hf][ps_], ALU.mult, [pB_, EninTH[hf]], [KTzH[hf]])
                    unpin(pA_, pB_)

                def I1(hf):
                    pLA = [bank(), bank()]
                    pMA = [bank(), bank()]
                    pLL = bank()
                    for j in range(4):
                        c4l, hh = j // 2, j % 2
                        ar_rhs = ARH[hf][:, c4l, :, :].rearrange("p a t -> p (a t)")
                        o1 = pLA[j // 2][:, (j % 2) * 256:(j % 2) * 256 + 256]
                        o2 = pMA[j // 2][:, (j % 2) * 256:(j % 2) * 256 + 256]
                        MM(o1, BTzH[hf][:, c4l, hh, :], ar_rhs, True, True, [BTzH[hf], ARH[hf]], [pLA[j // 2]], last=(j == 3))
                        MM(o2, KTzH[hf][:, c4l, hh, :], ar_rhs, True, True, [KTzH[hf], ARH[hf]], [pMA[j // 2]], last=(j == 3))
                        MM(pLL[:, j * 128:(j + 1) * 128], ARH[hf][:, c4l, 0, :], BTzH[hf][:, c4l, hh, :], True, True, [ARH[hf], BTzH[hf]], [pLL], last=(j == 3))
                    N0, L0 = Nb[hf][0], Lb[hf][0]
                    for q in range(2):
                        src = pLA[q][:].rearrange("p (h a t) -> p h a t", h=2, a=2)
                        TT("dve", N0[:, 2 * q:2 * q + 2, :], src[:, :, 0, :], mND[:], ALU.mult, [pLA[q], mND], [N0])
                        TT("dve", ArbT[hf][:, 2 * q:2 * q + 2, :], src[:, :, 1, :], mSI[:, :, 1, :], ALU.mult, [pLA[q], mSI], [ArbT[hf]])
                        src2 = pMA[q][:].rearrange("p (h a t) -> p h a t", h=2, a=2)
                        TT("dve", MakT[hf][:, 2 * q:2 * q + 2, :], src2[:, :, 0, :], mSI[:, :, 0, :], ALU.mult, [pMA[q], mSI], [MakT[hf]])
                        TT("dve", ArkT[hf][:, 2 * q:2 * q + 2, :], src2[:, :, 1, :], mSI[:, :, 1, :], ALU.mult, [pMA[q], mSI], [ArkT[hf]])
                    TT("dve", L0[:], v3(pLL[:], 4), mSL4[:, 0], ALU.mult, [pLL, mSL4], [L0])
                    TT("dve", bfv(sgm, hf), v3(pLL[:], 4), mSL4[:, 1], ALU.mult, [pLL, mSL4], [sgmH[hf]])
                    TT("pool", Sm[hf][:], N0[:], idb4[:], ALU.add, [N0, idb4], [Sm[hf]])

                def NLa(hf, lev):
                    cur = lev % 2
                    Nc, Lc = Nb[hf][cur], Lb[hf][cur]
                    Nn, Ln = Nb[hf][1 - cur], Lb[hf][1 - cur]
                    pL2 = bank()
                    for j in range(4):
                        MM(v3(pL2[:], 4)[:, j, :], Nc[:, j, :], Lc[:, j, :], True, True, [Nc, Lc], [pL2], last=(j == 3))
                    if lev < 4:
                        pN2 = bank()
                        for j in range(4):
                            MM(v3(pN2[:], 4)[:, j, :], Lc[:, j, :], Nc[:, j, :], True, True, [Nc, Lc], [pN2], last=(j == 3))
                    ACT(Ln[:], v3(pL2[:], 4), AF.Copy, [pL2], [Ln])
                    if lev < 4:
                        CP("dve", Nn[:], v3(pN2[:], 4), [pN2], [Nn])

                def NLb(hf, lev):
                    Ln = Lb[hf][1 - (lev % 2)]
                    pS = bank()
                    for j in range(4):
                        MM(v3(pS[:], 4)[:, j, :], Ln[:, j, :], Sm[hf][:, j, :], True, False, [Ln, Sm[hf]], [pS])
                        MM(v3(pS[:], 4)[:, j, :], identb[:], Sm[hf][:, j, :], False, True, [identb, Sm[hf]], [pS], last=(j == 3))
                    CP("act" if hf == 0 else "dve", Sm[hf][:], v3(pS[:], 4), [pS], [Sm[hf]])

                def MGa(hf):
                    Lo_ap, Tm_ap, Zb_ap = bfv(sgm, hf), bfv(a_t, hf), bfv(k_t, hf)
                    pTt = bank()
                    pTtb = pTt[:].bitcast(BF16)[:, 0:512].rearrange("p (c t) -> p c t", c=4)
                    for j in range(4):
                        TR(pTtb[:, j, :], Sm[hf][:, j, :], identb[:], [Sm[hf], identb], [pTt], last=(j == 3))
                    ACT(Tm_ap, pTtb, AF.Copy, [pTt], [a_tH[hf]])
                    pZ_ = bank()
                    for j in range(4):
                        MM(v3(pZ_[:], 4)[:, j, :], Lo_ap[:, j, :], Sm[hf][:, j, :], True, True, [sgmH[hf], Sm[hf]], [pZ_], last=(j == 3))
                    CP("dve", Zb_ap, v3(pZ_[:], 4), [pZ_], [k_tH[hf]])

                def MGb(hf):
                    Tm_ap, Zb_ap = bfv(a_t, hf), bfv(k_t, hf)
                    pS = bank()
                    for j in range(4):
                        MM(v3(pS[:], 4)[:, j, :], Tm_ap[:, j, :], Zb_ap[:, j, :], True, False, [a_tH[hf], k_tH[hf]], [pS])
                        MM(v3(pS[:], 4)[:, j, :], identb[:], Sm[hf][:, j, :], False, True, [identb, Sm[hf]], [pS], last=(j == 3))
                    ACT(Sm[hf][:], v3(pS[:], 4), AF.Copy, [pS], [Sm[hf]])

                def P1a(hf):
                    pMV = bank()
                    for j in range(4):
                        MM(pMV[:, j * 64:(j + 1) * 64], MakT[hf][:, j, :], v_bfH[hf][:, j * 64:(j + 1) * 64], True, True, [MakT[hf], v_bfH[hf]], [pMV], last=(j == 3))
                    ACT(XH[hf][:, :, 1, :], v4(pMV[:, 0:256]), AF.Copy, [pMV], [XH[hf]])

                def P1b(hf):
                    pWU = bank()
                    for j in range(4):
                        MM(pWU[:, j * 128:(j + 1) * 128], Sm[hf][:, j, :], XH[hf][:, j, :, :].rearrange("p a b -> p (a b)"), True, True, [Sm[hf], XH[hf]], [pWU], last=(j == 3))
                    ACT(WUH[hf][:].rearrange("p h a b -> p (h a b)"), pWU[:], AF.Copy, [pWU], [WUH[hf]])

                def P1c(hf):
                    pP = bank(True)
                    yield
                    for j in range(4):
                        MM(pP[0:64, j * 64:(j + 1) * 64], WUH[hf][:, j, 0, :], BhH[hf][:, j * 64:(j + 1) * 64], True, True, [WUH[hf], BhH[hf]], [pP], last=(j == 3))
                    yield
                    TT("pool", DgH[hf][:], id8f[:, 0:4, :], gCH[hf][:].unsqueeze(2).to_broadcast([64, 4, 64]), ALU.mult, [id8f, gCH[hf]], [DgH[hf]])
                    yield
                    TT("dve", PTbH[hf][0:64], v4(pP[0:64, 0:256]), DgH[hf][:], ALU.add, [pP, DgH[hf]], [PTbH[hf]])
                    yield
                    pQ = bank(True)
                    yield
                    for j in range(4):
                        p0 = (j % 2) * 64
                        oq = pQ[0:64, j * 128:(j + 1) * 128]
                        MM(oq, WUH[hf][:, j, 0, :], ArbT[hf][:, j, :], True, False, [WUH[hf], ArbT[hf]], [pQ])
                        MM(oq, identb[:, p0:p0 + 64], ARH[hf][:, j // 2, 1, :], False, True, [identb, ARH[hf]], [pQ], last=(j == 3))
                    yield
                    ACT(QeTH[hf][0:64], v3(pQ[0:64, :], 4), AF.Copy, [pQ], [QeTH[hf]])
                    unpin(pP, pQ)

                def P1d(hf):
                    pY = bank(True)
                    yield
                    for j in range(4):
                        oy = pY[:, j * 64:(j + 1) * 64]
                        vj = v_bfH[hf][:, j * 64:(j + 1) * 64]
                        MM(oy, QeTH[hf][:, j, :], Hc[hf][:, j, :], True, False, [QeTH[hf], Hc[hf]], [pY])
                        MM(oy, ArbT[hf][:, j, :], WUH[hf][:, j, 1, :], False, False, [ArbT[hf], WUH[hf]], [pY])
                        MM(oy, ArkT[hf][:, j, :], vj, False, True, [ArkT[hf], v_bfH[hf]], [pY], last=(j == 3))
                    yield
                    pH = bank(True)
                    yield
                    for j in range(4):
                        oh = pH[0:64, j * 64:(j + 1) * 64]
                        vj = v_bfH[hf][:, j * 64:(j + 1) * 64]
                        MM(oh, PTbH[hf][:, j, :], Hc[hf][:, j, :], True, False, [PTbH[hf], Hc[hf]], [pH])
                        MM(oh, BhH[hf][:, j * 64:(j + 1) * 64], WUH[hf][:, j, 1, :], False, False, [BhH[hf], WUH[hf]], [pH])
                        MM(oh, KhH[hf][:, j * 64:(j + 1) * 64], vj, False, True, [KhH[hf], v_bfH[hf]], [pH], last=(j == 3))
                    yield
                    ACT(Hn[hf][0:64], v4(pH[0:64, 0:256]), AF.Copy, [pH], [Hn[hf]])
                    unpin(pH)
                    pYs[hf] = pY

                def F1(hf):
                    cs, sm, pY = cs_(hf), smallH[hf], pYs[hf]
                    y_, q_ = S0H[hf], S2H[hf]
                    bc = lambda r_: sm[:, r_, :].unsqueeze(2).to_broadcast([128, 4, 64])
                    ACT(y_[:], pY[:, 0:256], AF.Copy, [pY], [y_])
                    unpin(pY)
                    yield
                    k.op("dve", lambda e: e.tensor_reduce(sm[:, 3, :], v4(y_[:]), AX.X, ALU.add), [y_], [sm])
                    yield
                    TT("pool", q_[:], y_[:], y_[:], ALU.mult, [y_], [q_])
                    yield
                    k.op("dve", lambda e: e.tensor_reduce(sm[:, 4, :], v4(q_[:]), AX.X, ALU.add), [q_], [sm])
                    yield
                    TS("dve", sm[:, 3, :], sm[:, 3, :], 1.0 / 64, None, ALU.mult, None, [sm], [sm])
                    yield
                    TT("dve", sm[:, 5, :], sm[:, 3, :], sm[:, 3, :], ALU.mult, [sm], [sm])
                    yield
                    STT("dve", sm[:, 4, :], sm[:, 4, :], 1.0 / 64, sm[:, 5, :], ALU.mult, ALU.subtract, [sm], [sm])
                    yield
                    TS("dve", sm[:, 4, :], sm[:, 4, :], 64e-5, None, ALU.add, None, [sm], [sm])
                    yield
                    k.op("dve", lambda e: e.reciprocal(sm[:, 4, :], sm[:, 4, :]), [sm], [sm])
                    yield
                    ACT(sm[:, 4, :], sm[:, 4, :], AF.Sqrt, [sm], [sm])

                def F1b(hf):
                    cs, sm = cs_(hf), smallH[hf]
                    y_ = S0H[hf]
                    bc = lambda r_: sm[:, r_, :].unsqueeze(2).to_broadcast([128, 4, 64])
                    TT("dve", v4(y_[:]), v4(y_[:]), bc(3), ALU.subtract, [y_, sm], [y_])
                    yield
                    TT("dve", v4(y_[:]), v4(y_[:]), bc(4), ALU.mult, [y_, sm], [y_])
                    yield
                    TT("pool", y_[:], y_[:], rows[:, 3, cs], ALU.mult, [y_, rows], [y_])
                    yield
                    TT("pool", y_[:], y_[:], rows[:, 4, cs], ALU.add, [y_, rows], [y_])
                    yield
                    TT("pool", y_[:], y_[:], bonH[hf][:], ALU.add, [y_, bonH[hf]], [y_])
                    yield
                    TT("pool", o_bfH[hf][:], y_[:], g_tH[hf][:], ALU.mult, [y_, g_tH[hf]], [o_bfH[hf]])
                    yield
                    pO = bank(True)
                    pOb = pO[:].bitcast(BF16)[:, 0:256].rearrange("p (c t) -> p c t", c=2)
                    yield
                    for i in range(2):
                        TR(pOb[:, i, :], o_bfH[hf][:, i * 128:(i + 1) * 128], identb[:], [o_bfH[hf], identb], [pO], last=(i == 1))
                    yield
                    ACT(orwT_ap[:, 2 * hf:2 * hf + 2, t0:t0 + 128], pOb, AF.Copy, [pO], [orwTH[hf]])
                    unpin(pO)

                def both(fn, *a):
                    gens = [fn(hf_, *a) for hf_ in range(2)]
                    gens = [g for g in gens if g is not None]
                    while gens:
                        for g in list(gens):
                            try:
                                next(g)
                            except StopIteration:
                                gens.remove(g)
                if not (n > 0 and e1_done):
                    both(E1)
                Eg()
                for fn in (C1, K1, X1, A1, O1, B1, T1, I1):
                    both(fn)
                for lev in range(5):
                    both(NLa, lev)
                    both(NLb, lev)
                for fn in (MGa, MGb, P1a, P1b, P1c, P1d):
                    both(fn)
                if n + 1 < nch_run:
                    nxt = PROJ(n + 1)
                both(F1)
                if n + 1 < nch_run:
                    pR, pK, pV, pW, pA, pG = nxt
                    both(E1)
                    e1_done = True
                else:
                    e1_done = False
                both(F1b)
            if "rw_short" in taps:
                tap("orwT", orwT_ap[:, :, 0:256], [128, 4, 256], orwTH)
            else:
                tap("orwT", orwT_ap, [128, 4, T], orwTH)
            k.barrier()
        if stop_after <= 2:
            k.finish()
            return nc, tap_d

        with ExitStack() as st:
            WG = [sb(st, f"WG{c}", [128, GL], BF16) for c in range(8)]
            for c in range(8):
                k.dma("pool", WG[c][:], win_v[:, c, RW:RW + GL], writes=[WG[c]])
            ga2 = sb(st, "ga2", [128, 256], F32)
            ngr = sb(st, "ngr", [128, 4, 128], F32)
            Gtri = sb(st, "Gtri", [128, 3, 128], F32)
            c16 = sb(st, "c16", [128, 1], F32)
            mI4 = sb(st, "mI4", [128, 4, 128], F32)
            k.op("dve", lambda e: e.memset(ga2[:], 0.0), writes=[ga2])
            k.dma("sp", ga2[0:16, :], ga2_d, writes=[ga2])
            gabr = sb(st, "gabr", [128, 256], F32)
            k.dma("sp", gabr[:], _bc_rows(gab_d, 128, 256), writes=[gabr])
            k.dma("sp", ngr[:], bass.AP(gng2_d.tensor, gng2_d.offset, [[0, 128], [0, 4], [1, 128]]), writes=[ngr])
            k.dma("sp", Gtri[:], ctri_d[:, 0:3, :], writes=[Gtri])
            TS("dve", Gtri[:], Gtri[:], -1.0 / 16, None, ALU.mult, None, [Gtri], [Gtri])
            k.op("dve", lambda e: e.memset(c16[:], -1.0 / 16), writes=[c16])
            for h4 in range(4):
                k.dma("sp", mI4[:, h4, :], ctri_d[:, 1, :], writes=[mI4])
            Sst = sb(st, "Sst", [128, 4, 128], F32)
            Sbf = sb(st, "Sbf", [128, 4, 128], BF16)
            k.op("dve", lambda e: e.memset(Sst[:], 0.0), writes=[Sst])
            k.op("dve", lambda e: e.memset(Sbf[:], 0.0), writes=[Sbf])

            def v3(ap, a):
                return ap.rearrange("p (a b) -> p a b", a=a)

            def gset(i):
                B = {}
                B["adT"] = sb(st, f"g_adT{i}", [128, 128], F32)
                k.op("dve", lambda e: e.memset(B["adT"][:], 0.0), writes=[B["adT"]])

                B["qkT"] = sb(st, f"g_qkT{i}", [128, 4, 128], F32)
                B["gk"] = sb(st, f"g_gk{i}", [128, 256], F32)
                B["ez"] = sb(st, f"g_ez{i}", [128, 256], F32)
                B["lz"] = sb(st, f"g_lz{i}", [128, 256], F32)
                B["Erev"] = sb(st, f"g_Erev{i}", [128, 256], F32)
                B["Ein"] = sb(st, f"g_Ein{i}", [128, 2, 128], F32)
                B["Enin"] = sb(st, f"g_Enin{i}", [128, 2, 128], F32)
                B["decs"] = sb(st, f"g_decs{i}", [128, 2], F32)
                B["kdec"] = sb(st, f"g_kdec{i}", [128, 256], BF16)
                B["gv"] = sb(st, f"g_v{i}", [128, 512], BF16)
                B["sgg"] = sb(st, f"g_sgg{i}", [128, 512], F32)
                B["QsT"] = sb(st, f"g_QsT{i}", [128, 2, 128], BF16)
                B["KsTz"] = sb(st, f"g_KsTz{i}", [128, 2, 2, 128], BF16)
                k.op("dve", lambda e: e.memset(B["KsTz"][:], 0.0), writes=[B["KsTz"]])
                B["attT"] = sb(st, f"g_attT{i}", [128, 4, 128], BF16)
                B["osb"] = sb(st, f"g_osb{i}", [128, 512], F32)
                B["osq"] = sb(st, f"g_osq{i}", [128, 512], F32)
                B["gsm"] = sb(st, f"g_sm{i}", [128, 2, 4], F32)
                B["of"] = sb(st, f"g_of{i}", [128, 512], BF16)
                return B
            GS = [gset(0), gset(1)]

            def G1(n, B):
                ucur = [uT_ap[:, c, n * 128 + 1:n * 128 + 129] for c in range(8)]
                pC, pD = bank(True), bank(True)
                pCv = v3(pC[:], 4)
                for i, c0 in enumerate((0, 128, 256, 384)):
                    for c in range(8):
                        MM(pCv[:, i, :], WG[c][:, c0:c0 + 128], ucur[c], c == 0, c == 7, [WG[c], uTb[c]], [pC])
                yield
                for c in range(8):
                    MM(pD[0:16, 0:128], WG[c][:, 1536:1552], ucur[c], c == 0, c == 7, [WG[c], uTb[c]], [pD])
                yield
                CP("dve", B["adT"][0:16, :], pD[0:16, 0:128], [pD], [B["adT"]])
                yield
                ACT(B["qkT"][:], pCv, AF.Copy, [pC], [B["qkT"]])
                unpin(pC, pD)

            def G2(n, B):
                ucur = [uT_ap[:, c, n * 128 + 1:n * 128 + 129] for c in range(8)]
                pK, pV, pGg = bank(True), bank(True), bank(True)
                for c in range(8):
                    MM(pK[:, 0:256], ucur[c], WG[c][:, 256:512], c == 0, c == 7, [WG[c], uTb[c]], [pK])
                yield
                for c in range(8):
                    MM(pV[:], ucur[c], WG[c][:, 512:1024], c == 0, c == 7, [WG[c], uTb[c]], [pV])
                yield
                for c in range(8):
                    MM(pGg[:], ucur[c], WG[c][:, 1024:1536], c == 0, c == 7, [WG[c], uTb[c]], [pGg])
                yield
                CP("dve", B["gk"][:], pK[:, 0:256], [pK], [B["gk"]])
                yield
                ACT(B["gv"][:], pV[:], AF.Copy, [pV], [B["gv"]])
                yield
                ACT(B["sgg"][:], pGg[:], AF.Silu, [pGg], [B["sgg"]])
                unpin(pK, pV, pGg)

            def G3(n, B):
                pZ = bank(True)
                MM(pZ[:, 0:256], B["adT"][:], ga2[:], True, True, [B["adT"], ga2], [pZ])
                yield
                TT("dve", B["ez"][:], pZ[:, 0:256], gabr[:], ALU.add, [pZ, gabr], [B["ez"]])
                unpin(pZ)
                yield
                ACT(B["ez"][:], B["ez"][:], AF.Exp, [B["ez"]], [B["ez"]], scale=-1.0)
                yield
                ACT(B["lz"][:], B["ez"][:], AF.Ln, [B["ez"]], [B["lz"]], bias=1.0)

            def G4(n, B):
                lz = B["lz"]
                pB, pDc = bank(True), bank(True)
                MM(pB[:, 0:256], Gtri[:, 2, :], lz[:], True, True, [Gtri, lz], [pB], last=False)
                yield
                for c2 in range(2):
                    MM(pB[:, 256 + c2 * 128:384 + c2 * 128], lz[:, c2 * 128:(c2 + 1) * 128], Gtri[:, 1, :], True, True, [Gtri, lz], [pB], last=(c2 == 1))
                yield
                for c2 in range(2):
                    MM(pDc[:, c2:c2 + 1], lz[:, c2 * 128:(c2 + 1) * 128], c16[:], True, True, [lz, c16], [pDc], last=(c2 == 1))
                yield
                ACT(B["Erev"][:], pB[:, 0:256], AF.Exp, [pB], [B["Erev"]])
                yield
                ACT(B["Ein"][:], v3(pB[:, 256:512], 2), AF.Exp, [pB], [B["Ein"]])
                yield
                ACT(B["Enin"][:], v3(pB[:, 256:512], 2), AF.Exp, [pB], [B["Enin"]], scale=-1.0)
                yield
                ACT(B["decs"][:], pDc[:, 0:2], AF.Exp, [pDc], [B["decs"]])
                unpin(pB, pDc)
                yield
                TT("dve", B["kdec"][:], B["gk"][:], B["Erev"][:], ALU.mult, [B["gk"], B["Erev"]], [B["kdec"]])
                yield
                STT("dve", B["QsT"][:], B["qkT"][:, 0:2, :], 0.125, B["Ein"][:], ALU.mult, ALU.mult, [B["qkT"], B["Ein"]], [B["QsT"]])
                yield
                for hh in range(2):
                    ps_ = slice(hh * 64, hh * 64 + 64)
                    TT("pool", B["KsTz"][ps_, :, hh, :], B["qkT"][ps_, 2:4, :], B["Enin"][ps_], ALU.mult, [B["qkT"], B["Enin"]], [B["KsTz"]])

            def G5(n, B):
                pA = bank(True)
                for h in range(4):
                    MM(v3(pA[:], 4)[:, h, :], B["KsTz"][:, h // 2, h % 2, :], B["QsT"][:, h // 2, :], True, True, [B["KsTz"], B["QsT"]], [pA], last=(h == 3))
                yield
                TT("dve", B["attT"][:], v3(pA[:], 4), mI4[:], ALU.mult, [pA, mI4], [B["attT"]])
                unpin(pA)

            def G6(n, B):
                pOo = bank(True)
                for h in range(4):
                    oo = v3(pOo[:], 4)[:, h, :]
                    MM(oo, B["attT"][:, h, :], B["gv"][:, h * 128:(h + 1) * 128], True, False, [B["attT"], B["gv"]], [pOo])
                    MM(oo, B["QsT"][:, h // 2, :], Sbf[:, h, :], False, True, [B["QsT"], Sbf], [pOo], last=(h == 3))
                pKV = bank(True)
                yield
                for h in range(4):
                    c2 = h // 2
                    MM(v3(pKV[:], 4)[:, h, :], B["kdec"][:, c2 * 128:(c2 + 1) * 128], B["gv"][:, h * 128:(h + 1) * 128], True, True, [B["kdec"], B["gv"]], [pKV], last=(h == 3))
                yield
                for h in range(4):
                    c2, p0 = h // 2, (h % 2) * 64
                    STT("dve", Sst[p0:p0 + 64, h, :], Sst[p0:p0 + 64, h, :], B["decs"][p0:p0 + 64, c2:c2 + 1],
                        v3(pKV[:], 4)[p0:p0 + 64, h, :], ALU.mult, ALU.add, [Sst, B["decs"], pKV], [Sst])
                yield
                CP("pool", Sbf[:], Sst[:], [Sst], [Sbf])
                unpin(pKV)
                B["pOo"] = pOo

            def G7(n, B):
                t0 = n * 128
                pOo, osb, osq, gsm = B["pOo"], B["osb"], B["osq"], B["gsm"]
                ACT(osb[:], pOo[:], AF.Copy, [pOo], [osb])
                unpin(pOo)
                yield
                TT("pool", osq[:], osb[:], osb[:], ALU.mult, [osb], [osq])
                yield
                k.op("dve", lambda e: e.tensor_reduce(gsm[:, 0, :], v3(osq[:], 4), AX.X, ALU.add), [osq], [gsm])
                yield
                TS("dve", gsm[:, 1, :], gsm[:, 0, :], 1.0 / 128, 1e-5, ALU.mult, ALU.add, [gsm], [gsm])
                yield
                k.op("dve", lambda e: e.reciprocal(gsm[:, 1, :], gsm[:, 1, :]), [gsm], [gsm])
                yield
                ACT(gsm[:, 1, :], gsm[:, 1, :], AF.Sqrt, [gsm], [gsm])
                yield
                TT("dve", v3(osb[:], 4), v3(osb[:], 4), gsm[:, 1, :].unsqueeze(2).to_broadcast([128, 4, 128]), ALU.mult, [osb, gsm], [osb])
                yield
                TT("pool", v3(osb[:], 4), v3(osb[:], 4), ngr[:], ALU.mult, [osb, ngr], [osb])
                yield
                TT("pool", B["of"][:], osb[:], B["sgg"][:], ALU.mult, [osb, B["sgg"]], [B["of"]])
                pO = bank(True)
                pOb = pO[:].bitcast(BF16)[:, 0:512].rearrange("p (c t) -> p c t", c=4)
                yield
                for c4 in range(4):
                    TR(pOb[:, c4, :], B["of"][:, c4 * 128:(c4 + 1) * 128], identb[:], [B["of"], identb], [pO], last=(c4 == 3))
                yield
                ACT(oglaT_ap[:, :, t0:t0 + 128], pOb, AF.Copy, [pO], [oglaTb])
                unpin(pO)

            nch_run = NCH if "gla_short" not in taps else 2
            for n in range(0, nch_run, 2):
                for fn in (G1, G2, G3, G4, G5, G6, G7):
                    if fn is G6:
                        for _ in fn(n, GS[0]):
                            pass
                        for _ in fn(n + 1, GS[1]):
                            pass
                        continue
                    gens = [fn(n, GS[0]), fn(n + 1, GS[1])]
                    while gens:
                        for g in list(gens):
                            try:
                                next(g)
                            except StopIteration:
                                gens.remove(g)
            if "gla_short" in taps:
                tap("oglaT", oglaT_ap[:, :, 0:256], [128, 4, 256], [oglaTb])
            else:
                tap("oglaT", oglaT_ap, [128, 4, T], [oglaTb])
            k.barrier()
        if stop_after <= 3:
            k.finish()
            return nc, tap_d

        with ExitStack() as st3:
            mT = sb(st3, "mT", [128, 8, T], BF16)
            mTb = [Buf(f"mT{d}", None) for d in range(8)]
            with ExitStack() as st:
                Wgt = [sb(st, f"Wgt{c}", [128, 2048], BF16) for c in range(8)]
                Wbr = sb(st, "Wbr", [128, 4, D], BF16)
                Wbg = sb(st, "Wbg", [128, 4, D], BF16)
                for c in range(8):
                    k.dma("pool", Wgt[c][:], win_v[:, c, RW + GL:NIN], writes=[Wgt[c]])
                k.dma("pool", Wbr[:], wbr_d.rearrange("(c p) n -> p c n", p=128), writes=[Wbr])
                k.dma("pool", Wbg[:], wbg_d.rearrange("(c p) n -> p c n", p=128), writes=[Wbg])
                sr = [sb(st, f"m_sr{i}", [128, 512], F32) for i in range(2)]
                sg_ = [sb(st, f"m_sg{i}", [128, 512], F32) for i in range(2)]
                it = 0
                for sbk in range(4):
                    tok = slice(sbk * 512, (sbk + 1) * 512)
                    for dt in range(8):
                        dsl = slice(dt * 128, (dt + 1) * 128)
                        p1, p2, p3, p4 = bank(), bank(), bank(), bank()
                        for c in range(8):
                            MM(p1[:], Wgt[c][:, dsl], uT_ap[:, c, sbk * 512 + 1:sbk * 512 + 513], c == 0, c == 7, [Wgt[c], uTb[c]], [p1])
                        for c in range(8):
                            MM(p2[:], Wgt[c][:, 1024 + dt * 128:1024 + (dt + 1) * 128], uT_ap[:, c, sbk * 512 + 1:sbk * 512 + 513],
                               c == 0, c == 7, [Wgt[c], uTb[c]], [p2])
                        for c in range(4):
                            MM(p3[:], Wbr[:, c, dsl], orwT_ap[:, c, tok], c == 0, c == 3, [Wbr] + orwTH, [p3])
                        for c in range(4):
                            MM(p4[:], Wbg[:, c, dsl], oglaT_ap[:, c, tok], c == 0, c == 3, [Wbg, oglaTb], [p4])
                        a_, b_ = sr[it % 2], sg_[it % 2]
                        it += 1
                        ACT(a_[:], p1[:], AF.Sigmoid, [p1], [a_])
                        ACT(b_[:], p2[:], AF.Sigmoid, [p2], [b_])
                        TT("dve", a_[:], a_[:], p3[:], ALU.mult, [a_, p3], [a_])
                        TT("dve", b_[:], b_[:], p4[:], ALU.mult, [b_, p4], [b_])
                        TT("pool", mT[:, dt, tok], a_[:], b_[:], ALU.add, [a_, b_], [mTb[dt]])
                tap("mT", mT[:], [128, 8, T], mTb)
                k.barrier()
            if stop_after <= 4:
                k.finish()
                return nc, tap_d

            with ExitStack() as st:
                Wmx = sb(st, "Wmx", [128, 8, D], BF16)
                k.dma("pool", Wmx[:], wmix_d.rearrange("(c p) n -> p c n", p=128), writes=[Wmx])
                g1row = sb(st, "g1row", [128, D], F32)
                lnr = sb(st, "ln1r", [128, 2, D], F32)
                k.dma("sp", lnr[:, 0, :], _bc_rows(ln1g_d, 128, D), writes=[lnr])
                k.dma("sp", lnr[:, 1, :], _bc_rows(ln1b_d, 128, D), writes=[lnr])
                screp = sb(st, "screp", [128, 8, 128], BF16)
                CP("dve", screp[:], scb[:].unsqueeze(2).to_broadcast([128, 8, 128]), [scb], [screp])
                brow1 = sb(st, "brow1", [1, D], F32)
                k.dma("sp", brow1[:], brow_d[:, 2 * D:3 * D], writes=[brow1])
                with ExitStack() as stw:
                    wg1 = sb(stw, "wg1", [128, 8, D], BF16)
                    k.dma("pool", wg1[:], wada_d.rearrange("(c p) n -> p c n", p=128)[:, :, 2 * D:3 * D], writes=[wg1])
                    for nh in range(2):
                        pg_ = bank()
                        for c in range(8):
                            MM(pg_[:], screp[:, c, :], wg1[:, c, nh * 512:(nh + 1) * 512], c == 0, False, [screp, wg1], [pg_])
                        MM(pg_[:], ones1[:], brow1[:, nh * 512:(nh + 1) * 512], False, True, [ones1, brow1], [pg_])
                        ACT(g1row[:, nh * 512:(nh + 1) * 512], pg_[:], AF.Copy, [pg_], [g1row])
                    k.barrier()
                xin = [sb(st, f"xin{i}", [128, D], F32) for i in range(2)]
                ybuf = [sb(st, f"ybuf{i}", [128, D], F32) for i in range(2)]
                stat = [sb(st, f"l1stat{i}", [128, 2, 6], F32) for i in range(2)]
                mv = [sb(st, f"l1mv{i}", [128, 4], F32) for i in range(2)]
                pms = {}

                def M0(tb):
                    xi = xin[tb % 2]
                    k.dma("sp", xi[:], x_d[tb * 128:(tb + 1) * 128, :], writes=[xi])
                    pm_ = [bank(True), bank(True)]
                    for nh in range(2):
                        for dt in range(8):
                            MM(pm_[nh][:], mT[:, dt, tb * 128:(tb + 1) * 128], Wmx[:, dt, nh * 512:(nh + 1) * 512], dt == 0, dt == 7, [mTb[dt], Wmx], [pm_[nh]])
                    pms[tb] = pm_

                def M1(tb):
                    yb, pm_ = ybuf[tb % 2], pms[tb]
                    for nh in range(2):
                        hs = slice(nh * 512, (nh + 1) * 512)
                        TT("dve", yb[:, hs], pm_[nh][:], g1row[:, hs], ALU.mult, [pm_[nh], g1row], [yb])
                    unpin(*pm_)
                    STT("dve", yb[:], xin[tb % 2][:], ALPHA, yb[:], ALU.mult, ALU.add, [xin[tb % 2], yb], [yb])

                def M2(tb):
                    yb, st_, mv_ = ybuf[tb % 2], stat[tb % 2], mv[tb % 2]
                    for nh in range(2):
                        k.op("dve", lambda e, nh=nh: e.bn_stats(st_[:, nh, :], yb[:, nh * 512:(nh + 1) * 512]), [yb], [st_])
                    k.op("dve", lambda e: e.bn_aggr(mv_[:, 0:2], st_[:]), [st_], [mv_])
                    TS("dve", mv_[:, 2:3], mv_[:, 1:2], 1e-5, None, ALU.add, None, [mv_], [mv_])
                    k.op("dve", lambda e: e.reciprocal(mv_[:, 2:3], mv_[:, 2:3]), [mv_], [mv_])
                    ACT(mv_[:, 2:3], mv_[:, 2:3], AF.Sqrt, [mv_], [mv_])

                def M3(tb):
                    yb, mv_ = ybuf[tb % 2], mv[tb % 2]
                    TS("dve", yb[:], yb[:], mv_[:, 0:1], mv_[:, 2:3], ALU.subtract, ALU.mult, [yb, mv_], [yb])
                    TT("pool", yb[:], yb[:], lnr[:, 0, :], ALU.mult, [yb, lnr], [yb])
                    TT("pool", x1_ap[:, tb, :], yb[:], lnr[:, 1, :], ALU.add, [yb, lnr], [x1b[tb]])

                for tb0 in range(0, 16, 2):
                    for fn in (M0, M1, M2, M3):
                        fn(tb0)
                        fn(tb0 + 1)
                tap("x1", x1_ap, [128, 16, D], x1b)
                k.barrier()
        if stop_after <= 5:
            k.finish()
            return nc, tap_d

        with ExitStack() as st:
            Wo = sb(st, "Wo", [128, 22, D], BF16)
            Wob = [Buf(f"Wo{f}", None) for f in range(22)]
            wfo_v = wfo_d.rearrange("(f p) n -> p f n", p=128)
            lnr = sb(st, "ln2r", [128, 2, D], F32)
            k.dma("sp", lnr[:, 0, :], _bc_rows(ln2g_d, 128, D), writes=[lnr])
            k.dma("sp", lnr[:, 1, :], _bc_rows(ln2b_d, 128, D), writes=[lnr])
            g2row = sb(st, "g2row", [128, D], BF16)
            screp = sb(st, "screp2", [128, 8, 128], BF16)
            CP("dve", screp[:], scb[:].unsqueeze(2).to_broadcast([128, 8, 128]), [scb], [screp])
            u2T = sb(st, "u2T", [128, 8, 1024], BF16)
            actT = sb(st, "actT", [128, 22, 1024], BF16)
            actb = [Buf(f"act{f}", None) for f in range(22)]
            Wgu = [sb(st, f"Wgu{i}", [128, 8, 2, 128], BF16) for i in range(3)]
            sgf = [sb(st, f"f_sg{i}", [128, 512], F32) for i in range(2)]
            ybuf = [sb(st, f"f_y{i}", [128, D], F32) for i in range(2)]
            stat = sb(st, "l2stat", [128, 2, 6], F32)
            mv = sb(st, "l2mv", [128, 4], F32)
            brow2 = ybuf[0]
            k.dma("sp", brow2[0:1, :], brow_d[:, 5 * D:6 * D], writes=[brow2])
            wg2 = actT[:, 14:22, :]
            wg2b = actb[14:22]

            def side_work(f):
                if f == 0:
                    k.dma("pool", wg2, wada_d.rearrange("(c p) n -> p c n", p=128)[:, :, 5 * D:6 * D], writes=wg2b)
                if f == 3:
                    for nh in range(2):
                        pg_ = bank()
                        for c in range(8):
                            MM(pg_[:], screp[:, c, :], wg2[:, c, nh * 512:(nh + 1) * 512], c == 0, False, [screp] + wg2b, [pg_])
                        MM(pg_[:], ones1[:], brow2[0:1, nh * 512:(nh + 1) * 512], False, True, [ones1, brow2], [pg_])
                        ACT(g2row[:, nh * 512:(nh + 1) * 512], pg_[:], AF.Copy, [pg_], [g2row])
                if 2 <= f < 13:
                    f0 = 2 * (f - 2)
                    k.dma("pool", Wo[:, f0:f0 + 2, :], wfo_v[:, f0:f0 + 2, :], writes=Wob[f0:f0 + 2])
                if 6 <= f < 17:
                    f0 = 2 * (f - 6)
                    TT("dve", Wo[:, f0, :], Wo[:, f0, :], g2row[:], ALU.mult, [Wob[f0], g2row], [Wob[f0]])
                    TT("pool", Wo[:, f0 + 1, :], Wo[:, f0 + 1, :], g2row[:], ALU.mult, [Wob[f0 + 1], g2row], [Wob[f0 + 1]])
            wfi_v = wfi_d.rearrange("(c p) (g n) -> p c g n", p=128, g=2)
            wi = 0
            it = 0
            for sbk in range(2):
                for c in range(8):
                    for jb in range(2):
                        pt_ = bank()
                        for j in range(4):
                            tb = sbk * 8 + jb * 4 + j
                            TR(pt_[:, j * 128:(j + 1) * 128], x1_ap[:, tb, c * 128:(c + 1) * 128], identf[:], [x1b[tb], identf], [pt_], last=(j == 3))
                        ACT(u2T[:, c, jb * 512:(jb + 1) * 512], pt_[:], AF.Identity, [pt_, modp], [u2T],
                            bias=modp[:, 16 + c:17 + c], scale=modp[:, 24 + c:25 + c])
                for f in range(22):
                    w_ = Wgu[wi % 3]
                    wi += 1
                    k.dma("pool", w_[:, :, 0, :], wfi_v[:, :, 0, f * 128:(f + 1) * 128], writes=[w_])
                    k.dma("pool", w_[:, :, 1, :], wfi_v[:, :, 1, f * 128:(f + 1) * 128], writes=[w_])
                    if sbk == 0:
                        side_work(f)
                    for hh in range(2):
                        ph, pu = bank(), bank()
                        for c in range(8):
                            MM(ph[:], w_[:, c, 0, :], u2T[:, c, hh * 512:(hh + 1) * 512], c == 0, c == 7, [w_, u2T], [ph])
                        for c in range(8):
                            MM(pu[:], w_[:, c, 1, :], u2T[:, c, hh * 512:(hh + 1) * 512], c == 0, c == 7, [w_, u2T], [pu])
                        s_ = sgf[it % 2]
                        it += 1
                        ACT(s_[:], ph[:], AF.Silu, [ph], [s_])
                        TT("dve", actT[:, f, hh * 512:(hh + 1) * 512], s_[:], pu[:], ALU.mult, [s_, pu], [actb[f]])
                for j in range(8):
                    tb = sbk * 8 + j
                    yb = ybuf[tb % 2]
                    po = [bank(), bank()]
                    for nh in range(2):
                        for f in range(22):
                            MM(po[nh][:], actT[:, f, j * 128:(j + 1) * 128], Wo[:, f, nh * 512:(nh + 1) * 512], f == 0, f == 21, [actb[f], Wob[f]], [po[nh]])
                    for nh in range(2):
                        hs = slice(nh * 512, (nh + 1) * 512)
                        STT("dve", yb[:, hs], x1_ap[:, tb, hs], ALPHA, po[nh][:], ALU.mult, ALU.add, [x1b[tb], po[nh]], [yb])
                    for nh in range(2):
                        k.op("dve", lambda e, nh=nh, yb=yb: e.bn_stats(stat[:, nh, :], yb[:, nh * 512:(nh + 1) * 512]), [yb], [stat])
                    k.op("dve", lambda e: e.bn_aggr(mv[:, 0:2], stat[:]), [stat], [mv])
                    TS("dve", mv[:, 2:3], mv[:, 1:2], 1e-5, None, ALU.add, None, [mv], [mv])
                    k.op("dve", lambda e: e.reciprocal(mv[:, 2:3], mv[:, 2:3]), [mv], [mv])
                    ACT(mv[:, 2:3], mv[:, 2:3], AF.Sqrt, [mv], [mv])
                    TS("dve", yb[:], yb[:], mv[:, 0:1], mv[:, 2:3], ALU.subtract, ALU.mult, [yb, mv], [yb])
                    TT("pool", yb[:], yb[:], lnr[:, 0, :], ALU.mult, [yb, lnr], [yb])
                    TT("pool", yb[:], yb[:], lnr[:, 1, :], ALU.add, [yb, lnr], [yb])
                    k.dma("sp", y_d[tb * 128:(tb + 1) * 128, :], yb[:], reads=[yb], is_output=True)
        k.finish()
    return nc, tap_d


def _host_inputs(inputs):
    f = lambda a: np.ascontiguousarray(a, dtype=np.float32)
    sh = {}
    b = inputs["b_ada"][0]
    sh["w_ada"] = f(inputs["w_ada"][0])
    sh["b_pp"] = f(b.reshape(6, 8, 128)[[0, 1, 3, 4]].transpose(2, 0, 1).reshape(128, 32))
    sh["b_row"] = f(b.reshape(1, -1))
    sh["w_in"] = f(inputs["w_in"][0])
    sh["mu"] = f(inputs["mu_rw"][0].reshape(1, -1))
    for nm in ("rw_w0", "rw_a0", "rw_k_k", "rw_k_a", "rw_r_k", "rw_gn_g", "rw_gn_b", "gla_a_b", "gla_norm_g",
               "ln1_g", "ln1_b", "ln2_g", "ln2_b"):
        sh[nm] = f(inputs[nm][0].reshape(1, -1))
    for nm in ("rw_w2", "rw_a2", "rw_g2", "gla_a2", "w_rw_branch", "w_gla_branch", "w_mix_out", "w_ffn_in", "w_ffn_out"):
        sh[nm] = f(inputs[nm][0])
    sh["c_ident"] = np.eye(128, dtype=np.float32)
    s = np.arange(128)[:, None]
    t = np.arange(128)[None, :]
    bd = (s // 64) == (t // 64)
    sh["c_tri"] = f(np.stack([(s < t), (s <= t), (s > t), (s < t) & bd, (s > t) & bd, (s >= 64) & (t < 64)], axis=1).astype(np.float32))
    maps = []
    x = inputs["x"]
    c = inputs["c"]
    for bi in range(x.shape[0]):
        m = dict(sh)
        m["xT"] = f(x[bi].T)
        m["x"] = f(x[bi])
        m["cpp"] = f(c[bi].reshape(8, 128).T)
        maps.append(m)
    return maps


def kernel(**inputs):
    maps = _host_inputs(inputs)
    nc, _ = build_nc()
    res = run_bass_kernel_spmd(nc, maps, core_ids=list(range(len(maps))))
    return np.stack([np.asarray(r["y"], dtype=np.float32) for r in res.results], axis=0)
```

```python
import math
from contextlib import ExitStack

import numpy as np
import concourse.bass as bass
import concourse.mybir as mybir
from concourse.bass_utils import run_bass_kernel_spmd

F32 = mybir.dt.float32
BF16 = mybir.dt.bfloat16
AF = mybir.ActivationFunctionType
ALU = mybir.AluOpType
AX = mybir.AxisListType

D = 1024
T = 2048
NCH = T // 128
RW = 1792
GL = 1552
NIN = 5392
DFF = 2816
ALPHA = 2.0 ** 0.25
EM05 = math.exp(-0.5)


class Buf:
    __slots__ = ("name", "ap", "writer", "readers")

    def __init__(self, name, ap):
        self.name = name
        self.ap = ap
        self.writer = None
        self.readers = {}

    def __getitem__(self, key):
        return self.ap[key]


class KB:
    def __init__(self, nc, stack, n_dma_sems=8):
        self.nc = nc
        self.engs = {"pe": nc.tensor, "act": nc.scalar, "dve": nc.vector, "pool": nc.gpsimd, "sp": nc.sync}
        self.sem, self.cnt, self.waited = {}, {}, {}
        for e in self.engs:
            self.sem[e] = stack.enter_context(nc.semaphore("s_" + e))
            self.cnt[e] = 0
            self.waited[e] = {}
        self.dma_sems, self.dma_val, self.dma_rr = {}, {}, {}
        for q in ("sp", "act", "pool"):
            self.dma_sems[q] = [stack.enter_context(nc.semaphore(f"d_{q}{i}")) for i in range(n_dma_sems)]
            self.dma_val[q] = [0] * n_dma_sems
            self.dma_rr[q] = 0
        self.out_events = []
        self.pending = {}

    def _wait(self, eng, ev):
        sem, val, _ = ev
        if self.waited[eng].get(sem.name, 0) >= val:
            return
        self.engs[eng].wait_ge(sem, val)
        self.waited[eng][sem.name] = val

    def _collect(self, eng, reads, writes):
        evs = {}

        def add(ev, kind):
            if ev is None:
                return
            sem, val, src = ev
            if src == eng and (eng == "pe" or (kind == "war" and eng != "pool")):
                return
            if sem.name not in evs or evs[sem.name][1] < val:
                evs[sem.name] = ev
        for b in reads:
            add(b.writer, "raw")
        for b in writes:
            add(b.writer, "waw")
            for ev in b.readers.values():
                add(ev, "war")
        return evs

    def _record(self, ev, reads, writes):
        for b in reads:
            b.readers[ev[0].name] = ev
        for b in writes:
            b.writer = ev
            b.readers = {}

    def op(self, eng, fn, reads=(), writes=(), inc=True):
        for ev in self._collect(eng, reads, writes).values():
            self._wait(eng, ev)
        ins = fn(self.engs[eng])
        pend = self.pending.setdefault(eng, [])
        if not inc:
            pend.append((tuple(reads), tuple(writes)))
            return
        self.cnt[eng] += 1
        ins.then_inc(self.sem[eng], 1)
        ev = (self.sem[eng], self.cnt[eng], eng)
        for r_, w_ in pend:
            self._record(ev, r_, w_)
        pend.clear()
        self._record(ev, reads, writes)

    def dma(self, q, out, in_, reads=(), writes=(), is_output=False):
        for ev in self._collect(q, reads, writes).values():
            self._wait(q, ev)
        i = self.dma_rr[q]
        self.dma_rr[q] = (i + 1) % len(self.dma_sems[q])
        sem = self.dma_sems[q][i]
        prev = self.dma_val[q][i]
        if prev > 0:
            self._wait(q, (sem, prev, "dma"))
        ins = self.engs[q].dma_start(out=out, in_=in_)
        ins.then_inc(sem, 16)
        self.dma_val[q][i] = prev + 16
        ev = (sem, prev + 16, "dma")
        self._record(ev, reads, writes)
        if is_output:
            self.out_events.append(ev)

    def barrier(self):
        assert not any(self.pending.values()), "pending non-incrementing ops at barrier"
        evs = [(self.sem[e], self.cnt[e], e) for e in self.engs if self.cnt[e] > 0]
        for q in self.dma_sems:
            for s, v in zip(self.dma_sems[q], self.dma_val[q]):
                if v > 0:
                    evs.append((s, v, "dma"))
        for e in self.engs:
            for ev in evs:
                if ev[2] != e or e != "pe":
                    self._wait(e, ev)

    def finish(self):
        for ev in self.out_events:
            self._wait("sp", ev)
        self.final_counts = dict(self.cnt)
        KB.last = self


def _bc_rows(ap, nparts, n):
    return bass.AP(ap.tensor, ap.offset, [[0, nparts], [1, n]])


def build_nc(stop_after=99, taps=()):
    nc = bass.Bass("TRN2", target_bir_lowering=False)
    din = {}

    def inp(name, shape):
        din[name] = nc.dram_tensor(name, list(shape), F32, kind="ExternalInput").ap()
        return din[name]

    xT_d = inp("xT", [D, T])
    x_d = inp("x", [T, D])
    cpp_d = inp("cpp", [128, 8])
    wada_d = inp("w_ada", [D, 6 * D])
    bpp_d = inp("b_pp", [128, 32])
    brow_d = inp("b_row", [1, 6 * D])
    win_d = inp("w_in", [D, NIN])
    mu_d = inp("mu", [1, RW])
    w0_d = inp("rw_w0", [1, 512])
    a0_d = inp("rw_a0", [1, 512])
    w2_d = inp("rw_w2", [64, 512])
    a2_d = inp("rw_a2", [64, 512])
    g2_d = inp("rw_g2", [128, 512])
    kk_d = inp("rw_k_k", [1, 512])
    ka_d = inp("rw_k_a", [1, 512])
    rk_d = inp("rw_r_k", [1, 512])
    gng_d = inp("rw_gn_g", [1, 512])
    gnb_d = inp("rw_gn_b", [1, 512])
    ga2_d = inp("gla_a2", [16, 256])
    gab_d = inp("gla_a_b", [1, 256])
    gng2_d = inp("gla_norm_g", [1, 128])
    wbr_d = inp("w_rw_branch", [512, D])
    wbg_d = inp("w_gla_branch", [512, D])
    wmix_d = inp("w_mix_out", [D, D])
    ln1g_d = inp("ln1_g", [1, D])
    ln1b_d = inp("ln1_b", [1, D])
    wfi_d = inp("w_ffn_in", [D, 2 * DFF])
    wfo_d = inp("w_ffn_out", [DFF, D])
    ln2g_d = inp("ln2_g", [1, D])
    ln2b_d = inp("ln2_b", [1, D])
    cident_d = inp("c_ident", [128, 128])
    ctri_d = inp("c_tri", [128, 6, 128])
    y_d = nc.dram_tensor("y", [T, D], F32, kind="ExternalOutput").ap()
    tap_d = {}

    with ExitStack() as st0:
        k = KB(nc, st0)

        def sb(stack, name, shape, dt):
            t = stack.enter_context(nc.sbuf_tensor("sb_" + name, list(shape), dt))
            return Buf(name, t[:])

        def MM(out, lhsT, rhs, st, sp, R, W, last=None):
            k.op("pe", lambda e: e.matmul(out, lhsT, rhs, start=st, stop=sp), R, W, inc=(sp if last is None else last))

        def TR(out, in_, idn, R, W, last=True):
            k.op("pe", lambda e: e.transpose(out, in_, idn), R, W, inc=last)

        def ACT(out, in_, fn, R, W, bias=None, scale=None):
            kw = {}
            if bias is not None:
                kw["bias"] = bias
            if scale is not None:
                kw["scale"] = scale
            k.op("act", lambda e: e.activation(out, in_, fn, **kw), R, W)

        def TT(eng, out, a, b, op, R, W):
            if "nopool" in taps and eng == "pool":
                eng = "dve"
            k.op(eng, lambda e: e.tensor_tensor(out, a, b, op), R, W)

        def STT(eng, out, a, s, b, op0, op1, R, W):
            k.op(eng, lambda e: e.scalar_tensor_tensor(out, a, s, b, op0, op1), R, W)

        def TS(eng, out, a, s1, s2, op0, op1, R, W):
            if op1 is None:
                k.op(eng, lambda e: e.tensor_scalar(out, a, s1, None, op0), R, W)
            else:
                k.op(eng, lambda e: e.tensor_scalar(out, a, s1, s2, op0, op1), R, W)

        def CP(eng, out, in_, R, W):
            if "nopool" in taps and eng == "pool":
                eng = "dve"
            if eng == "act":
                ACT(out, in_, AF.Copy, R, W)
            else:
                k.op(eng, lambda e: e.tensor_copy(out, in_), R, W)

        def tap(name, ap, shape, reads):
            if name not in taps:
                return
            tap_d[name] = nc.dram_tensor("tap_" + name, list(shape), F32, kind="ExternalOutput").ap()
            k.dma("pool", tap_d[name], ap, reads=reads, is_output=True)

        banks = []
        for i in range(8):
            t = st0.enter_context(nc.psum_tensor(f"pb{i}", [128, 512], F32))
            banks.append(Buf(f"pb{i}", t[:]))
        bank_rr = [0]

        pinned = set()

        def bank(pin=False):
            for _ in range(8):
                b = banks[bank_rr[0]]
                bank_rr[0] = (bank_rr[0] + 1) % 8
                if b.name not in pinned:
                    if pin:
                        pinned.add(b.name)
                    return b
            raise RuntimeError("all PSUM banks pinned")

        def unpin(*bs):
            for b in bs:
                pinned.discard(b.name)

        big = sb(st0, "big", [128, 16400], F32)
        bigb = big.ap.bitcast(BF16)
        uT_ap = bigb[:, 0:16416].rearrange("p (c t) -> p c t", c=8)
        orwT_ap = bigb[:, 16416:24608].rearrange("p (c t) -> p c t", c=4)
        oglaT_ap = bigb[:, 24608:32800].rearrange("p (c t) -> p c t", c=4)
        x1_ap = big.ap[:, 0:16384].rearrange("p (b d) -> p b d", b=16)
        uTb = [Buf(f"uT{c}", None) for c in range(8)]
        orwTb = Buf("orwT", None)
        oglaTb = Buf("oglaT", None)
        x1b = [Buf(f"x1_{b}", None) for b in range(16)]

        identf = sb(st0, "identf", [128, 128], F32)
        identb = sb(st0, "identb", [128, 128], BF16)
        modp = sb(st0, "modp", [128, 32], F32)
        ones1 = sb(st0, "ones1", [1, 128], F32)
        scb = sb(st0, "scb", [128, 8], BF16)
        k.dma("sp", identf[:], cident_d, writes=[identf])
        k.dma("pool", identb[:], cident_d, writes=[identb])
        k.op("dve", lambda e: e.memset(ones1[:], 1.0), writes=[ones1])

        win_v = win_d.rearrange("(c p) n -> p c n", p=128)

        with ExitStack() as st:
            Nb = [[sb(st, f"N{h}{i}", [128, 4, 128], BF16) for i in range(2)] for h in range(2)]
            Lb = [[sb(st, f"L{h}{i}", [128, 4, 128], BF16) for i in range(2)] for h in range(2)]
            Sm = [sb(st, f"Sm{h}", [128, 4, 128], BF16) for h in range(2)]
            W1 = [sb(st, f"W1_{c}", [128, RW], BF16) for c in range(8)]
            W2 = [sb(st, f"W2_{c}", [128, RW], BF16) for c in range(8)]
            with ExitStack() as stp:
                cpp = sb(stp, "cpp", [128, 8], F32)
                bpp = sb(stp, "bpp", [128, 32], F32)
                wa = [sb(stp, f"wa{i}", [128, 8, 1024], BF16) for i in range(2)]
                mur = sb(stp, "mur", [128, RW], F32)
                omr = sb(stp, "omr", [128, RW], F32)
                stg = [sb(stp, f"stg{i}", [128, T], F32) for i in range(2)]
                k.dma("sp", cpp[:], cpp_d, writes=[cpp])
                k.dma("sp", bpp[:], bpp_d, writes=[bpp])
                ACT(scb[:], cpp[:], AF.Silu, [cpp], [scb])
                wada_v = wada_d.rearrange("(c p) n -> p c n", p=128)
                parts = (0, 1, 3, 4)
                for pi in range(2):
                    k.dma("pool", wa[pi][:], wada_v[:, :, parts[pi] * 1024:(parts[pi] + 1) * 1024], writes=[wa[pi]])
                k.dma("sp", mur[:], _bc_rows(mu_d, 128, RW), writes=[mur])
                TS("dve", omr[:], mur[:], -1.0, 1.0, ALU.mult, ALU.add, [mur], [omr])
                pm = bank(True)

                def p0_mm(pi):
                    w = wa[pi % 2]
                    for m in range(8):
                        col = pi * 8 + m
                        for c in range(8):
                            MM(pm[:, col:col + 1], w[:, c, m * 128:(m + 1) * 128], scb[:, c:c + 1], c == 0, c == 7, [w, scb], [pm], last=(m == 7 and c == 7))

                def prep(c):
                    w_ = stg[c % 2]
                    k.dma("sp", w_[:, 0:RW], win_v[:, c, 0:RW], writes=[w_])
                    TT("dve", W1[c][:], w_[:, 0:RW], omr[:], ALU.mult, [w_, omr], [W1[c]])
                    TT("pool", W2[c][:], w_[:, 0:RW], mur[:], ALU.mult, [w_, mur], [W2[c]])
                for c in range(4):
                    prep(c)
                p0_mm(0)
                p0_mm(1)
                for pi in range(2, 4):
                    k.dma("pool", wa[pi % 2][:], wada_v[:, :, parts[pi] * 1024:(parts[pi] + 1) * 1024], writes=[wa[pi % 2]])
                for c in range(4, 8):
                    prep(c)
                p0_mm(2)
                p0_mm(3)
                TT("dve", modp[:], pm[:, 0:32], bpp[:], ALU.add, [pm, bpp], [modp])
                unpin(pm)
                TS("dve", modp[:, 8:16], modp[:, 8:16], 1.0, None, ALU.add, None, [modp], [modp])
                TS("dve", modp[:, 24:32], modp[:, 24:32], 1.0, None, ALU.add, None, [modp], [modp])
                tap("modp", modp[:], [128, 32], [modp])
                for c in range(8):
                    s_ = stg[c % 2]
                    k.dma("sp", s_[:], xT_d[c * 128:(c + 1) * 128, :], writes=[s_])
                    k.op("dve", lambda e, c=c: e.memset(uT_ap[:, c, 0:1], 0.0), writes=[uTb[c]])
                    ACT(uT_ap[:, c, 1:T + 1], s_[:], AF.Identity, [s_, modp], [uTb[c]],
                        bias=modp[:, c:c + 1], scale=modp[:, 8 + c:9 + c])
                tap("uT", uT_ap[:, :, 1:T + 1], [128, 8, T], uTb)
                k.barrier()

            rows = sb(st, "rwrows", [128, 5, 512], F32)
            w0r = sb(st, "w0r", [1, 512], F32)
            a0r = sb(st, "a0r", [1, 512], F32)
            w2b = sb(st, "w2b", [128, 512], BF16)
            a2b = sb(st, "a2b", [128, 512], BF16)
            g2b = sb(st, "g2b", [128, 512], BF16)
            Mtri = sb(st, "Mtri", [128, 3, 128], F32)
            negcol = sb(st, "negcol", [128, 1], F32)
            mSI = sb(st, "mSI", [128, 2, 2, 128], F32)
            mND = sb(st, "mND", [128, 2, 128], F32)
            mSL4 = sb(st, "mSL4", [128, 2, 4, 128], F32)
            idb4 = sb(st, "idb4", [128, 4, 128], BF16)
            id8f = sb(st, "id8f", [64, 8, 64], F32)
            Hb = [sb(st, f"Hb{i}", [128, 8, 64], BF16) for i in range(2)]
            for i, d_ in enumerate((kk_d, ka_d, rk_d, gng_d, gnb_d)):
                k.dma("sp", rows[:, i, :], _bc_rows(d_, 128, 512), writes=[rows])
            k.dma("sp", w0r[:], w0_d, writes=[w0r])
            k.dma("sp", a0r[:], a0_d, writes=[a0r])
            k.op("dve", lambda e: e.memset(w2b[:], 0.0), writes=[w2b])
            k.op("dve", lambda e: e.memset(a2b[:], 0.0), writes=[a2b])
            k.dma("pool", w2b[0:64, :], w2_d, writes=[w2b])
            k.dma("pool", a2b[64:128, :], a2_d, writes=[a2b])
            k.dma("pool", g2b[:], g2_d, writes=[g2b])
            k.dma("sp", Mtri[:], ctri_d[:, 0:3, :], writes=[Mtri])
            TS("dve", Mtri[:], Mtri[:], -EM05, None, ALU.mult, None, [Mtri], [Mtri])
            k.op("dve", lambda e: e.memset(negcol[:], -EM05), writes=[negcol])
            for h2 in range(2):
                k.dma("sp", mSI[:, h2, 0, :], ctri_d[:, 0, :], writes=[mSI])
                k.dma("sp", mND[:, h2, :], ctri_d[:, 3, :], writes=[mND])
                k.dma("sp", mSI[:, h2, 1, :], ctri_d[:, 1, :], writes=[mSI])
            for h4 in range(4):
                k.dma("sp", mSL4[:, 0, h4, :], ctri_d[:, 4, :], writes=[mSL4])
                k.dma("sp", mSL4[:, 1, h4, :], ctri_d[:, 5, :], writes=[mSL4])
                k.dma("pool", idb4[:, h4, :], cident_d, writes=[idb4])
            for h in range(8):
                k.dma("sp", id8f[:, h, :], cident_d[0:64, 0:64], writes=[id8f])
            k.op("dve", lambda e: e.memset(Hb[0][:], 0.0), writes=[Hb[0]])
            k.op("dve", lambda e: e.memset(Hb[1][:], 0.0), writes=[Hb[1]])

            def f32t(name):
                return sb(st, name, [128, 512], F32)
            sgm, a_t, g_t, r_t, k_t, v_t = [f32t(n) for n in ("sgm", "a_t", "g_t", "r_t", "k_t", "v_t")]
            kkn, kmod, bvec = f32t("kkn"), f32t("kmod"), f32t("bvec")
            S0 = f32t("S0")
            ogl_f = big.ap[:, 12304:16400]
            EinT = Buf("EinT", ogl_f[:, 0:512].rearrange("p (c t) -> p c t", c=4))
            EninT = Buf("EninT", ogl_f[:, 512:1024].rearrange("p (c t) -> p c t", c=4))
            EexT = Buf("EexT", ogl_f[:, 1024:1536].rearrange("p (c t) -> p c t", c=4))
            bon = Buf("bon", ogl_f[:, 1536:2048])
            S1 = Buf("S1", ogl_f[:, 2048:2560])
            S2 = Buf("S2", ogl_f[:, 2560:3072])
            Eex = Buf("Eex", ogl_f[:, 3072:3584])
            Erev = Buf("Erev", ogl_f[:, 3584:4096])
            v_bf = sb(st, "v_bf", [128, 512], BF16)
            twad = sb(st, "twad", [128, 128], BF16)
            sgT = sb(st, "sgT", [128, 128], BF16)
            small = sb(st, "small", [128, 6, 8], F32)
            X = sb(st, "X", [128, 8, 2, 64], BF16)
            Bh = sb(st, "Bh", [128, 512], BF16)
            Kh = sb(st, "Kh", [128, 512], BF16)
            AR = sb(st, "AR", [128, 4, 2, 128], BF16)
            BTz = sb(st, "BTz", [128, 4, 2, 128], BF16)
            KTz = sb(st, "KTz", [128, 4, 2, 128], BF16)
            k.op("dve", lambda e: e.memset(BTz[:], 0.0), writes=[BTz])
            k.op("dve", lambda e: e.memset(KTz[:], 0.0), writes=[KTz])
            gC = sb(st, "gC", [64, 8], F32)
            ArbT = [sb(st, f"ArbT{h}", [128, 4, 128], BF16) for h in range(2)]
            MakT = [sb(st, f"MakT{h}", [128, 4, 128], BF16) for h in range(2)]
            ArkT = [sb(st, f"ArkT{h}", [128, 4, 128], BF16) for h in range(2)]
            WU = sb(st, "WU", [128, 8, 2, 64], BF16)
            Dg = sb(st, "Dg", [64, 8, 64], F32)
            PTb = sb(st, "PTb", [128, 8, 64], BF16)
            QeT = sb(st, "QeT", [128, 8, 128], BF16)
            k.op("dve", lambda e: e.memset(PTb[:], 0.0), writes=[PTb])
            k.op("dve", lambda e: e.memset(QeT[:], 0.0), writes=[QeT])
            o_bf = sb(st, "o_bf", [128, 512], BF16)

            def v3(ap, a):
                return ap.rearrange("p (a b) -> p a b", a=a)

            def bfv(b_, half):
                return b_.ap.bitcast(BF16)[:, half * 512:(half + 1) * 512].rearrange("p (c t) -> p c t", c=4)
            alias3 = [(bfv(sgm, hf_), bfv(a_t, hf_), bfv(k_t, hf_)) for hf_ in range(2)]
            aliasb = [Buf(f"alias{hf_}", None) for hf_ in range(2)]

            def hv(b_, kind):
                out = []
                for hf_ in range(2):
                    if kind == "tok":
                        ap_ = b_.ap[:, hf_ * 256:(hf_ + 1) * 256]
                    elif kind == "ch":
                        ap_ = b_.ap[:, 2 * hf_:2 * hf_ + 2]
                    elif kind == "hd":
                        ap_ = b_.ap[:, 4 * hf_:4 * hf_ + 4]
                    else:
                        ap_ = b_.ap[:, :, 4 * hf_:4 * hf_ + 4]
                    out.append(Buf(f"{b_.name}_{hf_}", ap_))
                return out
            sgmH, a_tH, g_tH, r_tH, k_tH, v_tH = [hv(b_, "tok") for b_ in (sgm, a_t, g_t, r_t, k_t, v_t)]
            kknH, kmodH, bvecH, S0H, S1H, S2H = [hv(b_, "tok") for b_ in (kkn, kmod, bvec, S0, S1, S2)]
            EexH, ErevH, bonH, v_bfH, BhH, KhH, o_bfH = [hv(b_, "tok") for b_ in (Eex, Erev, bon, v_bf, Bh, Kh, o_bf)]
            EinTH, EninTH, EexTH, ARH, BTzH, KTzH = [hv(b_, "ch") for b_ in (EinT, EninT, EexT, AR, BTz, KTz)]
            XH, WUH, DgH, PTbH, QeTH, gCH = [hv(b_, "hd") for b_ in (X, WU, Dg, PTb, QeT, gC)]
            HbH = [hv(b_, "hd") for b_ in Hb]
            smallH = hv(small, "sm")
            orwTH = [Buf(f"orwT_{hf_}", None) for hf_ in range(2)]

            nch_run = NCH if "rw_short" not in taps else 2
            if "rw_cut0" in taps:
                nch_run = 0
            def PROJ(n):
                t0 = n * 128
                ucur = [uT_ap[:, c, t0 + 1:t0 + 129] for c in range(8)]
                uprv = [uT_ap[:, c, t0:t0 + 128] for c in range(8)]

                def proj_tok(pb_, c0, c1):
                    for c in range(8):
                        MM(pb_[:, 0:c1 - c0], ucur[c], W1[c][:, c0:c1], c == 0, False, [uTb[c], W1[c]], [pb_])
                        MM(pb_[:, 0:c1 - c0], uprv[c], W2[c][:, c0:c1], False, c == 7, [uTb[c], W2[c]], [pb_])

                def proj_ch(out_ap, pb_, c0, c1):
                    for c in range(8):
                        MM(out_ap, W1[c][:, c0:c1], ucur[c], c == 0, False, [uTb[c], W1[c]], [pb_])
                        MM(out_ap, W2[c][:, c0:c1], uprv[c], False, c == 7, [uTb[c], W2[c]], [pb_])

                pL = bank()
                pLv = v3(pL[:], 4)
                proj_ch(pLv[:, 0, :], pL, 1536, 1664)
                proj_ch(pLv[:, 2, :], pL, 1664, 1792)
                ACT(twad[0:64, :], pLv[0:64, 0, :], AF.Tanh, [pL], [twad])
                ACT(twad[64:128, :], pLv[64:128, 0, :], AF.Copy, [pL], [twad])
                ACT(sgT[:], pLv[:, 2, :], AF.Sigmoid, [pL], [sgT])
                pR, pK, pV = bank(True), bank(True), bank(True)
                proj_tok(pR, 0, 512)
                proj_tok(pK, 512, 1024)
                proj_tok(pV, 1024, 1536)
                pW, pA, pG = bank(True), bank(True), bank(True)
                MM(pW[:], twad[:], w2b[:], True, False, [twad, w2b], [pW])
                MM(pW[:], ones1[:], w0r[:], False, True, [ones1, w0r], [pW])
                MM(pA[:], twad[:], a2b[:], True, False, [twad, a2b], [pA])
                MM(pA[:], ones1[:], a0r[:], False, True, [ones1, a0r], [pA])
                MM(pG[:], sgT[:], g2b[:], True, True, [sgT, g2b], [pG])
                return pR, pK, pV, pW, pA, pG

            nxt = PROJ(0) if nch_run > 0 else None
            e1_done = False
            for n in range(nch_run):
                t0 = n * 128
                if not (n > 0 and e1_done):
                    pR, pK, pV, pW, pA, pG = nxt
                Hc, Hn = HbH[n % 2], HbH[(n + 1) % 2]
                pg = bank(True)
                pCs, pTs, pYs = [None, None], [None, None], [None, None]
                v4 = lambda ap: ap.rearrange("p (a b) -> p a b", a=4)
                cs_ = lambda hf: slice(hf * 256, hf * 256 + 256)

                def E1(hf):
                    if hf == 1:
                        return
                    ACT(sgm[:], pW[:], AF.Sigmoid, [pW], sgmH)
                    CP("dve", k_t[:], pK[:], [pK], k_tH)
                    ACT(a_t[:], pA[:], AF.Sigmoid, [pA], a_tH)
                    ACT(r_t[:], pR[:], AF.Copy, [pR], r_tH)
                    ACT(v_t[:], pV[:], AF.Copy, [pV], v_tH)
                    CP("pool", v_bf[:], v_t[:], v_tH, v_bfH)
                    unpin(pR, pK, pV, pW, pA)

                def Eg():
                    ACT(g_t[:], pG[:], AF.Copy, [pG], g_tH)
                    unpin(pG)

                def C1(hf):
                    s_ = sgmH[hf]
                    pC, pT = bank(True), bank(True)
                    MM(pC[:, 0:256], Mtri[:, 0, :], s_[:], True, True, [Mtri, s_], [pC], last=False)
                    MM(pC[:, 256:512], Mtri[:, 2, :], s_[:], True, True, [Mtri, s_], [pC])
                    for i in range(2):
                        MM(v3(pT[:], 4)[:, i, :], s_[:, i * 128:(i + 1) * 128], Mtri[:, 1, :], True, True, [Mtri, s_], [pT], last=False)
                        MM(v3(pT[:], 4)[:, 2 + i, :], s_[:, i * 128:(i + 1) * 128], Mtri[:, 0, :], True, True, [Mtri, s_], [pT], last=(i == 1))
                    for j in range(4):
                        MM(pg[0:64, 4 * hf + j:4 * hf + j + 1], s_[:, j * 64:(j + 1) * 64], negcol[:], True, True, [s_, negcol], [pg], last=(j == 3))
                    pCs[hf], pTs[hf] = pC, pT

                def X1(hf):
                    pC, pT = pCs[hf], pTs[hf]
                    ACT(EexH[hf][:], pC[:, 0:256], AF.Exp, [pC], [EexH[hf]])
                    yield
                    ACT(ErevH[hf][:], pC[:, 256:512], AF.Exp, [pC], [ErevH[hf]])
                    yield
                    ACT(EinTH[hf][:], v3(pT[:], 4)[:, 0:2, :], AF.Exp, [pT], [EinTH[hf]])
                    yield
                    ACT(EninTH[hf][:], v3(pT[:], 4)[:, 0:2, :], AF.Exp, [pT], [EninTH[hf]], scale=-1.0)
                    yield
                    ACT(EexTH[hf][:], v3(pT[:], 4)[:, 2:4, :], AF.Exp, [pT], [EexTH[hf]])
                    yield
                    ACT(gCH[hf][:], pg[0:64, 4 * hf:4 * hf + 4], AF.Exp, [pg], [gCH[hf]])
                    unpin(pC, pT)
                    if hf == 1:
                        unpin(pg)

                def K1(hf):
                    cs, sm = cs_(hf), smallH[hf]
                    TT("pool", S0H[hf][:], k_tH[hf][:], rows[:, 0, cs], ALU.mult, [k_tH[hf], rows], [S0H[hf]])
                    yield
                    TT("pool", S1H[hf][:], S0H[hf][:], S0H[hf][:], ALU.mult, [S0H[hf]], [S1H[hf]])
                    yield
                    k.op("dve", lambda e: e.tensor_reduce(sm[:, 0, :], v4(S1H[hf][:]), AX.X, ALU.add), [S1H[hf]], [sm])
                    yield
                    TS("dve", sm[:, 1, :], sm[:, 0, :], 1e-24, None, ALU.add, None, [sm], [sm])
                    yield
                    k.op("dve", lambda e: e.reciprocal(sm[:, 1, :], sm[:, 1, :]), [sm], [sm])
                    yield
                    ACT(sm[:, 1, :], sm[:, 1, :], AF.Sqrt, [sm], [sm])
                    yield
                    TT("dve", v4(kknH[hf][:]), v4(S0H[hf][:]), sm[:, 1, :].unsqueeze(2).to_broadcast([128, 4, 64]), ALU.mult, [S0H[hf], sm], [kknH[hf]])

                def A1(hf):
                    cs = cs_(hf)
                    STT("dve", S2H[hf][:], a_tH[hf][:], -1.0, rows[:, 1, cs], ALU.add, ALU.mult, [a_tH[hf], rows], [S2H[hf]])
                    yield
                    STT("dve", kmodH[hf][:], S2H[hf][:], 1.0, k_tH[hf][:], ALU.add, ALU.mult, [S2H[hf], k_tH[hf]], [kmodH[hf]])
                    yield
                    TT("pool", bvecH[hf][:], kknH[hf][:], a_tH[hf][:], ALU.mult, [kknH[hf], a_tH[hf]], [bvecH[hf]])

                def O1(hf):
                    STT("dve", XH[hf][:, :, 0, :], v4(kknH[hf][:]), -1.0, v4(EexH[hf][:]), ALU.mult, ALU.mult, [kknH[hf], EexH[hf]], [XH[hf]])
                    yield
                    TT("dve", BhH[hf][:], bvecH[hf][:], ErevH[hf][:], ALU.mult, [bvecH[hf], ErevH[hf]], [BhH[hf]])
                    yield
                    TT("pool", KhH[hf][:], kmodH[hf][:], ErevH[hf][:], ALU.mult, [kmodH[hf], ErevH[hf]], [KhH[hf]])

                def B1(hf):
                    cs, sm = cs_(hf), smallH[hf]
                    TT("pool", S1H[hf][:], r_tH[hf][:], kmodH[hf][:], ALU.mult, [r_tH[hf], kmodH[hf]], [S1H[hf]])
                    yield
                    TT("pool", S1H[hf][:], S1H[hf][:], rows[:, 2, cs], ALU.mult, [S1H[hf], rows], [S1H[hf]])
                    yield
                    k.op("dve", lambda e: e.tensor_reduce(sm[:, 2, :], v4(S1H[hf][:]), AX.X, ALU.add), [S1H[hf]], [sm])
                    yield
                    TT("dve", v4(bonH[hf][:]), v4(v_tH[hf][:]), sm[:, 2, :].unsqueeze(2).to_broadcast([128, 4, 64]), ALU.mult, [v_tH[hf], sm], [bonH[hf]])

                def T1(hf):
                    pA_, pB_ = bank(True), bank(True)
                    for i in range(2):
                        sl = slice(i * 128, (i + 1) * 128)
                        TR(v3(pA_[:], 4)[:, i, :], kknH[hf][:, sl], identf[:], [kknH[hf], identf], [pA_], last=False)
                        TR(v3(pA_[:], 4)[:, 2 + i, :], r_tH[hf][:, sl], identf[:], [r_tH[hf], identf], [pA_], last=(i == 1))
                        TR(v3(pB_[:], 4)[:, i, :], bvecH[hf][:, sl], identf[:], [bvecH[hf], identf], [pB_], last=False)
                        TR(v3(pB_[:], 4)[:, 2 + i, :], kmodH[hf][:, sl], identf[:], [kmodH[hf], identf], [pB_], last=(i == 1))
                    ARh = ARH[hf]
                    yield
                    STT("dve", ARh[:, :, 0, :], v3(pA_[:], 4)[:, 0:2, :], -1.0, EexTH[hf][:], ALU.mult, ALU.mult, [pA_, EexTH[hf]], [ARh])
                    yield
                    TT("dve", ARh[:, :, 1, :], v3(pA_[:], 4)[:, 2:4, :], EinTH[hf][:], ALU.mult, [pA_, EinTH[hf]], [ARh])
                    yield
                    for hh in range(2):
                        ps_ = slice(hh * 64, hh * 64 + 64)
                        TT("dve", BTzH[hf][ps_, :, hh, :], v3(pB_[:], 4)[ps_, 0:2, :], EninTH[hf][ps_], ALU.mult, [pB_, EninTH[hf]], [BTzH[hf]])
                        TT("dve", KTzH[hf][ps_, :, hh, :], v3(pB_[:], 4)[ps_, 2:4, :], EninTH[hf][ps_], ALU.mult, [pB_, EninTH[hf]], [KTzH[hf]])
                    unpin(pA_, pB_)

                def I1(hf):
                    pLA = [bank(), bank()]
                    pMA = [bank(), bank()]
                    pLL = bank()
                    for j in range(4):
                        c4l, hh = j // 2, j % 2
                        ar_rhs = ARH[hf][:, c4l, :, :].rearrange("p a t -> p (a t)")
                        o1 = pLA[j // 2][:, (j % 2) * 256:(j % 2) * 256 + 256]
                        o2 = pMA[j // 2][:, (j % 2) * 256:(j % 2) * 256 + 256]
                        MM(o1, BTzH[hf][:, c4l, hh, :], ar_rhs, True, True, [BTzH[hf], ARH[hf]], [pLA[j // 2]], last=(j == 3))
                        MM(o2, KTzH[hf][:, c4l, hh, :], ar_rhs, True, True, [KTzH[hf], ARH[hf]], [pMA[j // 2]], last=(j == 3))
                        MM(pLL[:, j * 128:(j + 1) * 128], ARH[hf][:, c4l, 0, :], BTzH[hf][:, c4l, hh, :], True, True, [ARH[hf], BTzH[hf]], [pLL], last=(j == 3))
                    N0, L0 = Nb[hf][0], Lb[hf][0]
                    for q in range(2):
                        src = pLA[q][:].rearrange("p (h a t) -> p h a t", h=2, a=2)
                        TT("dve", N0[:, 2 * q:2 * q + 2, :], src[:, :, 0, :], mND[:], ALU.mult, [pLA[q], mND], [N0])
                        TT("dve", ArbT[hf][:, 2 * q:2 * q + 2, :], src[:, :, 1, :], mSI[:, :, 1, :], ALU.mult, [pLA[q], mSI], [ArbT[hf]])
                        src2 = pMA[q][:].rearrange("p (h a t) -> p h a t", h=2, a=2)
                        TT("dve", MakT[hf][:, 2 * q:2 * q + 2, :], src2[:, :, 0, :], mSI[:, :, 0, :], ALU.mult, [pMA[q], mSI], [MakT[hf]])
                        TT("dve", ArkT[hf][:, 2 * q:2 * q + 2, :], src2[:, :, 1, :], mSI[:, :, 1, :], ALU.mult, [pMA[q], mSI], [ArkT[hf]])
                    TT("dve", L0[:], v3(pLL[:], 4), mSL4[:, 0], ALU.mult, [pLL, mSL4], [L0])
                    TT("dve", bfv(sgm, hf), v3(pLL[:], 4), mSL4[:, 1], ALU.mult, [pLL, mSL4], [sgmH[hf]])
                    TT("pool", Sm[hf][:], N0[:], idb4[:], ALU.add, [N0, idb4], [Sm[hf]])

                def NLa(hf, lev):
                    cur = lev % 2
                    Nc, Lc = Nb[hf][cur], Lb[hf][cur]
                    Nn, Ln = Nb[hf][1 - cur], Lb[hf][1 - cur]
                    pL2 = bank()
                    for j in range(4):
                        MM(v3(pL2[:], 4)[:, j, :], Nc[:, j, :], Lc[:, j, :], True, True, [Nc, Lc], [pL2], last=(j == 3))
                    if lev < 4:
                        pN2 = bank()
                        for j in range(4):
                            MM(v3(pN2[:], 4)[:, j, :], Lc[:, j, :], Nc[:, j, :], True, True, [Nc, Lc], [pN2], last=(j == 3))
                    ACT(Ln[:], v3(pL2[:], 4), AF.Copy, [pL2], [Ln])
                    if lev < 4:
                        CP("dve", Nn[:], v3(pN2[:], 4), [pN2], [Nn])

                def NLb(hf, lev):
                    Ln = Lb[hf][1 - (lev % 2)]
                    pS = bank()
                    for j in range(4):
                        MM(v3(pS[:], 4)[:, j, :], Ln[:, j, :], Sm[hf][:, j, :], True, False, [Ln, Sm[hf]], [pS])
                        MM(v3(pS[:], 4)[:, j, :], identb[:], Sm[hf][:, j, :], False, True, [identb, Sm[hf]], [pS], last=(j == 3))
                    CP("act" if hf == 0 else "dve", Sm[hf][:], v3(pS[:], 4), [pS], [Sm[hf]])

                def MGa(hf):
                    Lo_ap, Tm_ap, Zb_ap = bfv(sgm, hf), bfv(a_t, hf), bfv(k_t, hf)
                    pTt = bank()
                    pTtb = pTt[:].bitcast(BF16)[:, 0:512].rearrange("p (c t) -> p c t", c=4)
                    for j in range(4):
                        TR(pTtb[:, j, :], Sm[hf][:, j, :], identb[:], [Sm[hf], identb], [pTt], last=(j == 3))
                    ACT(Tm_ap, pTtb, AF.Copy, [pTt], [a_tH[hf]])
                    pZ_ = bank()
                    for j in range(4):
                        MM(v3(pZ_[:], 4)[:, j, :], Lo_ap[:, j, :], Sm[hf][:, j, :], True, True, [sgmH[hf], Sm[hf]], [pZ_], last=(j == 3))
                    CP("dve", Zb_ap, v3(pZ_[:], 4), [pZ_], [k_tH[hf]])

                def MGb(hf):
                    Tm_ap, Zb_ap = bfv(a_t, hf), bfv(k_t, hf)
                    pS = bank()
                    for j in range(4):
                        MM(v3(pS[:], 4)[:, j, :], Tm_ap[:, j, :], Zb_ap[:, j, :], True, False, [a_tH[hf], k_tH[hf]], [pS])
                        MM(v3(pS[:], 4)[:, j, :], identb[:], Sm[hf][:, j, :], False, True, [identb, Sm[hf]], [pS], last=(j == 3))
                    ACT(Sm[hf][:], v3(pS[:], 4), AF.Copy, [pS], [Sm[hf]])

                def P1a(hf):
                    pMV = bank()
                    for j in range(4):
                        MM(pMV[:, j * 64:(j + 1) * 64], MakT[hf][:, j, :], v_bfH[hf][:, j * 64:(j + 1) * 64], True, True, [MakT[hf], v_bfH[hf]], [pMV], last=(j == 3))
                    ACT(XH[hf][:, :, 1, :], v4(pMV[:, 0:256]), AF.Copy, [pMV], [XH[hf]])

                def P1b(hf):
                    pWU = bank()
                    for j in range(4):
                        MM(pWU[:, j * 128:(j + 1) * 128], Sm[hf][:, j, :], XH[hf][:, j, :, :].rearrange("p a b -> p (a b)"), True, True, [Sm[hf], XH[hf]], [pWU], last=(j == 3))
                    ACT(WUH[hf][:].rearrange("p h a b -> p (h a b)"), pWU[:], AF.Copy, [pWU], [WUH[hf]])

                def P1c(hf):
                    pP = bank(True)
                    yield
                    for j in range(4):
                        MM(pP[0:64, j * 64:(j + 1) * 64], WUH[hf][:, j, 0, :], BhH[hf][:, j * 64:(j + 1) * 64], True, True, [WUH[hf], BhH[hf]], [pP], last=(j == 3))
                    yield
                    TT("pool", DgH[hf][:], id8f[:, 0:4, :], gCH[hf][:].unsqueeze(2).to_broadcast([64, 4, 64]), ALU.mult, [id8f, gCH[hf]], [DgH[hf]])
                    yield
                    TT("dve", PTbH[hf][0:64], v4(pP[0:64, 0:256]), DgH[hf][:], ALU.add, [pP, DgH[hf]], [PTbH[hf]])
                    yield
                    pQ = bank(True)
                    yield
                    for j in range(4):
                        p0 = (j % 2) * 64
                        oq = pQ[0:64, j * 128:(j + 1) * 128]
                        MM(oq, WUH[hf][:, j, 0, :], ArbT[hf][:, j, :], True, False, [WUH[hf], ArbT[hf]], [pQ])
                        MM(oq, identb[:, p0:p0 + 64], ARH[hf][:, j // 2, 1, :], False, True, [identb, ARH[hf]], [pQ], last=(j == 3))
                    yield
                    ACT(QeTH[hf][0:64], v3(pQ[0:64, :], 4), AF.Copy, [pQ], [QeTH[hf]])
                    unpin(pP, pQ)

                def P1d(hf):
                    pY = bank(True)
                    yield
                    for j in range(4):
                        oy = pY[:, j * 64:(j + 1) * 64]
                        vj = v_bfH[hf][:, j * 64:(j + 1) * 64]
                        MM(oy, QeTH[hf][:, j, :], Hc[hf][:, j, :], True, False, [QeTH[hf], Hc[hf]], [pY])
                        MM(oy, ArbT[hf][:, j, :], WUH[hf][:, j, 1, :], False, False, [ArbT[hf], WUH[hf]], [pY])
                        MM(oy, ArkT[hf][:, j, :], vj, False, True, [ArkT[hf], v_bfH[hf]], [pY], last=(j == 3))
                    yield
                    pH = bank(True)
                    yield
                    for j in range(4):
                        oh = pH[0:64, j * 64:(j + 1) * 64]
                        vj = v_bfH[hf][:, j * 64:(j + 1) * 64]
                        MM(oh, PTbH[hf][:, j, :], Hc[hf][:, j, :], True, False, [PTbH[hf], Hc[hf]], [pH])
                        MM(oh, BhH[hf][:, j * 64:(j + 1) * 64], WUH[hf][:, j, 1, :], False, False, [BhH[hf], WUH[hf]], [pH])
                        MM(oh, KhH[hf][:, j * 64:(j + 1) * 64], vj, False, True, [KhH[hf], v_bfH[hf]], [pH], last=(j == 3))
                    yield
                    ACT(Hn[hf][0:64], v4(pH[0:64, 0:256]), AF.Copy, [pH], [Hn[hf]])
                    unpin(pH)
                    pYs[hf] = pY

                def F1(hf):
                    cs, sm, pY = cs_(hf), smallH[hf], pYs[hf]
                    y_, q_ = S0H[hf], S2H[hf]
                    bc = lambda r_: sm[:, r_, :].unsqueeze(2).to_broadcast([128, 4, 64])
                    ACT(y_[:], pY[:, 0:256], AF.Copy, [pY], [y_])
                    unpin(pY)
                    yield
                    k.op("dve", lambda e: e.tensor_reduce(sm[:, 3, :], v4(y_[:]), AX.X, ALU.add), [y_], [sm])
                    yield
                    TT("pool", q_[:], y_[:], y_[:], ALU.mult, [y_], [q_])
                    yield
                    k.op("dve", lambda e: e.tensor_reduce(sm[:, 4, :], v4(q_[:]), AX.X, ALU.add), [q_], [sm])
                    yield
                    TS("dve", sm[:, 3, :], sm[:, 3, :], 1.0 / 64, None, ALU.mult, None, [sm], [sm])
                    yield
                    TT("dve", sm[:, 5, :], sm[:, 3, :], sm[:, 3, :], ALU.mult, [sm], [sm])
                    yield
                    STT("dve", sm[:, 4, :], sm[:, 4, :], 1.0 / 64, sm[:, 5, :], ALU.mult, ALU.subtract, [sm], [sm])
                    yield
                    TS("dve", sm[:, 4, :], sm[:, 4, :], 64e-5, None, ALU.add, None, [sm], [sm])
                    yield
                    k.op("dve", lambda e: e.reciprocal(sm[:, 4, :], sm[:, 4, :]), [sm], [sm])
                    yield
                    ACT(sm[:, 4, :], sm[:, 4, :], AF.Sqrt, [sm], [sm])

                def F1b(hf):
                    cs, sm = cs_(hf), smallH[hf]
                    y_ = S0H[hf]
                    bc = lambda r_: sm[:, r_, :].unsqueeze(2).to_broadcast([128, 4, 64])
                    TT("dve", v4(y_[:]), v4(y_[:]), bc(3), ALU.subtract, [y_, sm], [y_])
                    yield
                    TT("dve", v4(y_[:]), v4(y_[:]), bc(4), ALU.mult, [y_, sm], [y_])
                    yield
                    TT("pool", y_[:], y_[:], rows[:, 3, cs], ALU.mult, [y_, rows], [y_])
                    yield
                    TT("pool", y_[:], y_[:], rows[:, 4, cs], ALU.add, [y_, rows], [y_])
                    yield
                    TT("pool", y_[:], y_[:], bonH[hf][:], ALU.add, [y_, bonH[hf]], [y_])
                    yield
                    TT("pool", o_bfH[hf][:], y_[:], g_tH[hf][:], ALU.mult, [y_, g_tH[hf]], [o_bfH[hf]])
                    yield
                    pO = bank(True)
                    pOb = pO[:].bitcast(BF16)[:, 0:256].rearrange("p (c t) -> p c t", c=2)
                    yield
                    for i in range(2):
                        TR(pOb[:, i, :], o_bfH[hf][:, i * 128:(i + 1) * 128], identb[:], [o_bfH[hf], identb], [pO], last=(i == 1))
                    yield
                    ACT(orwT_ap[:, 2 * hf:2 * hf + 2, t0:t0 + 128], pOb, AF.Copy, [pO], [orwTH[hf]])
                    unpin(pO)

                def both(fn, *a):
                    gens = [fn(hf_, *a) for hf_ in range(2)]
                    gens = [g for g in gens if g is not None]
                    while gens:
                        for g in list(gens):
                            try:
                                next(g)
                            except StopIteration:
                                gens.remove(g)
                if not (n > 0 and e1_done):
                    both(E1)
                Eg()
                for fn in (C1, K1, X1, A1, O1, B1, T1, I1):
                    both(fn)
                for lev in range(5):
                    both(NLa, lev)
                    both(NLb, lev)
                for fn in (MGa, MGb, P1a, P1b, P1c, P1d):
                    both(fn)
                if n + 1 < nch_run:
                    nxt = PROJ(n + 1)
                both(F1)
                if n + 1 < nch_run:
                    pR, pK, pV, pW, pA, pG = nxt
                    both(E1)
                    e1_done = True
                else:
                    e1_done = False
                both(F1b)
            if "rw_short" in taps:
                tap("orwT", orwT_ap[:, :, 0:256], [128, 4, 256], orwTH)
            else:
                tap("orwT", orwT_ap, [128, 4, T], orwTH)
            k.barrier()
        if stop_after <= 2:
            k.finish()
            return nc, tap_d

        with ExitStack() as st:
            WG = [sb(st, f"WG{c}", [128, GL], BF16) for c in range(8)]
            for c in range(8):
                k.dma("pool", WG[c][:], win_v[:, c, RW:RW + GL], writes=[WG[c]])
            ga2 = sb(st, "ga2", [128, 256], F32)
            ngr = sb(st, "ngr", [128, 4, 128], F32)
            Gtri = sb(st, "Gtri", [128, 3, 128], F32)
            c16 = sb(st, "c16", [128, 1], F32)
            mI4 = sb(st, "mI4", [128, 4, 128], F32)
            k.op("dve", lambda e: e.memset(ga2[:], 0.0), writes=[ga2])
            k.dma("sp", ga2[0:16, :], ga2_d, writes=[ga2])
            gabr = sb(st, "gabr", [128, 256], F32)
            k.dma("sp", gabr[:], _bc_rows(gab_d, 128, 256), writes=[gabr])
            k.dma("sp", ngr[:], bass.AP(gng2_d.tensor, gng2_d.offset, [[0, 128], [0, 4], [1, 128]]), writes=[ngr])
            k.dma("sp", Gtri[:], ctri_d[:, 0:3, :], writes=[Gtri])
            TS("dve", Gtri[:], Gtri[:], -1.0 / 16, None, ALU.mult, None, [Gtri], [Gtri])
            k.op("dve", lambda e: e.memset(c16[:], -1.0 / 16), writes=[c16])
            for h4 in range(4):
                k.dma("sp", mI4[:, h4, :], ctri_d[:, 1, :], writes=[mI4])
            Sst = sb(st, "Sst", [128, 4, 128], F32)
            Sbf = sb(st, "Sbf", [128, 4, 128], BF16)
            k.op("dve", lambda e: e.memset(Sst[:], 0.0), writes=[Sst])
            k.op("dve", lambda e: e.memset(Sbf[:], 0.0), writes=[Sbf])

            def v3(ap, a):
                return ap.rearrange("p (a b) -> p a b", a=a)

            def gset(i):
                B = {}
                B["adT"] = sb(st, f"g_adT{i}", [128, 128], F32)
                k.op("dve", lambda e: e.memset(B["adT"][:], 0.0), writes=[B["adT"]])

                B["qkT"] = sb(st, f"g_qkT{i}", [128, 4, 128], F32)
                B["gk"] = sb(st, f"g_gk{i}", [128, 256], F32)
                B["ez"] = sb(st, f"g_ez{i}", [128, 256], F32)
                B["lz"] = sb(st, f"g_lz{i}", [128, 256], F32)
                B["Erev"] = sb(st, f"g_Erev{i}", [128, 256], F32)
                B["Ein"] = sb(st, f"g_Ein{i}", [128, 2, 128], F32)
                B["Enin"] = sb(st, f"g_Enin{i}", [128, 2, 128], F32)
                B["decs"] = sb(st, f"g_decs{i}", [128, 2], F32)
                B["kdec"] = sb(st, f"g_kdec{i}", [128, 256], BF16)
                B["gv"] = sb(st, f"g_v{i}", [128, 512], BF16)
                B["sgg"] = sb(st, f"g_sgg{i}", [128, 512], F32)
                B["QsT"] = sb(st, f"g_QsT{i}", [128, 2, 128], BF16)
                B["KsTz"] = sb(st, f"g_KsTz{i}", [128, 2, 2, 128], BF16)
                k.op("dve", lambda e: e.memset(B["KsTz"][:], 0.0), writes=[B["KsTz"]])
                B["attT"] = sb(st, f"g_attT{i}", [128, 4, 128], BF16)
                B["osb"] = sb(st, f"g_osb{i}", [128, 512], F32)
                B["osq"] = sb(st, f"g_osq{i}", [128, 512], F32)
                B["gsm"] = sb(st, f"g_sm{i}", [128, 2, 4], F32)
                B["of"] = sb(st, f"g_of{i}", [128, 512], BF16)
                return B
            GS = [gset(0), gset(1)]

            def G1(n, B):
                ucur = [uT_ap[:, c, n * 128 + 1:n * 128 + 129] for c in range(8)]
                pC, pD = bank(True), bank(True)
                pCv = v3(pC[:], 4)
                for i, c0 in enumerate((0, 128, 256, 384)):
                    for c in range(8):
                        MM(pCv[:, i, :], WG[c][:, c0:c0 + 128], ucur[c], c == 0, c == 7, [WG[c], uTb[c]], [pC])
                yield
                for c in range(8):
                    MM(pD[0:16, 0:128], WG[c][:, 1536:1552], ucur[c], c == 0, c == 7, [WG[c], uTb[c]], [pD])
                yield
                CP("dve", B["adT"][0:16, :], pD[0:16, 0:128], [pD], [B["adT"]])
                yield
                ACT(B["qkT"][:], pCv, AF.Copy, [pC], [B["qkT"]])
                unpin(pC, pD)

            def G2(n, B):
                ucur = [uT_ap[:, c, n * 128 + 1:n * 128 + 129] for c in range(8)]
                pK, pV, pGg = bank(True), bank(True), bank(True)
                for c in range(8):
                    MM(pK[:, 0:256], ucur[c], WG[c][:, 256:512], c == 0, c == 7, [WG[c], uTb[c]], [pK])
                yield
                for c in range(8):
                    MM(pV[:], ucur[c], WG[c][:, 512:1024], c == 0, c == 7, [WG[c], uTb[c]], [pV])
                yield
                for c in range(8):
                    MM(pGg[:], ucur[c], WG[c][:, 1024:1536], c == 0, c == 7, [WG[c], uTb[c]], [pGg])
                yield
                CP("dve", B["gk"][:], pK[:, 0:256], [pK], [B["gk"]])
                yield
                ACT(B["gv"][:], pV[:], AF.Copy, [pV], [B["gv"]])
                yield
                ACT(B["sgg"][:], pGg[:], AF.Silu, [pGg], [B["sgg"]])
                unpin(pK, pV, pGg)

            def G3(n, B):
                pZ = bank(True)
                MM(pZ[:, 0:256], B["adT"][:], ga2[:], True, True, [B["adT"], ga2], [pZ])
                yield
                TT("dve", B["ez"][:], pZ[:, 0:256], gabr[:], ALU.add, [pZ, gabr], [B["ez"]])
                unpin(pZ)
                yield
                ACT(B["ez"][:], B["ez"][:], AF.Exp, [B["ez"]], [B["ez"]], scale=-1.0)
                yield
                ACT(B["lz"][:], B["ez"][:], AF.Ln, [B["ez"]], [B["lz"]], bias=1.0)

            def G4(n, B):
                lz = B["lz"]
                pB, pDc = bank(True), bank(True)
                MM(pB[:, 0:256], Gtri[:, 2, :], lz[:], True, True, [Gtri, lz], [pB], last=False)
                yield
                for c2 in range(2):
                    MM(pB[:, 256 + c2 * 128:384 + c2 * 128], lz[:, c2 * 128:(c2 + 1) * 128], Gtri[:, 1, :], True, True, [Gtri, lz], [pB], last=(c2 == 1))
                yield
                for c2 in range(2):
                    MM(pDc[:, c2:c2 + 1], lz[:, c2 * 128:(c2 + 1) * 128], c16[:], True, True, [lz, c16], [pDc], last=(c2 == 1))
                yield
                ACT(B["Erev"][:], pB[:, 0:256], AF.Exp, [pB], [B["Erev"]])
                yield
                ACT(B["Ein"][:], v3(pB[:, 256:512], 2), AF.Exp, [pB], [B["Ein"]])
                yield
                ACT(B["Enin"][:], v3(pB[:, 256:512], 2), AF.Exp, [pB], [B["Enin"]], scale=-1.0)
                yield
                ACT(B["decs"][:], pDc[:, 0:2], AF.Exp, [pDc], [B["decs"]])
                unpin(pB, pDc)
                yield
                TT("dve", B["kdec"][:], B["gk"][:], B["Erev"][:], ALU.mult, [B["gk"], B["Erev"]], [B["kdec"]])
                yield
                STT("dve", B["QsT"][:], B["qkT"][:, 0:2, :], 0.125, B["Ein"][:], ALU.mult, ALU.mult, [B["qkT"], B["Ein"]], [B["QsT"]])
                yield
                for hh in range(2):
                    ps_ = slice(hh * 64, hh * 64 + 64)
                    TT("pool", B["KsTz"][ps_, :, hh, :], B["qkT"][ps_, 2:4, :], B["Enin"][ps_], ALU.mult, [B["qkT"], B["Enin"]], [B["KsTz"]])

            def G5(n, B):
                pA = bank(True)
                for h in range(4):
                    MM(v3(pA[:], 4)[:, h, :], B["KsTz"][:, h // 2, h % 2, :], B["QsT"][:, h // 2, :], True, True, [B["KsTz"], B["QsT"]], [pA], last=(h == 3))
                yield
                TT("dve", B["attT"][:], v3(pA[:], 4), mI4[:], ALU.mult, [pA, mI4], [B["attT"]])
                unpin(pA)

            def G6(n, B):
                pOo = bank(True)
                for h in range(4):
                    oo = v3(pOo[:], 4)[:, h, :]
                    MM(oo, B["attT"][:, h, :], B["gv"][:, h * 128:(h + 1) * 128], True, False, [B["attT"], B["gv"]], [pOo])
                    MM(oo, B["QsT"][:, h // 2, :], Sbf[:, h, :], False, True, [B["QsT"], Sbf], [pOo], last=(h == 3))
                pKV = bank(True)
                yield
                for h in range(4):
                    c2 = h // 2
                    MM(v3(pKV[:], 4)[:, h, :], B["kdec"][:, c2 * 128:(c2 + 1) * 128], B["gv"][:, h * 128:(h + 1) * 128], True, True, [B["kdec"], B["gv"]], [pKV], last=(h == 3))
                yield
                for h in range(4):
                    c2, p0 = h // 2, (h % 2) * 64
                    STT("dve", Sst[p0:p0 + 64, h, :], Sst[p0:p0 + 64, h, :], B["decs"][p0:p0 + 64, c2:c2 + 1],
                        v3(pKV[:], 4)[p0:p0 + 64, h, :], ALU.mult, ALU.add, [Sst, B["decs"], pKV], [Sst])
                yield
                CP("pool", Sbf[:], Sst[:], [Sst], [Sbf])
                unpin(pKV)
                B["pOo"] = pOo

            def G7(n, B):
                t0 = n * 128
                pOo, osb, osq, gsm = B["pOo"], B["osb"], B["osq"], B["gsm"]
                ACT(osb[:], pOo[:], AF.Copy, [pOo], [osb])
                unpin(pOo)
                yield
                TT("pool", osq[:], osb[:], osb[:], ALU.mult, [osb], [osq])
                yield
                k.op("dve", lambda e: e.tensor_reduce(gsm[:, 0, :], v3(osq[:], 4), AX.X, ALU.add), [osq], [gsm])
                yield
                TS("dve", gsm[:, 1, :], gsm[:, 0, :], 1.0 / 128, 1e-5, ALU.mult, ALU.add, [gsm], [gsm])
                yield
                k.op("dve", lambda e: e.reciprocal(gsm[:, 1, :], gsm[:, 1, :]), [gsm], [gsm])
                yield
                ACT(gsm[:, 1, :], gsm[:, 1, :], AF.Sqrt, [gsm], [gsm])
                yield
                TT("dve", v3(osb[:], 4), v3(osb[:], 4), gsm[:, 1, :].unsqueeze(2).to_broadcast([128, 4, 128]), ALU.mult, [osb, gsm], [osb])
                yield
                TT("pool", v3(osb[:], 4), v3(osb[:], 4), ngr[:], ALU.mult, [osb, ngr], [osb])
                yield
                TT("pool", B["of"][:], osb[:], B["sgg"][:], ALU.mult, [osb, B["sgg"]], [B["of"]])
                pO = bank(True)
                pOb = pO[:].bitcast(BF16)[:, 0:512].rearrange("p (c t) -> p c t", c=4)
                yield
                for c4 in range(4):
                    TR(pOb[:, c4, :], B["of"][:, c4 * 128:(c4 + 1) * 128], identb[:], [B["of"], identb], [pO], last=(c4 == 3))
                yield
                ACT(oglaT_ap[:, :, t0:t0 + 128], pOb, AF.Copy, [pO], [oglaTb])
                unpin(pO)

            nch_run = NCH if "gla_short" not in taps else 2
            for n in range(0, nch_run, 2):
                for fn in (G1, G2, G3, G4, G5, G6, G7):
                    if fn is G6:
                        for _ in fn(n, GS[0]):
                            pass
                        for _ in fn(n + 1, GS[1]):
                            pass
                        continue
                    gens = [fn(n, GS[0]), fn(n + 1, GS[1])]
                    while gens:
                        for g in list(gens):
                            try:
                                next(g)
                            except StopIteration:
                                gens.remove(g)
            if "gla_short" in taps:
                tap("oglaT", oglaT_ap[:, :, 0:256], [128, 4, 256], [oglaTb])
            else:
                tap("oglaT", oglaT_ap, [128, 4, T], [oglaTb])
            k.barrier()
        if stop_after <= 3:
            k.finish()
            return nc, tap_d

        with ExitStack() as st3:
            mT = sb(st3, "mT", [128, 8, T], BF16)
            mTb = [Buf(f"mT{d}", None) for d in range(8)]
            with ExitStack() as st:
                Wgt = [sb(st, f"Wgt{c}", [128, 2048], BF16) for c in range(8)]
                Wbr = sb(st, "Wbr", [128, 4, D], BF16)
                Wbg = sb(st, "Wbg", [128, 4, D], BF16)
                for c in range(8):
                    k.dma("pool", Wgt[c][:], win_v[:, c, RW + GL:NIN], writes=[Wgt[c]])
                k.dma("pool", Wbr[:], wbr_d.rearrange("(c p) n -> p c n", p=128), writes=[Wbr])
                k.dma("pool", Wbg[:], wbg_d.rearrange("(c p) n -> p c n", p=128), writes=[Wbg])
                sr = [sb(st, f"m_sr{i}", [128, 512], F32) for i in range(2)]
                sg_ = [sb(st, f"m_sg{i}", [128, 512], F32) for i in range(2)]
                it = 0
                for sbk in range(4):
                    tok = slice(sbk * 512, (sbk + 1) * 512)
                    for dt in range(8):
                        dsl = slice(dt * 128, (dt + 1) * 128)
                        p1, p2, p3, p4 = bank(), bank(), bank(), bank()
                        for c in range(8):
                            MM(p1[:], Wgt[c][:, dsl], uT_ap[:, c, sbk * 512 + 1:sbk * 512 + 513], c == 0, c == 7, [Wgt[c], uTb[c]], [p1])
                        for c in range(8):
                            MM(p2[:], Wgt[c][:, 1024 + dt * 128:1024 + (dt + 1) * 128], uT_ap[:, c, sbk * 512 + 1:sbk * 512 + 513],
                               c == 0, c == 7, [Wgt[c], uTb[c]], [p2])
                        for c in range(4):
                            MM(p3[:], Wbr[:, c, dsl], orwT_ap[:, c, tok], c == 0, c == 3, [Wbr] + orwTH, [p3])
                        for c in range(4):
                            MM(p4[:], Wbg[:, c, dsl], oglaT_ap[:, c, tok], c == 0, c == 3, [Wbg, oglaTb], [p4])
                        a_, b_ = sr[it % 2], sg_[it % 2]
                        it += 1
                        ACT(a_[:], p1[:], AF.Sigmoid, [p1], [a_])
                        ACT(b_[:], p2[:], AF.Sigmoid, [p2], [b_])
                        TT("dve", a_[:], a_[:], p3[:], ALU.mult, [a_, p3], [a_])
                        TT("dve", b_[:], b_[:], p4[:], ALU.mult, [b_, p4], [b_])
                        TT("pool", mT[:, dt, tok], a_[:], b_[:], ALU.add, [a_, b_], [mTb[dt]])
                tap("mT", mT[:], [128, 8, T], mTb)
                k.barrier()
            if stop_after <= 4:
                k.finish()
                return nc, tap_d

            with ExitStack() as st:
                Wmx = sb(st, "Wmx", [128, 8, D], BF16)
                k.dma("pool", Wmx[:], wmix_d.rearrange("(c p) n -> p c n", p=128), writes=[Wmx])
                g1row = sb(st, "g1row", [128, D], F32)
                lnr = sb(st, "ln1r", [128, 2, D], F32)
                k.dma("sp", lnr[:, 0, :], _bc_rows(ln1g_d, 128, D), writes=[lnr])
                k.dma("sp", lnr[:, 1, :], _bc_rows(ln1b_d, 128, D), writes=[lnr])
                screp = sb(st, "screp", [128, 8, 128], BF16)
                CP("dve", screp[:], scb[:].unsqueeze(2).to_broadcast([128, 8, 128]), [scb], [screp])
                brow1 = sb(st, "brow1", [1, D], F32)
                k.dma("sp", brow1[:], brow_d[:, 2 * D:3 * D], writes=[brow1])
                with ExitStack() as stw:
                    wg1 = sb(stw, "wg1", [128, 8, D], BF16)
                    k.dma("pool", wg1[:], wada_d.rearrange("(c p) n -> p c n", p=128)[:, :, 2 * D:3 * D], writes=[wg1])
                    for nh in range(2):
                        pg_ = bank()
                        for c in range(8):
                            MM(pg_[:], screp[:, c, :], wg1[:, c, nh * 512:(nh + 1) * 512], c == 0, False, [screp, wg1], [pg_])
                        MM(pg_[:], ones1[:], brow1[:, nh * 512:(nh + 1) * 512], False, True, [ones1, brow1], [pg_])
                        ACT(g1row[:, nh * 512:(nh + 1) * 512], pg_[:], AF.Copy, [pg_], [g1row])
                    k.barrier()
                xin = [sb(st, f"xin{i}", [128, D], F32) for i in range(2)]
                ybuf = [sb(st, f"ybuf{i}", [128, D], F32) for i in range(2)]
                stat = [sb(st, f"l1stat{i}", [128, 2, 6], F32) for i in range(2)]
                mv = [sb(st, f"l1mv{i}", [128, 4], F32) for i in range(2)]
                pms = {}

                def M0(tb):
                    xi = xin[tb % 2]
                    k.dma("sp", xi[:], x_d[tb * 128:(tb + 1) * 128, :], writes=[xi])
                    pm_ = [bank(True), bank(True)]
                    for nh in range(2):
                        for dt in range(8):
                            MM(pm_[nh][:], mT[:, dt, tb * 128:(tb + 1) * 128], Wmx[:, dt, nh * 512:(nh + 1) * 512], dt == 0, dt == 7, [mTb[dt], Wmx], [pm_[nh]])
                    pms[tb] = pm_

                def M1(tb):
                    yb, pm_ = ybuf[tb % 2], pms[tb]
                    for nh in range(2):
                        hs = slice(nh * 512, (nh + 1) * 512)
                        TT("dve", yb[:, hs], pm_[nh][:], g1row[:, hs], ALU.mult, [pm_[nh], g1row], [yb])
                    unpin(*pm_)
                    yield
                    STT("dve", yb[:], xin[tb % 2][:], ALPHA, yb[:], ALU.mult, ALU.add, [xin[tb % 2], yb], [yb])

                def M2(tb):
                    yb, st_, mv_ = ybuf[tb % 2], stat[tb % 2], mv[tb % 2]
                    for nh in range(2):
                        k.op("dve", lambda e, nh=nh: e.bn_stats(st_[:, nh, :], yb[:, nh * 512:(nh + 1) * 512]), [yb], [st_])
                    yield
                    k.op("dve", lambda e: e.bn_aggr(mv_[:, 0:2], st_[:]), [st_], [mv_])
                    yield
                    TS("dve", mv_[:, 2:3], mv_[:, 1:2], 1e-5, None, ALU.add, None, [mv_], [mv_])
                    yield
                    k.op("dve", lambda e: e.reciprocal(mv_[:, 2:3], mv_[:, 2:3]), [mv_], [mv_])
                    yield
                    ACT(mv_[:, 2:3], mv_[:, 2:3], AF.Sqrt, [mv_], [mv_])

                def M3(tb):
                    yb, mv_ = ybuf[tb % 2], mv[tb % 2]
                    TS("dve", yb[:], yb[:], mv_[:, 0:1], mv_[:, 2:3], ALU.subtract, ALU.mult, [yb, mv_], [yb])
                    yield
                    TT("pool", yb[:], yb[:], lnr[:, 0, :], ALU.mult, [yb, lnr], [yb])
                    yield
                    TT("pool", x1_ap[:, tb, :], yb[:], lnr[:, 1, :], ALU.add, [yb, lnr], [x1b[tb]])

                for tb0 in range(0, 16, 2):
                    M0(tb0)
                    M0(tb0 + 1)
                    for fn in (M1, M2, M3):
                        gens = [fn(tb0), fn(tb0 + 1)]
                        while gens:
                            for g in list(gens):
                                try:
                                    next(g)
                                except StopIteration:
                                    gens.remove(g)
                tap("x1", x1_ap, [128, 16, D], x1b)
                k.barrier()
        if stop_after <= 5:
            k.finish()
            return nc, tap_d

        with ExitStack() as st:
            Wo = sb(st, "Wo", [128, 22, D], BF16)
            Wob = [Buf(f"Wo{f}", None) for f in range(22)]
            wfo_v = wfo_d.rearrange("(f p) n -> p f n", p=128)
            lnr = sb(st, "ln2r", [128, 2, D], F32)
            k.dma("sp", lnr[:, 0, :], _bc_rows(ln2g_d, 128, D), writes=[lnr])
            k.dma("sp", lnr[:, 1, :], _bc_rows(ln2b_d, 128, D), writes=[lnr])
            g2row = sb(st, "g2row", [128, D], BF16)
            screp = sb(st, "screp2", [128, 8, 128], BF16)
            CP("dve", screp[:], scb[:].unsqueeze(2).to_broadcast([128, 8, 128]), [scb], [screp])
            u2T = sb(st, "u2T", [128, 8, 1024], BF16)
            actT = sb(st, "actT", [128, 22, 1024], BF16)
            actb = [Buf(f"act{f}", None) for f in range(22)]
            Wgu = [sb(st, f"Wgu{i}", [128, 8, 2, 128], BF16) for i in range(3)]
            sgf = [sb(st, f"f_sg{i}", [128, 512], F32) for i in range(2)]
            ybuf = [sb(st, f"f_y{i}", [128, D], F32) for i in range(2)]
            stat = sb(st, "l2stat", [128, 2, 6], F32)
            mv = sb(st, "l2mv", [128, 4], F32)
            brow2 = ybuf[0]
            k.dma("sp", brow2[0:1, :], brow_d[:, 5 * D:6 * D], writes=[brow2])
            wg2 = actT[:, 14:22, :]
            wg2b = actb[14:22]

            def side_work(f):
                if f == 0:
                    k.dma("pool", wg2, wada_d.rearrange("(c p) n -> p c n", p=128)[:, :, 5 * D:6 * D], writes=wg2b)
                if f == 3:
                    for nh in range(2):
                        pg_ = bank()
                        for c in range(8):
                            MM(pg_[:], screp[:, c, :], wg2[:, c, nh * 512:(nh + 1) * 512], c == 0, False, [screp] + wg2b, [pg_])
                        MM(pg_[:], ones1[:], brow2[0:1, nh * 512:(nh + 1) * 512], False, True, [ones1, brow2], [pg_])
                        ACT(g2row[:, nh * 512:(nh + 1) * 512], pg_[:], AF.Copy, [pg_], [g2row])
                if 2 <= f < 13:
                    f0 = 2 * (f - 2)
                    k.dma("pool", Wo[:, f0:f0 + 2, :], wfo_v[:, f0:f0 + 2, :], writes=Wob[f0:f0 + 2])
                if 6 <= f < 17:
                    f0 = 2 * (f - 6)
                    TT("dve", Wo[:, f0, :], Wo[:, f0, :], g2row[:], ALU.mult, [Wob[f0], g2row], [Wob[f0]])
                    TT("pool", Wo[:, f0 + 1, :], Wo[:, f0 + 1, :], g2row[:], ALU.mult, [Wob[f0 + 1], g2row], [Wob[f0 + 1]])
            wfi_v = wfi_d.rearrange("(c p) (g n) -> p c g n", p=128, g=2)
            wi = 0
            it = 0
            for sbk in range(2):
                for c in range(8):
                    for jb in range(2):
                        pt_ = bank()
                        for j in range(4):
                            tb = sbk * 8 + jb * 4 + j
                            TR(pt_[:, j * 128:(j + 1) * 128], x1_ap[:, tb, c * 128:(c + 1) * 128], identf[:], [x1b[tb], identf], [pt_], last=(j == 3))
                        ACT(u2T[:, c, jb * 512:(jb + 1) * 512], pt_[:], AF.Identity, [pt_, modp], [u2T],
                            bias=modp[:, 16 + c:17 + c], scale=modp[:, 24 + c:25 + c])
                for f in range(22):
                    w_ = Wgu[wi % 3]
                    wi += 1
                    k.dma("pool", w_[:, :, 0, :], wfi_v[:, :, 0, f * 128:(f + 1) * 128], writes=[w_])
                    k.dma("pool", w_[:, :, 1, :], wfi_v[:, :, 1, f * 128:(f + 1) * 128], writes=[w_])
                    if sbk == 0:
                        side_work(f)
                    for hh in range(2):
                        ph, pu = bank(), bank()
                        for c in range(8):
                            MM(ph[:], w_[:, c, 0, :], u2T[:, c, hh * 512:(hh + 1) * 512], c == 0, c == 7, [w_, u2T], [ph])
                        for c in range(8):
                            MM(pu[:], w_[:, c, 1, :], u2T[:, c, hh * 512:(hh + 1) * 512], c == 0, c == 7, [w_, u2T], [pu])
                        s_ = sgf[it % 2]
                        it += 1
                        ACT(s_[:], ph[:], AF.Silu, [ph], [s_])
                        TT("dve", actT[:, f, hh * 512:(hh + 1) * 512], s_[:], pu[:], ALU.mult, [s_, pu], [actb[f]])
                for j in range(8):
                    tb = sbk * 8 + j
                    yb = ybuf[tb % 2]
                    po = [bank(), bank()]
                    for nh in range(2):
                        for f in range(22):
                            MM(po[nh][:], actT[:, f, j * 128:(j + 1) * 128], Wo[:, f, nh * 512:(nh + 1) * 512], f == 0, f == 21, [actb[f], Wob[f]], [po[nh]])
                    for nh in range(2):
                        hs = slice(nh * 512, (nh + 1) * 512)
                        STT("dve", yb[:, hs], x1_ap[:, tb, hs], ALPHA, po[nh][:], ALU.mult, ALU.add, [x1b[tb], po[nh]], [yb])
                    for nh in range(2):
                        k.op("dve", lambda e, nh=nh, yb=yb: e.bn_stats(stat[:, nh, :], yb[:, nh * 512:(nh + 1) * 512]), [yb], [stat])
                    k.op("dve", lambda e: e.bn_aggr(mv[:, 0:2], stat[:]), [stat], [mv])
                    TS("dve", mv[:, 2:3], mv[:, 1:2], 1e-5, None, ALU.add, None, [mv], [mv])
                    k.op("dve", lambda e: e.reciprocal(mv[:, 2:3], mv[:, 2:3]), [mv], [mv])
                    ACT(mv[:, 2:3], mv[:, 2:3], AF.Sqrt, [mv], [mv])
                    TS("dve", yb[:], yb[:], mv[:, 0:1], mv[:, 2:3], ALU.subtract, ALU.mult, [yb, mv], [yb])
                    TT("pool", yb[:], yb[:], lnr[:, 0, :], ALU.mult, [yb, lnr], [yb])
                    TT("pool", yb[:], yb[:], lnr[:, 1, :], ALU.add, [yb, lnr], [yb])
                    k.dma("sp", y_d[tb * 128:(tb + 1) * 128, :], yb[:], reads=[yb], is_output=True)
        k.finish()
    return nc, tap_d


def _host_inputs(inputs):
    f = lambda a: np.ascontiguousarray(a, dtype=np.float32)
    sh = {}
    b = inputs["b_ada"][0]
    sh["w_ada"] = f(inputs["w_ada"][0])
    sh["b_pp"] = f(b.reshape(6, 8, 128)[[0, 1, 3, 4]].transpose(2, 0, 1).reshape(128, 32))
    sh["b_row"] = f(b.reshape(1, -1))
    sh["w_in"] = f(inputs["w_in"][0])
    sh["mu"] = f(inputs["mu_rw"][0].reshape(1, -1))
    for nm in ("rw_w0", "rw_a0", "rw_k_k", "rw_k_a", "rw_r_k", "rw_gn_g", "rw_gn_b", "gla_a_b", "gla_norm_g",
               "ln1_g", "ln1_b", "ln2_g", "ln2_b"):
        sh[nm] = f(inputs[nm][0].reshape(1, -1))
    for nm in ("rw_w2", "rw_a2", "rw_g2", "gla_a2", "w_rw_branch", "w_gla_branch", "w_mix_out", "w_ffn_in", "w_ffn_out"):
        sh[nm] = f(inputs[nm][0])
    sh["c_ident"] = np.eye(128, dtype=np.float32)
    s = np.arange(128)[:, None]
    t = np.arange(128)[None, :]
    bd = (s // 64) == (t // 64)
    sh["c_tri"] = f(np.stack([(s < t), (s <= t), (s > t), (s < t) & bd, (s > t) & bd, (s >= 64) & (t < 64)], axis=1).astype(np.float32))
    maps = []
    x = inputs["x"]
    c = inputs["c"]
    for bi in range(x.shape[0]):
        m = dict(sh)
        m["xT"] = f(x[bi].T)
        m["x"] = f(x[bi])
        m["cpp"] = f(c[bi].reshape(8, 128).T)
        maps.append(m)
    return maps


def kernel(**inputs):
    maps = _host_inputs(inputs)
    nc, _ = build_nc()
    res = run_bass_kernel_spmd(nc, maps, core_ids=list(range(len(maps))))
    return np.stack([np.asarray(r["y"], dtype=np.float32) for r in res.results], axis=0)
```
